# Optimizing a Trainium2 kernel written in Bass

```python
import math
import jax, jax.numpy as jnp
from jax import lax
import numpy as np

D_MODEL = 1024
BATCH = 8
SEQ = 2048
DEPTH = 1
DEC_BATCH = 128
DEC_SEQ = 4
PAST_LEN = 16384
PAGE_SIZE = 128

MIX_WIDTH = D_MODEL
W_S5 = MIX_WIDTH // 2
W_HG = MIX_WIDTH - W_S5
S5_CH = 16
S5_GROUPS = W_S5 // S5_CH
S5_STATE = 64
HG_DK = 128
HG_HEADS = W_HG // HG_DK
HG_DV = W_HG // HG_HEADS
HG_CHUNK = 64
EPS = 1e-6
LAMBDA_RE_MAX = -1e-4
PROJ_SIZES = (W_S5, W_S5, HG_HEADS * HG_DK, HG_HEADS * HG_DK, HG_HEADS * HG_DV, W_HG)
PROJ_SPLITS = tuple(int(v) for v in np.cumsum(PROJ_SIZES)[:-1])
PROJ_OUT = sum(PROJ_SIZES)

kernel_name = "hybrid_s5_hgrn2_parallel_heads_step"


def rmsnorm(x, g):
    xf = x.astype(jnp.float32)
    return xf * lax.rsqrt(jnp.mean(xf * xf, axis=-1, keepdims=True) + EPS) * g.astype(jnp.float32)


def s5_scan(bu_re, bu_im, a_re, a_im):
    ar = jnp.broadcast_to(a_re, bu_re.shape)
    ai = jnp.broadcast_to(a_im, bu_re.shape)

    def combine(e1, e2):
        ar1, ai1, br1, bi1 = e1
        ar2, ai2, br2, bi2 = e2
        return (ar2 * ar1 - ai2 * ai1, ar2 * ai1 + ai2 * ar1,
                ar2 * br1 - ai2 * bi1 + br2, ar2 * bi1 + ai2 * br1 + bi2)

    _, _, hr, hi = lax.associative_scan(combine, (ar, ai, bu_re, bu_im), axis=1)
    return hr, hi


def hgrn2_chunked(q, k, v, log_f, S0):
    Bn, L, H, DK = q.shape
    DV = v.shape[-1]
    C = HG_CHUNK if L % HG_CHUNK == 0 else L
    n = L // C

    def to_chunks(t):
        return t.reshape(Bn, n, C, H, t.shape[-1]).transpose(1, 0, 3, 2, 4)

    qc, kc, vc, gc = to_chunks(q), to_chunks(k), to_chunks(v), to_chunks(log_f)
    mask = jnp.tril(jnp.ones((C, C), dtype=bool))[:, :, None]

    def step(S, inp):
        qi, ki, vi, gi = inp
        b = jnp.cumsum(gi, axis=2)
        diff = b[:, :, :, None, :] - b[:, :, None, :, :]
        decay = jnp.exp(jnp.where(mask, diff, -jnp.inf))
        att = jnp.einsum('bhtk,bhsk,bhtsk->bhts', qi, ki, decay)
        o = jnp.einsum('bhts,bhsv->bhtv', att, vi) + jnp.einsum('bhtk,bhkv->bhtv', qi * jnp.exp(b), S)
        b_last = b[:, :, -1:, :]
        S_new = jnp.exp(b_last[:, :, 0, :])[..., None] * S + jnp.einsum('bhsk,bhsv->bhkv', ki * jnp.exp(b_last - b), vi)
        return S_new, o

    S_fin, o = lax.scan(step, S0, (qc, kc, vc, gc))
    o = o.transpose(1, 0, 3, 2, 4).reshape(Bn, L, H, DV)
    return o, S_fin


def mixer_layer(x, h0_re, h0_im, S0, norm_g, w_in, lam_re, lam_im, log_step, b_re, b_im, c_re, c_im, d,
                w_glu, b_glu, lb, onorm_g, w_out):
    Bn, L, _ = x.shape
    f32 = jnp.float32
    h = rmsnorm(x, norm_g)
    proj = h @ w_in.astype(f32)
    u, z_s, q, gf, iv, z_h = jnp.split(proj, PROJ_SPLITS, axis=-1)

    lr = jnp.minimum(lam_re.astype(f32), LAMBDA_RE_MAX)
    li = lam_im.astype(f32)
    dt = jnp.exp(log_step.astype(f32))[:, None]
    er = jnp.exp(lr * dt)
    a_re = er * jnp.cos(li * dt)
    a_im = er * jnp.sin(li * dt)
    den = lr * lr + li * li
    zr = ((a_re - 1.0) * lr + a_im * li) / den
    zi = (a_im * lr - (a_re - 1.0) * li) / den
    br, bi = b_re.astype(f32), b_im.astype(f32)
    bb_re = zr[..., None] * br - zi[..., None] * bi
    bb_im = zr[..., None] * bi + zi[..., None] * br
    ug = u.reshape(Bn, L, S5_GROUPS, S5_CH)
    bu_re = jnp.einsum('blgc,gpc->blgp', ug, bb_re)
    bu_im = jnp.einsum('blgc,gpc->blgp', ug, bb_im)
    h0r, h0i = h0_re.astype(f32), h0_im.astype(f32)
    bu_re = bu_re.at[:, 0].add(a_re * h0r - a_im * h0i)
    bu_im = bu_im.at[:, 0].add(a_re * h0i + a_im * h0r)
    hr, hi = s5_scan(bu_re, bu_im, a_re, a_im)
    y = (jnp.einsum('blgp,gcp->blgc', hr, c_re.astype(f32))
         - jnp.einsum('blgp,gcp->blgc', hi, c_im.astype(f32))
         + d.astype(f32) * ug).reshape(Bn, L, W_S5)
    g = jax.nn.gelu(y)
    s5_out = g * jax.nn.sigmoid(g @ w_glu.astype(f32) + b_glu.astype(f32)) * jax.nn.silu(z_s)

    lbf = lb.astype(f32)
    f = lbf + (1.0 - lbf) * jax.nn.sigmoid(gf)
    log_f = jnp.log(f)
    k = (1.0 - lbf) * jax.nn.sigmoid(-gf)
    qh = q.reshape(Bn, L, HG_HEADS, HG_DK)
    kh = k.reshape(Bn, L, HG_HEADS, HG_DK)
    gh = log_f.reshape(Bn, L, HG_HEADS, HG_DK)
    vh = iv.reshape(Bn, L, HG_HEADS, HG_DV)
    o, S_fin = hgrn2_chunked(qh, kh, vh, gh, S0.astype(f32))
    o = o * lax.rsqrt(jnp.mean(o * o, axis=-1, keepdims=True) + EPS)
    hg_out = o.reshape(Bn, L, W_HG) * onorm_g.astype(f32) * jax.nn.silu(z_h)

    out = jnp.concatenate([s5_out, hg_out], axis=-1) @ w_out.astype(f32)
    return x + out.astype(x.dtype), hr[:, -1], hi[:, -1], S_fin


def setup_inputs(seed: int = 0) -> dict:
    key = jax.random.key(seed)
    ks = jax.random.split(key, 24)
    f32 = jnp.float32
    n = jnp.arange(S5_STATE, dtype=f32)
    lam_re = -0.5 + 0.01 * jax.random.normal(ks[5], (DEPTH, S5_GROUPS, S5_STATE), f32)
    lam_im = math.pi * n + 0.01 * jax.random.normal(ks[6], (DEPTH, S5_GROUPS, S5_STATE), f32)
    log_step = jax.random.uniform(ks[7], (DEPTH, S5_GROUPS), f32, math.log(1e-3), math.log(1e-1))
    return {
        "x_prompt": jax.random.normal(ks[0], (BATCH, SEQ, D_MODEL), f32),
        "x_sample": jax.random.normal(ks[1], (DEC_BATCH, DEC_SEQ, D_MODEL), f32),
        "state_s5_re": 0.1 * jax.random.normal(ks[2], (DEPTH, DEC_BATCH, S5_GROUPS, S5_STATE), f32),
        "state_s5_im": 0.1 * jax.random.normal(ks[3], (DEPTH, DEC_BATCH, S5_GROUPS, S5_STATE), f32),
        "state_hgrn": 0.5 * jax.random.normal(ks[4], (DEPTH, DEC_BATCH, HG_HEADS, HG_DK, HG_DV), f32),
        "norm_g": 1.0 + 0.02 * jax.random.normal(ks[8], (DEPTH, D_MODEL), f32),
        "w_in": jax.random.normal(ks[9], (DEPTH, D_MODEL, PROJ_OUT), f32) * D_MODEL ** -0.5,
        "s5_lambda_re": lam_re,
        "s5_lambda_im": lam_im,
        "s5_log_step": log_step,
        "s5_b_re": 0.5 * jax.random.normal(ks[10], (DEPTH, S5_GROUPS, S5_STATE, S5_CH), f32),
        "s5_b_im": 0.5 * jax.random.normal(ks[11], (DEPTH, S5_GROUPS, S5_STATE, S5_CH), f32),
        "s5_c_re": jax.random.normal(ks[12], (DEPTH, S5_GROUPS, S5_CH, S5_STATE), f32) * S5_STATE ** -0.5,
        "s5_c_im": jax.random.normal(ks[13], (DEPTH, S5_GROUPS, S5_CH, S5_STATE), f32) * S5_STATE ** -0.5,
        "s5_d": 0.5 * jax.random.normal(ks[14], (DEPTH, S5_GROUPS, S5_CH), f32),
        "w_glu": jax.random.normal(ks[15], (DEPTH, W_S5, W_S5), f32) * W_S5 ** -0.5,
        "b_glu": 0.02 * jax.random.normal(ks[16], (DEPTH, W_S5), f32),
        "hgrn_lb_logits": 0.1 * jax.random.normal(ks[17], (DEPTH + 1, HG_HEADS * HG_DK), f32),
        "hgrn_onorm_g": 1.0 + 0.02 * jax.random.normal(ks[18], (DEPTH, W_HG), f32),
        "w_out": jax.random.normal(ks[19], (DEPTH, MIX_WIDTH, D_MODEL), f32) * MIX_WIDTH ** -0.5,
        "final_norm_g": 1.0 + 0.02 * jax.random.normal(ks[20], (D_MODEL,), f32),
    }


def reference(x_prompt, x_sample, state_s5_re, state_s5_im, state_hgrn, norm_g, w_in, s5_lambda_re, s5_lambda_im,
              s5_log_step, s5_b_re, s5_b_im, s5_c_re, s5_c_im, s5_d, w_glu, b_glu, hgrn_lb_logits, hgrn_onorm_g,
              w_out, final_norm_g):
    f32 = jnp.float32
    lb_all = jnp.cumsum(jax.nn.softmax(hgrn_lb_logits.astype(f32), axis=0), axis=0)
    xp, xs = x_prompt, x_sample
    zero_re = jnp.zeros((x_prompt.shape[0], S5_GROUPS, S5_STATE), f32)
    zero_S = jnp.zeros((x_prompt.shape[0], HG_HEADS, HG_DK, HG_DV), f32)
    p_re, p_im, p_hg, s_re, s_im, s_hg = [], [], [], [], [], []
    for l in range(DEPTH):
        params = (norm_g[l], w_in[l], s5_lambda_re[l], s5_lambda_im[l], s5_log_step[l], s5_b_re[l], s5_b_im[l],
                  s5_c_re[l], s5_c_im[l], s5_d[l], w_glu[l], b_glu[l], lb_all[l], hgrn_onorm_g[l], w_out[l])
        xp, hr, hi, S = mixer_layer(xp, zero_re, zero_re, zero_S, *params)
        p_re.append(hr); p_im.append(hi); p_hg.append(S)
        xs, hr, hi, S = mixer_layer(xs, state_s5_re[l], state_s5_im[l], state_hgrn[l], *params)
        s_re.append(hr); s_im.append(hi); s_hg.append(S)
    sd = state_s5_re.dtype
    y_prompt = rmsnorm(xp, final_norm_g).astype(x_prompt.dtype)
    y_sample = rmsnorm(xs, final_norm_g).astype(x_sample.dtype)
    new_s5_re_prompt = jnp.stack(p_re).astype(sd)
    new_s5_im_prompt = jnp.stack(p_im).astype(sd)
    new_hgrn_prompt = jnp.stack(p_hg).astype(state_hgrn.dtype)
    new_s5_re_sample = jnp.stack(s_re).astype(sd)
    new_s5_im_sample = jnp.stack(s_im).astype(sd)
    new_hgrn_sample = jnp.stack(s_hg).astype(state_hgrn.dtype)
    return (y_prompt, y_sample, new_s5_re_prompt, new_s5_im_prompt, new_hgrn_prompt,
            new_s5_re_sample, new_s5_im_sample, new_hgrn_sample)
```

```python
import math
import numpy as np
import concourse.bass as bass
import concourse.mybir as mybir
from concourse.bass_utils import run_bass_kernel_spmd

F32 = mybir.dt.float32
BF16 = mybir.dt.bfloat16
I32 = mybir.dt.int32
ALU = mybir.AluOpType
AF = mybir.ActivationFunctionType

NCORES = 8
D = 1024
LP = 2048
NS = 16
LS = 4
NSAMP = NS * LS
NT = LP + NSAMP
TC = 128
EPS = 1e-6
ENGS = ("pe", "act", "dve", "pool", "sp")
N_DMA_SEMS = 20
N_SW_SEMS = 8
SB_BASE = 16512
SB_LIMIT = 229376


class Buf:
    __slots__ = ("name", "writer", "readers", "excl")

    def __init__(self, name, excl=False):
        self.name = name
        self.writer = None
        self.readers = []
        self.excl = excl


class Op:
    __slots__ = ("eng", "fn", "deps", "signals", "count", "is_dma", "dma_sem", "dma_val")

    def __init__(self, eng, fn, is_dma=False):
        self.eng = eng
        self.fn = fn
        self.deps = []
        self.signals = False
        self.count = None
        self.is_dma = is_dma
        self.dma_sem = None
        self.dma_val = None


class Prog:
    def __init__(self):
        self.ops = {e: [] for e in ENGS}
        self.n_dma = 0
        self.n_swdma = 0
        self.dma_last = [None] * (N_DMA_SEMS + N_SW_SEMS)
        self.dma_cnt = [0] * (N_DMA_SEMS + N_SW_SEMS)
        self.out_dmas = []
        self.pending_barrier = {}
        self.last_op = {e: None for e in ENGS}
        self.dmas_since_barrier = []

    def _add(self, op, reads, writes):
        ex = [b for b in reads if b.excl]
        if ex:
            reads = [b for b in reads if not b.excl]
            writes = list(writes) + [b for b in ex if b not in writes]
        deps = []
        for b in reads:
            if b.writer is not None:
                deps.append(b.writer)
        for b in writes:
            if b.writer is not None:
                deps.append(b.writer)
            deps.extend(b.readers)
        if op.eng in self.pending_barrier:
            deps.extend(self.pending_barrier.pop(op.eng))
        seen = set()
        for d in deps:
            if d is op or id(d) in seen:
                continue
            seen.add(id(d))
            op.deps.append(d)
            d.signals = True
        for b in reads:
            b.readers.append(op)
        for b in writes:
            b.writer = op
            b.readers = []
        self.ops[op.eng].append(op)
        if op.is_dma:
            self.dmas_since_barrier.append(op)
        else:
            self.last_op[op.eng] = op
        return op

    def op(self, eng, fn, reads=(), writes=()):
        return self._add(Op(eng, fn), reads, writes)

    def dma(self, out, in_, reads=(), writes=(), eng="sp", is_output=False, **kw):
        def fn(e, out=out, in_=in_, kw=kw):
            return e.dma_start(out=out, in_=in_, **kw)
        op = Op(eng, fn, is_dma=True)
        if eng == "pool":
            k = N_DMA_SEMS + (self.n_swdma % N_SW_SEMS)
            self.n_swdma += 1
        else:
            k = self.n_dma % N_DMA_SEMS
            self.n_dma += 1
        prev = self.dma_last[k]
        self.dma_cnt[k] += 1
        op.dma_sem = k
        op.dma_val = 16 * self.dma_cnt[k]
        self._add(op, reads, writes)
        if prev is not None and prev not in op.deps:
            op.deps.append(prev)
            prev.signals = True
        self.dma_last[k] = op
        if is_output:
            self.out_dmas.append(op)
            op.signals = True
        return op

    def barrier(self):
        pre = [o for o in self.last_op.values() if o is not None] + list(self.dmas_since_barrier)
        self.dmas_since_barrier = []
        for e in ENGS:
            self.pending_barrier[e] = list(self.pending_barrier.get(e, [])) + pre

    def emit(self, nc):
        import contextlib
        for e in ENGS:
            c = 0
            for op in self.ops[e]:
                if not op.is_dma and op.signals:
                    c += 1
                    op.count = c
        with contextlib.ExitStack() as st:
            esem = {e: st.enter_context(nc.semaphore("s_" + e)) for e in ENGS}
            dsem = [st.enter_context(nc.semaphore("d_%d" % i)) for i in range(N_DMA_SEMS + N_SW_SEMS)]
            block = st.enter_context(nc.Block())

            def run(e, engobj):
                waited = {}

                def wait_for(d):
                    if d.is_dma:
                        key, sem, val = ("d", d.dma_sem), dsem[d.dma_sem], d.dma_val
                    else:
                        key, sem, val = ("e", d.eng), esem[d.eng], d.count
                    if waited.get(key, 0) >= val:
                        return
                    waited[key] = val
                    engobj.wait_ge(sem, val)

                for op in self.ops[e]:
                    for d in op.deps:
                        wait_for(d)
                    ins = op.fn(engobj)
                    if op.is_dma:
                        ins.then_inc(dsem[op.dma_sem], 16)
                    elif op.signals:
                        ins.then_inc(esem[e], 1)
                if e == "sp":
                    for d in self.out_dmas:
                        wait_for(d)

            @block.tensor
            def _(eng):
                run("pe", eng)

            @block.scalar
            def _(eng):
                run("act", eng)

            @block.vector
            def _(eng):
                run("dve", eng)

            @block.gpsimd
            def _(eng):
                run("pool", eng)

            @block.sync
            def _(eng):
                run("sp", eng)


class Arena:
    def __init__(self, nc):
        self.nc = nc
        self.off = SB_BASE
        self.n = 0
        self.peak = SB_BASE

    def alloc(self, name, shape, dt):
        esz = 2 if dt == BF16 else 4
        size = esz
        for s in shape[1:]:
            size *= s
        size = (size + 31) // 32 * 32
        self.n += 1
        h = self.nc.alloc_sbuf_tensor_at("%s_%d" % (name, self.n), list(shape), dt, offset=self.off)
        self.off += size
        self.peak = max(self.peak, self.off)
        assert self.off <= SB_LIMIT, "SBUF overflow at %s: %d" % (name, self.off)
        return h

    def mark(self):
        return self.off

    def reset(self, m):
        self.off = m


def col_chunks():
    return [(i * 512, 512) for i in range(4)] + [(LP, NSAMP)]


def build_program():
    nc = bass.Bass("TRN2", target_bir_lowering=False)

    def din(name, shape, dt=F32):
        return nc.dram_tensor(name, list(shape), dt, kind="ExternalInput").ap()

    def dout(name, shape):
        return nc.dram_tensor(name, list(shape), F32, kind="ExternalOutput").ap()

    xp = din("xp", [LP, D])
    xs = din("xs", [NSAMP, D])
    s5re0 = din("s5re0", [NS, 2048])
    s5im0 = din("s5im0", [NS, 2048])
    hg0 = din("hg0", [NS, 4, 128, 128])
    norm_g = din("norm_g", [D])
    w_in = din("w_in", [D, 3072])
    lam_re = din("lam_re", [32, 64])
    lam_im = din("lam_im", [32, 64])
    log_step = din("log_step", [32])
    b_re = din("b_re", [32, 64, 16])
    b_im = din("b_im", [32, 64, 16])
    c_re = din("c_re", [32, 16, 64])
    c_im = din("c_im", [32, 16, 64])
    s5_d = din("s5_d", [512])
    w_glu = din("w_glu", [512, 512])
    b_glu = din("b_glu", [512])
    lb_logits = din("lb_logits", [2, 512])
    onorm_g = din("onorm_g", [512])
    w_out = din("w_out", [D, D])
    fin_g = din("fin_g", [1, D])
    c_ident = din("c_ident", [128, 128])
    c_mask2 = din("c_mask2", [128, 128])
    c_masks = din("c_masks", [64, 64])
    c_seg = din("c_seg", [1, 512 + NSAMP])
    c_rowm = din("c_rowm", [64, NS])
    c_glm = din("c_glm", [128, 2])
    c_rm4 = din("c_rm4", [128, 4])

    yp = dout("yp", [LP, D])
    ys = dout("ys", [NSAMP, D])
    o_pre = dout("o_pre", [16, 128])
    o_pim = dout("o_pim", [16, 128])
    o_phg = dout("o_phg", [4, 128, 128])
    o_sre = dout("o_sre", [NS, 2048])
    o_sim = dout("o_sim", [NS, 2048])
    o_shg = dout("o_shg", [NS, 4, 128, 128])

    P = Prog()
    A = Arena(nc)

    def sb(name, shape, dt=F32):
        return A.alloc(name, shape, dt)

    def ACT(reads, writes, out, in_, func, scale=1.0, bias=0.0, accum_out=None):
        def fn(e):
            if accum_out is not None:
                return e.activation(out=out, in_=in_, func=func, scale=scale, bias=bias, accum_out=accum_out)
            return e.activation(out=out, in_=in_, func=func, scale=scale, bias=bias)
        return P.op("act", fn, reads, writes)

    def TT(eng, reads, writes, out, in0, in1, op):
        return P.op(eng, lambda e: e.tensor_tensor(out=out, in0=in0, in1=in1, op=op), reads, writes)

    def TS(eng, reads, writes, out, in0, s1, s2, op0, op1=None):
        if op1 is None:
            return P.op(eng, lambda e: e.tensor_scalar(out=out, in0=in0, scalar1=s1, scalar2=None, op0=op0), reads, writes)
        return P.op(eng, lambda e: e.tensor_scalar(out=out, in0=in0, scalar1=s1, scalar2=s2, op0=op0, op1=op1), reads, writes)

    def STT(reads, writes, out, in0, scalar, in1, op0, op1):
        return P.op("dve", lambda e: e.scalar_tensor_tensor(out=out, in0=in0, scalar=scalar, in1=in1, op0=op0, op1=op1), reads, writes)

    def CP(eng, reads, writes, out, in_):
        if eng == "act":
            return ACT(reads, writes, out, in_, AF.Copy)
        return P.op(eng, lambda e: e.tensor_copy(out=out, in_=in_), reads, writes)

    def MEMSET(eng, writes, ap, val):
        return P.op(eng, lambda e: e.memset(ap, val), (), writes)

    def RECIP(reads, writes, out, in_):
        return P.op("dve", lambda e: e.reciprocal(out=out, in_=in_), reads, writes)

    psT = nc.alloc_psum_tensor("psT", [128, 1024], BF16)
    psI = [nc.alloc_psum_tensor("psI%d" % i, [128, 512], F32) for i in range(2)]
    psE = nc.alloc_psum_tensor("psE", [128, 2048], F32)
    psBU = [psE[:, 0:1024], psE[:, 1024:2048]]
    psX = nc.alloc_psum_tensor("psX", [128, 512], F32)
    bpsT = [Buf("psT", True)]
    bpsI = [Buf("psI%d" % i, True) for i in range(2)]
    bpsBU = [Buf("psBU%d" % i, True) for i in range(2)]
    b_psX = Buf("psX", True)

    ident = sb("ident", [128, 128]); b_ident = Buf("ident")
    identb = sb("identb", [128, 128], BF16); b_identb = Buf("identb")
    onesb = sb("onesb", [128, 128], BF16); b_onesb = Buf("onesb")
    xT = sb("xT", [128, 8, NT], BF16); b_xT = [Buf("xT%d" % i) for i in range(17)]
    uT = sb("uT", [128, 4, NT], BF16); b_uT = [Buf("uT%d" % i) for i in range(4)]
    b_gT = Buf("gT")
    b_mixH = [Buf("mixH%d" % i) for i in range(4)]
    b_wo = Buf("wo")
    wg = sb("wg", [128, 4, 512], BF16); b_wg = Buf("wg")
    wst = [sb("wst%d" % i, [128, 8, 128]) for i in range(2)]; b_wst = [Buf("wst%d" % i) for i in range(2)]
    wbf = [sb("wbf%d" % i, [128, 8, 128], BF16) for i in range(2)]; b_wbf = [Buf("wbf%d" % i) for i in range(2)]
    gcol = sb("gcol", [128, 8]); b_gcol = Buf("gcol")
    fgb = sb("fgb", [128, D]); b_fgb = Buf("fgb")
    segm = sb("segm", [128, 512 + NSAMP]); b_segm = Buf("segm")
    mask2 = sb("mask2", [128, 128]); b_mask2 = Buf("mask2")
    masks = sb("masks", [64, 64]); b_masks = Buf("masks")
    rowm = sb("rowm", [64, NS]); b_rowm = Buf("rowm")
    glm = sb("glm", [128, 2]); b_glm = Buf("glm")
    rm4 = sb("rm4", [128, 4]); b_rm4 = Buf("rm4")
    dcol = sb("dcol", [128, 4]); b_dcol = Buf("dcol")
    nbg = sb("nbg", [128, 4]); b_nbg = Buf("nbg")
    pbg = sb("pbg", [128, 4])
    lbc = sb("lbc", [128, 2, 4]); b_lbc = Buf("lbc")
    nom = sb("nom", [128, 4]); b_nom = Buf("nom")
    ogc = sb("ogc", [128, 4]); b_ogc = Buf("ogc")
    xin = [sb("xin%d" % i, [128, D]) for i in range(2)]; b_xin = [Buf("xin%d" % i) for i in range(2)]
    xnb = [sb("xnb%d" % i, [128, D], BF16) for i in range(2)]; b_xnb = [Buf("xnb%d" % i) for i in range(2)]
    junk = sb("junk", [128, D], BF16); b_junk = Buf("junk")
    stat = [sb("stat%d" % i, [128, 4]) for i in range(2)]; b_stat = [Buf("stat%d" % i) for i in range(2)]

    P.dma(ident[:], c_ident[:, :], writes=[b_ident])
    CP("dve", [b_ident], [b_identb], identb[:], ident[:])
    MEMSET("pool", [b_onesb], onesb[:], 1.0)
    P.dma(gcol[:], norm_g.rearrange("(k p) -> p k", p=128), writes=[b_gcol], allow_slow_non_contiguous=True)
    P.dma(fgb[:], fin_g[0:1, :].partition_broadcast(128), writes=[b_fgb])
    P.dma(segm[:], c_seg[0:1, :].partition_broadcast(128), writes=[b_segm])
    P.dma(mask2[:], c_mask2[:, :], writes=[b_mask2])
    P.dma(masks[:], c_masks[:, :], writes=[b_masks])
    P.dma(rowm[:], c_rowm[:, :], writes=[b_rowm])
    P.dma(glm[:], c_glm[:, :], writes=[b_glm])
    P.dma(rm4[:], c_rm4[:, :], writes=[b_rm4])
    P.dma(dcol[:], s5_d.rearrange("(t p) -> p t", p=128), writes=[b_dcol], allow_slow_non_contiguous=True)
    P.dma(nbg[:], b_glu.rearrange("(t p) -> p t", p=128), writes=[b_nbg], allow_slow_non_contiguous=True)
    CP("dve", [b_nbg], [b_nbg], pbg[:], nbg[:])
    TS("dve", [b_nbg], [b_nbg], nbg[:], nbg[:], -1.0, None, ALU.mult)
    P.dma(ogc[:], onorm_g.rearrange("(t p) -> p t", p=128), writes=[b_ogc], allow_slow_non_contiguous=True)
    P.dma(lbc[:], lb_logits.rearrange("r (t p) -> p r t", p=128), writes=[b_lbc], allow_slow_non_contiguous=True)
    TT("dve", [b_lbc], [b_nom], nom[:], lbc[:, 1, :], lbc[:, 0, :], ALU.subtract)
    ACT([b_nom], [b_nom], nom[:], nom[:], AF.Exp)
    TS("dve", [b_nom], [b_nom], nom[:], nom[:], 1.0, None, ALU.add)
    RECIP([b_nom], [b_lbc], lbc[:, 0, :], nom[:])
    TS("dve", [b_lbc], [b_lbc], lbc[:, 1, :], lbc[:, 0, :], -1.0, 1.0, ALU.mult, ALU.add)
    TS("dve", [b_lbc], [b_nom], nom[:], lbc[:, 1, :], -1.0, None, ALU.mult)

    b_ea = Buf("ea")
    b_eb2 = Buf("eb2")
    res = [sb("res%d" % i, [128, D]) for i in range(2)]; b_res = [Buf("res%d" % i) for i in range(2)]
    m_s5p = A.mark()

    def phase_a():
        for tt in range(17):
            rows = 128 if tt < 16 else NSAMP
            src = xp[tt * 128:(tt + 1) * 128, :] if tt < 16 else xs[:, :]
            s = tt % 2
            P.dma(xin[s][0:rows, :], src, writes=[b_xin[s]])
            ACT([b_xin[s]], [b_junk, b_stat[s]], junk[0:rows, :], xin[s][0:rows, :], AF.Square, accum_out=stat[s][0:rows, 0:1])
            ACT([b_stat[s]], [b_stat[s]], stat[s][0:rows, 1:2], stat[s][0:rows, 0:1], AF.Ln, scale=1.0 / D, bias=EPS)
            ACT([b_stat[s]], [b_stat[s]], stat[s][0:rows, 2:3], stat[s][0:rows, 1:2], AF.Exp, scale=-0.5)
            ACT([b_xin[s], b_stat[s]], [b_xnb[s]], xnb[s][0:rows, :], xin[s][0:rows, :], AF.Copy, scale=stat[s][0:rows, 2:3])

            def tr(e, s=s, rows=rows):
                ins = None
                for kd in range(8):
                    ins = e.transpose(out=psT[:, kd * 128:kd * 128 + rows], in_=xnb[s][0:rows, kd * 128:(kd + 1) * 128],
                                      identity=identb[0:rows, 0:rows])
                return ins
            P.op("pe", tr, [b_xnb[s], b_identb], bpsT)
            c0 = tt * 128
            TT("dve", bpsT + [b_gcol], [b_xT[tt]], xT[:, :, c0:c0 + rows],
               psT[:, :].rearrange("p (k c) -> p k c", k=8)[:, :, 0:rows],
               gcol[:, :].unsqueeze(2).to_broadcast([128, 8, rows]), ALU.mult)
            yield

    wcount = [0]
    wcache = {}

    def _load_w(col0):
        s = wcount[0] % 2
        wcount[0] += 1
        P.dma(wst[s][:], w_in[:, col0:col0 + 128].rearrange("(k p) c -> p k c", p=128), writes=[b_wst[s]])
        CP("act", [b_wst[s]], [b_wbf[s]], wbf[s][:], wst[s][:])
        return wbf[s], b_wbf[s]

    def load_w_tile(col0, nxt=None):
        if col0 in wcache:
            r = wcache.pop(col0)
        else:
            r = _load_w(col0)
        if nxt is not None and nxt not in wcache:
            wcache[nxt] = _load_w(nxt)
        return r

    icount = [0]
    ibanks2 = [(psI[0], bpsI[0]), (psI[1], bpsI[1])]
    ibanks4 = ibanks2 + [(psX, b_psX), (psT[:, :].bitcast(F32), bpsT[0])]
    ibank_sel = [ibanks2]

    def inproj_chunk(wt, bw, c0, n):
        banks = ibank_sel[0]
        s = icount[0] % len(banks)
        icount[0] += 1
        pst_, bst_ = banks[s]
        tts = sorted(set([c0 // 128 + i for i in range((n + 127) // 128)]))

        def mm(e):
            ins = None
            for kd in range(8):
                ins = e.matmul(pst_[:, 0:n], lhsT=wt[:, kd, :], rhs=xT[:, kd, c0:c0 + n], start=(kd == 0), stop=(kd == 7))
            return ins
        P.op("pe", mm, [bw] + [b_xT[t] for t in tts], [bst_])
        return pst_[:, 0:n], bst_

    gT = sb("gT", [128, 4, NT], BF16)
    prm = sb("prm", [128, 7, 16]); b_prm = Buf("prm")
    prm8 = sb("prm8", [128, 2, 3, 16]); b_prm8 = Buf("prm8")
    Apw = sb("Apw", [128, 3, 9, 16]); b_Apw = Buf("Apw")
    PK = sb("PK", [128, 4, 2, 8, 128], BF16); b_PK = Buf("PK")
    CWt = sb("CWt", [128, 16, 8, 2, 32], BF16); b_CWt = Buf("CWt")
    KT = sb("KT", [128, 4, 8, 128], BF16); b_KT = Buf("KT")
    TCB = 64
    Ut = sb("Ut", [128, 2, 16, TCB]); b_Ut = Buf("Ut")
    pw = sb("pw", [128, 2, 2, 16]); b_pw = Buf("pw")
    h0s = sb("h0s", [128, 2, 16, NS]); b_h0s = Buf("h0s")
    carry = sb("carry", [128, 2, 16]); b_carry = [Buf("carry%d" % i) for i in range(4)]
    hS = sb("hS", [128, 2, 16, NS]); b_hS = Buf("hS")
    m_loop = A.mark()

    def s5_prep():
        rl = sb("rl", [16, 16, 128]); b_rl = Buf("rl")
        lsr = sb("lsr", [16, 2]); b_lsr = Buf("lsr")
        rli = sb("rli", [16, 128], I32); b_rli = Buf("rli")
        R = lambda k: rl[:, k, :]
        R3 = lambda k: rl[:, k, :].rearrange("j (gl p) -> j gl p", gl=2)
        b_rl1 = Buf("rl1")
        P.dma(R(0), lam_re.rearrange("(j gl) p -> j (gl p)", gl=2), writes=[b_rl])
        P.dma(R(1), lam_im.rearrange("(j gl) p -> j (gl p)", gl=2), writes=[b_rl1])
        P.dma(lsr[:], log_step.rearrange("(j gl) -> j gl", gl=2), writes=[b_lsr])
        ACT([b_lsr], [b_lsr], lsr[:], lsr[:], AF.Exp)
        dtb = lsr[:, :].unsqueeze(2).to_broadcast([16, 2, 64])
        rr, rw = [b_rl], [b_rl]
        TS("dve", rr + [b_rl1], rw, R(0), R(0), -1e-4, None, ALU.min)
        TT("dve", rr + [b_lsr], rw, R3(2), R3(0), dtb, ALU.mult)
        TT("dve", rr + [b_lsr], rw, R3(3), R3(1), dtb, ALU.mult)
        ACT(rr, rw, R(4), R(2), AF.Exp)
        TS("dve", rr, rw, R(3), R(3), 1.0 / (2.0 * math.pi), None, ALU.mult)

        def wrap(dst, src, add):
            TS("dve", rr, rw, dst, src, add, None, ALU.add)
            CP("dve", rr, [b_rli], rli[:], dst)
            CP("dve", [b_rli], rw, R(11), rli[:])
            TT("dve", rr, rw, dst, dst, R(11), ALU.subtract)
            TS("dve", rr, rw, R(11), dst, 0.5, None, ALU.is_gt)
            TT("dve", rr, rw, dst, dst, R(11), ALU.subtract)
            TS("dve", rr, rw, R(11), dst, -0.5, None, ALU.is_lt)
            TT("dve", rr, rw, dst, dst, R(11), ALU.add)
        wrap(R(5), R(3), 0.0)
        wrap(R(6), R(3), 0.25)
        ACT(rr, rw, R(5), R(5), AF.Sin, scale=6.28318)
        ACT(rr, rw, R(6), R(6), AF.Sin, scale=6.28318)
        TT("dve", rr, rw, R(7), R(4), R(6), ALU.mult)
        TT("dve", rr, rw, R(8), R(4), R(5), ALU.mult)
        TT("dve", rr, rw, R(12), R(0), R(0), ALU.mult)
        TT("dve", rr, rw, R(13), R(1), R(1), ALU.mult)
        TT("dve", rr, rw, R(12), R(12), R(13), ALU.add)
        RECIP(rr, rw, R(12), R(12))
        TS("dve", rr, rw, R(13), R(7), -1.0, None, ALU.add)
        TT("dve", rr, rw, R(14), R(13), R(0), ALU.mult)
        TT("dve", rr, rw, R(15), R(8), R(1), ALU.mult)
        TT("dve", rr, rw, R(14), R(14), R(15), ALU.add)
        TT("dve", rr, rw, R(9), R(14), R(12), ALU.mult)
        TT("dve", rr, rw, R(14), R(8), R(0), ALU.mult)
        TT("dve", rr, rw, R(15), R(13), R(1), ALU.mult)
        TT("dve", rr, rw, R(14), R(14), R(15), ALU.subtract)
        TT("dve", rr, rw, R(10), R(14), R(12), ALU.mult)

        order = [7, 8, 9, 10, 6, 5, 4]

        def trp(e):
            ins = None
            for k, slot in enumerate(order):
                ins = e.transpose(out=psI[0][:, k * 16:(k + 1) * 16], in_=R(slot), identity=ident[0:16, 0:16])
            return ins
        P.op("pe", trp, rr + [b_ident], [bpsI[0]])
        CP("dve", [bpsI[0]], [b_prm], prm[:, :, :], psI[0][:, 0:112].rearrange("p (k j) -> p k j", k=7))

        Bs = sb("Bs", [128, 2, 16, 16]); b_Bs = Buf("Bs")
        bb = sb("bb", [128, 2, 16, 16]); b_bb = Buf("bb")
        T12 = sb("T12", [128, 2, 16, 32]); b_T12 = Buf("T12")
        b_Bs1 = Buf("Bs1")
        P.dma(Bs[:, 0, :, :], b_re.rearrange("(j gl) p c -> (gl p) j c", gl=2), writes=[b_Bs])
        P.dma(Bs[:, 1, :, :], b_im.rearrange("(j gl) p c -> (gl p) j c", gl=2), writes=[b_Bs1])
        zrb = prm[:, 2, :].unsqueeze(2).to_broadcast([128, 16, 16])
        zib = prm[:, 3, :].unsqueeze(2).to_broadcast([128, 16, 16])
        t1 = T12[:, 0, :, 0:16]
        t2 = T12[:, 1, :, 0:16]
        TT("dve", [b_Bs, b_Bs1, b_prm], [b_T12], t1, Bs[:, 0, :, :], zrb, ALU.mult)
        TT("dve", [b_Bs, b_prm, b_T12], [b_T12], t2, Bs[:, 1, :, :], zib, ALU.mult)
        TT("dve", [b_T12], [b_bb], bb[:, 0, :, :], t1, t2, ALU.subtract)
        TT("dve", [b_Bs, b_prm, b_bb], [b_T12], t1, Bs[:, 1, :, :], zrb, ALU.mult)
        TT("dve", [b_Bs, b_prm, b_T12], [b_T12], t2, Bs[:, 0, :, :], zib, ALU.mult)
        TT("dve", [b_T12, b_bb], [b_bb], bb[:, 1, :, :], t1, t2, ALU.add)

        yield
        io = [b_Apw, b_prm, b_T12]
        q1 = T12[:, 0, :, 16]
        q2 = T12[:, 1, :, 16]
        MEMSET("dve", [b_Apw], Apw[:, 0, 0, :], 1.0)
        MEMSET("dve", [b_Apw], Apw[:, 1:3, 0, :], 0.0)
        CP("dve", io, [b_Apw], Apw[:, 0:2, 1, :], prm[:, 0:2, :])
        TS("dve", io, [b_Apw], Apw[:, 2, 1, :], prm[:, 1, :], -1.0, None, ALU.mult)
        for k in range(2, 9):
            TT("dve", io, [b_T12], q1, Apw[:, 0, k - 1, :], prm[:, 0, :], ALU.mult)
            TT("dve", io, [b_T12], q2, Apw[:, 1, k - 1, :], prm[:, 1, :], ALU.mult)
            TT("dve", io, [b_Apw], Apw[:, 0, k, :], q1, q2, ALU.subtract)
            TT("dve", io, [b_T12], q1, Apw[:, 0, k - 1, :], prm[:, 1, :], ALU.mult)
            TT("dve", io, [b_T12], q2, Apw[:, 1, k - 1, :], prm[:, 0, :], ALU.mult)
            TT("dve", io, [b_Apw], Apw[:, 1, k, :], q1, q2, ALU.add)
            TS("dve", io, [b_Apw], Apw[:, 2, k, :], Apw[:, 1, k, :], -1.0, None, ALU.mult)
        yield
        io8 = [b_prm8, b_prm, b_T12]
        CP("dve", io8, [b_prm8], prm8[:, 1, :, :], prm[:, 4:7, :])
        cur = 1
        for _ in range(3):
            nx = 1 - cur
            c_, s_, r_ = prm8[:, cur, 0, :], prm8[:, cur, 1, :], prm8[:, cur, 2, :]
            TT("dve", io8, [b_T12], q1, c_, c_, ALU.mult)
            TT("dve", io8, [b_T12], q2, s_, s_, ALU.mult)
            TT("dve", io8, [b_prm8], prm8[:, nx, 0, :], q1, q2, ALU.subtract)
            STT(io8, [b_prm8], prm8[:, nx, 1, :], c_, 2.0, s_, ALU.mult, ALU.mult)
            TT("dve", io8, [b_prm8], prm8[:, nx, 2, :], r_, r_, ALU.mult)
            cur = nx
        assert cur == 0

        yield
        Cn = sb("Cn", [128, 2, 2, 128]); b_Cn = Buf("Cn")
        b_Cnl = []
        for x, csrc in enumerate((c_re, c_im)):
            for j in range(16):
                b_Cnl.append(Buf("Cn%d_%d" % (x, j)))
                P.dma(Cn[16 * (j % 8):16 * (j % 8) + 16, x, j // 8, :].rearrange("c (gl p) -> c gl p", gl=2),
                      csrc[2 * j:2 * j + 2, :, :].rearrange("gl c p -> c gl p"), writes=[b_Cnl[-1]])

        def trc(e):
            ins = None
            for x in range(2):
                for jj in range(2):
                    sl = (x * 2 + jj) * 128
                    ins = e.transpose(out=psI[1][:, sl:sl + 128], in_=Cn[:, x, jj, :], identity=ident[:, :])
            return ins
        P.op("pe", trc, b_Cnl + [b_ident], [bpsI[1]])
        Csl = sb("Csl", [128, 3, 16, 16]); b_Csl = Buf("Csl")
        for x in range(2):
            for jj in range(2):
                sl = (x * 2 + jj) * 128
                CP("dve", [bpsI[1], b_Csl], [b_Csl], Csl[:, x, 8 * jj:8 * jj + 8, :],
                   psI[1][:, sl:sl + 128].rearrange("p (j c) -> p j c", j=8))
        TS("dve", [b_Csl], [b_Csl], Csl[:, 2, :, :], Csl[:, 1, :, :], -1.0, None, ALU.mult)
        yield
        Czp = sb("Czp", [128, 16, 2, 128], BF16); b_Czp = Buf("Czp")
        MEMSET("pool", [b_Czp], Czp[:], 0.0)
        glm4t = glm[:, :].unsqueeze(1).unsqueeze(3).to_broadcast([128, 4, 2, 16])
        glm4 = glm[:, :].unsqueeze(1).unsqueeze(3).to_broadcast([128, 16, 2, 16])
        for jm in range(4):
            for x in range(2):
                TT("pool", [b_Csl, b_glm, b_Czp], [b_Czp],
                   Czp[:, jm::4, x, 32 * jm:32 * jm + 32].rearrange("p t (g c) -> p t g c", g=2),
                   Csl[:, (0 if x == 0 else 2), jm::4, :].unsqueeze(2).to_broadcast([128, 4, 2, 16]), glm4t, ALU.mult)
        yield
        Pq = sb("Pq", [128, 2, 16, 16]); b_Pq = Buf("Pq")
        T12p = sb("T12p", [128, 2, 16, 16]); b_T12p = Buf("T12p")
        u1 = T12p[:, 0, :, :]
        u2 = T12p[:, 1, :, :]
        for tau in range(8):
            yield
            k = tau + 1
            Ar = Apw[:, 0, k, :].unsqueeze(2).to_broadcast([128, 16, 16])
            Ai = Apw[:, 1, k, :].unsqueeze(2).to_broadcast([128, 16, 16])
            nAi = Apw[:, 2, k, :].unsqueeze(2).to_broadcast([128, 16, 16])
            ioc = [b_Csl, b_Apw, b_T12p, b_Pq]
            TT("pool", ioc, [b_T12p], u1, Csl[:, 0, :, :], Ar, ALU.mult)
            TT("pool", ioc, [b_T12p], u2, Csl[:, 1, :, :], Ai, ALU.mult)
            TT("pool", ioc, [b_Pq], Pq[:, 0, :, :], u1, u2, ALU.subtract)
            TT("pool", ioc, [b_T12p], u1, Csl[:, 0, :, :], nAi, ALU.mult)
            TT("pool", ioc, [b_T12p], u2, Csl[:, 1, :, :], Ar, ALU.mult)
            TT("pool", ioc, [b_Pq], Pq[:, 1, :, :], u1, u2, ALU.subtract)
            for x in range(2):
                TT("pool", [b_Pq, b_glm, b_CWt], [b_CWt], CWt[:, :, tau, x, :].rearrange("p j (g c) -> p j g c", g=2),
                   Pq[:, x, :, :].unsqueeze(2).to_broadcast([128, 16, 2, 16]), glm4, ALU.mult)

        yield
        Xx = sb("Xx", [128, 2, 16, 16]); b_Xx = Buf("Xx")
        XKb = [sb("XKb", [128, 2, 16, 2, 16], BF16) for _ in range(2)]; b_XKb = [Buf("XKb") for _ in range(2)]
        for k in range(8):
            yield
            i = k % 2
            if k == 0:
                Xsrc, b_Xsrc = bb, b_bb
            else:
                Ar = Apw[:, 0, k, :].unsqueeze(2).to_broadcast([128, 16, 16])
                Ai = Apw[:, 1, k, :].unsqueeze(2).to_broadcast([128, 16, 16])
                iox = [b_bb, b_Apw, b_T12, b_Xx]
                TT("dve", iox, [b_T12], t1, bb[:, 0, :, :], Ar, ALU.mult)
                TT("dve", iox, [b_T12], t2, bb[:, 1, :, :], Ai, ALU.mult)
                TT("dve", iox, [b_Xx], Xx[:, 0, :, :], t1, t2, ALU.subtract)
                TT("dve", iox, [b_T12], t1, bb[:, 0, :, :], Ai, ALU.mult)
                TT("dve", iox, [b_T12], t2, bb[:, 1, :, :], Ar, ALU.mult)
                TT("dve", iox, [b_Xx], Xx[:, 1, :, :], t1, t2, ALU.add)
                Xsrc, b_Xsrc = Xx, b_Xx
            for x in range(2):
                TT("dve", [b_Xsrc, b_glm, b_XKb[i]], [b_XKb[i]], XKb[i][:, x, :, :, :],
                   Xsrc[:, x, :, :].unsqueeze(2).to_broadcast([128, 16, 2, 16]), glm4, ALU.mult)

            def trk(e, i=i):
                ins = None
                for jt in range(4):
                    for x in range(2):
                        sl = (jt * 2 + x) * 128
                        ins = e.transpose(out=psT[:, sl:sl + 128],
                                          in_=XKb[i][:, x, 4 * jt:4 * jt + 4, :, :].rearrange("p j g c -> p (j g c)"),
                                          identity=identb[:, :])
                return ins
            P.op("pe", trk, [b_XKb[i], b_identb], bpsT)
            CP("act", bpsT, [b_PK], PK[:, :, :, k, :].rearrange("p t x c -> p (t x) c"),
               psT[:, :].rearrange("p (s c) -> p s c", s=8))
            kb = k % 2

            def mk(e, i=i, kb=kb):
                ins = None
                for jt in range(4):
                    for jm in range(4):
                        j = 4 * jt + jm
                        for x in range(2):
                            ins = e.matmul(psI[kb][32 * jm:32 * jm + 32, jt * 128:(jt + 1) * 128],
                                           lhsT=XKb[i][:, x, j, :, :].rearrange("p g c -> p (g c)"), rhs=Czp[:, j, x, :],
                                           start=(x == 0), stop=(x == 1), tile_position=(0, 32 * jm))
                return ins
            P.op("pe", mk, [b_XKb[i], b_Czp], [bpsI[kb]])
            if k == 0:
                for jt in range(4):
                    STT([b_ident, b_dcol], [b_KT, bpsI[kb]], KT[:, jt, 0, :], ident[:, :], dcol[:, jt:jt + 1],
                        psI[kb][:, jt * 128:(jt + 1) * 128], ALU.mult, ALU.add)
            else:
                CP("act", [bpsI[kb]], [b_KT], KT[:, :, k, :], psI[kb][:, :].rearrange("p (t c) -> p t c", t=4))

        yield
        h0v = rl[:, :, :].rearrange("s a b -> s (a b)")
        for x in range(2):
            P.dma(h0v, (s5re0 if x == 0 else s5im0)[:, :], writes=[b_rl])

            def trh(e, x=x):
                ins = None
                for j in range(16):
                    ins = e.transpose(out=psBU[x][:, j * 16:(j + 1) * 16], in_=h0v[:, 128 * j:128 * j + 128],
                                      identity=ident[0:16, 0:16])
                return ins
            P.op("pe", trh, [b_rl, b_ident], [bpsBU[x]])
            CP("act", [bpsBU[x]], [b_h0s], h0s[:, x, :, :], psBU[x][:, 0:256].rearrange("p (j s) -> p j s", j=16))

        yield
        CP("pool", [b_prm8], [b_Ut], Ut[:, 0, :, 0], prm8[:, 0, 0, :])
        CP("pool", [b_prm8, b_Ut], [b_Ut], Ut[:, 1, :, 0], prm8[:, 0, 1, :])
        CP("pool", [b_prm8], [b_pw], pw[:, 0, :, :], prm8[:, 0, 0:2, :])
        n = 1
        cur = 0
        ta = T12p[:, :, :, :].rearrange("p x j c -> p (x j c)").rearrange("p (j n) -> p j n", n=32)
        tb = Pq[:, :, :, :].rearrange("p x j c -> p (x j c)").rearrange("p (j n) -> p j n", n=32)
        while n < TCB:
            yield
            cn = pw[:, cur, 0, :].unsqueeze(2).to_broadcast([128, 16, n])
            sn = pw[:, cur, 1, :].unsqueeze(2).to_broadcast([128, 16, n])
            ur = Ut[:, 0, :, 0:n]
            ui = Ut[:, 1, :, 0:n]
            io = [b_Ut, b_pw, b_T12p, b_Pq]
            TT("pool", io, [b_T12p], ta[:, :, 0:n], ur, cn, ALU.mult)
            TT("pool", io, [b_T12p], tb[:, :, 0:n], ui, sn, ALU.mult)
            TT("pool", io, [b_Ut], Ut[:, 0, :, n:2 * n], ta[:, :, 0:n], tb[:, :, 0:n], ALU.subtract)
            TT("pool", io, [b_T12p], ta[:, :, 0:n], ur, sn, ALU.mult)
            TT("pool", io, [b_T12p], tb[:, :, 0:n], ui, cn, ALU.mult)
            TT("pool", io, [b_Ut], Ut[:, 1, :, n:2 * n], ta[:, :, 0:n], tb[:, :, 0:n], ALU.add)
            if 2 * n < TCB:
                c_ = pw[:, cur, 0, :]
                s_ = pw[:, cur, 1, :]
                nx = 1 - cur
                TT("pool", io, [b_T12p], ta[:, :, 0], c_, c_, ALU.mult)
                TT("pool", io, [b_T12p], tb[:, :, 0], s_, s_, ALU.mult)
                TT("pool", io, [b_pw], pw[:, nx, 0, :], ta[:, :, 0], tb[:, :, 0], ALU.subtract)
                TT("pool", io, [b_T12p], tb[:, :, 0], c_, s_, ALU.mult)
                TS("pool", io, [b_pw], pw[:, nx, 1, :], tb[:, :, 0], 2.0, 1.0, ALU.mult, ALU.mult)
                cur = nx
            n *= 2

    def s5_loop():
        NB = LP // 8
        NCH = NB // TCB
        um = [sb("um", [128, NT], BF16) for _ in range(2)]; b_um = [Buf("um") for _ in range(2)]
        Hp = [sb("Hp", [128, 4, 2, NB + 8], BF16) for _ in range(2)]; b_Hp = [Buf("Hp") for _ in range(2)]
        HpS = [sb("HpS", [128, 4, 2, NS], BF16) for _ in range(2)]; b_HpS = [Buf("HpS") for _ in range(2)]
        tm = sb("tm", [128, 4, 4, TCB]); b_tm = Buf("tm")
        gin = sb("gin", [128, NCH, 2, 4, TCB]); b_gin = Buf("gin")
        gflat = gin[:, :, :, :, :].rearrange("p h x j c -> p (h x j c)")
        glu_tmp2.append((gflat[:, 0:512], gflat[:, 1024:1536], b_gin))
        r8m = xnb[0][:, :].bitcast(F32).rearrange("p (x j c) -> p x j c", x=2, j=4); b_r8m = Buf("r8m")
        cinj = sb("cinj", [128, 2, 4]); b_cinj = Buf("cinj")
        G2 = [sb("G", [128, 2, 4, TCB]) for _ in range(2)]; b_G2 = [Buf("G") for _ in range(2)]
        dmd, b_dmd = tm, b_tm
        gcnt = [0]
        lc = sb("lc", [128, 4, 4]); b_lc = Buf("lc")
        ls_ = sb("ls_", [128, 4, 4, NS]); b_ls = Buf("ls")
        bpsE = bpsBU
        Ev = psE[:, :].rearrange("p (j x c) -> p j x c", j=4, x=2)
        Es = psX[:, 0:8 * NS].rearrange("p (j x s) -> p j x s", j=4, x=2)
        for p_ in range(2):
            MEMSET("pool", [b_Hp[p_]], Hp[p_][:, :, :, 0:1], 0.0)
        ycount = [0]

        def stage_e(jt):
            for jm in range(4):
                j = 4 * jt + jm
                i = j % 2
                ACT([b_uT[jt], b_rm4], [b_um[i]], um[i][:, :], uT[:, jt, :], AF.Copy, scale=rm4[:, jm:jm + 1])

                def me(e, jm=jm, i=i):
                    ins = None
                    for x in range(2):
                        for sg in range(8):
                            ins = e.matmul(Ev[:, jm, x, :], lhsT=PK[:, jt, x, 7 - sg, :], rhs=um[i][:, sg * 256:(sg + 1) * 256],
                                           start=(sg == 0), stop=(sg == 7))
                    return ins
                P.op("pe", me, [b_PK, b_um[i]], bpsE)

                def mes(e, jm=jm, i=i):
                    ins = None
                    for x in range(2):
                        for sg in range(LS):
                            ins = e.matmul(Es[:, jm, x, :], lhsT=PK[:, jt, x, LS - 1 - sg, :], rhs=um[i][:, LP + sg * NS:LP + (sg + 1) * NS],
                                           start=(sg == 0), stop=(sg == LS - 1))
                    return ins
                P.op("pe", mes, [b_PK, b_um[i]], [b_psX])

        def stage_mod(jt):
            js = slice(4 * jt, 4 * jt + 4)
            Cr = Ut[:, 0, js, :].unsqueeze(2).to_broadcast([128, 4, NCH, TCB])
            Ci = Ut[:, 1, js, :].unsqueeze(2).to_broadcast([128, 4, NCH, TCB])
            v4 = lambda ap: ap.rearrange("p j (h c) -> p j h c", h=NCH)
            Br = v4(Ev[:, :, 0, :])
            Bi = v4(Ev[:, :, 1, :])
            gr = gin[:, :, 0, :, :].rearrange("p h j c -> p j h c")
            gi = gin[:, :, 1, :, :].rearrange("p h j c -> p j h c")
            TT("dve", [b_prm8, b_segm, b_r8m], [b_r8m], r8m,
               prm8[:, 0, 2, js].unsqueeze(1).unsqueeze(3).to_broadcast([128, 2, 4, TCB]),
               segm[:, 0:TCB].unsqueeze(1).unsqueeze(2).to_broadcast([128, 2, 4, TCB]), ALU.mult)
            tmp = tm[:, :, :, :]
            rd = bpsE + [b_Ut]
            TT("dve", rd + [b_gin], [b_gin], gr, Br, Cr, ALU.mult)
            TT("dve", rd, [b_tm], tmp, Bi, Ci, ALU.mult)
            TT("dve", [b_tm, b_gin], [b_gin], gr, gr, tmp, ALU.add)
            TT("dve", rd + [b_gin], [b_gin], gi, Bi, Cr, ALU.mult)
            TT("dve", rd + [b_tm], [b_tm], tmp, Br, Ci, ALU.mult)
            TT("dve", [b_tm, b_gin], [b_gin], gi, gi, tmp, ALU.subtract)

        def stage_sample(jt):
            js = slice(4 * jt, 4 * jt + 4)
            p_ = jt % 2
            A4r = Apw[:, 0, LS, js].unsqueeze(2).to_broadcast([128, 4, NS])
            A4i = Apw[:, 1, LS, js].unsqueeze(2).to_broadcast([128, 4, NS])
            hr, hi = h0s[:, 0, js, :], h0s[:, 1, js, :]
            io = [b_h0s, b_Apw, b_ls]
            TT("dve", io, [b_ls], ls_[:, 0, :, :], hr, A4r, ALU.mult)
            TT("dve", io, [b_ls], ls_[:, 1, :, :], hi, A4i, ALU.mult)
            TT("dve", io, [b_ls], ls_[:, 2, :, :], hr, A4i, ALU.mult)
            TT("dve", io, [b_ls], ls_[:, 3, :, :], hi, A4r, ALU.mult)
            TT("dve", io, [b_ls], ls_[:, 0, :, :], ls_[:, 0, :, :], ls_[:, 1, :, :], ALU.subtract)
            TT("dve", io, [b_ls], ls_[:, 2, :, :], ls_[:, 2, :, :], ls_[:, 3, :, :], ALU.add)
            TT("dve", [b_ls, b_hS], [b_hS, b_psX], hS[:, 0, js, :], ls_[:, 0, :, :], Es[:, :, 0, :], ALU.add)
            TT("dve", [b_ls, b_hS], [b_hS, b_psX], hS[:, 1, js, :], ls_[:, 2, :, :], Es[:, :, 1, :], ALU.add)
            for x in range(2):
                CP("act", [b_h0s, b_HpS[p_]], [b_HpS[p_]], HpS[p_][:, :, x, :], h0s[:, x, js, :])

        def stage_scan(jt, ch):
            G, b_G = G2[gcnt[0] % 2], b_G2[gcnt[0] % 2]
            gcnt[0] += 1
            js = slice(4 * jt, 4 * jt + 4)
            p_ = jt % 2
            cs = slice(ch * TCB, (ch + 1) * TCB)
            n = TCB
            gch = gin[:, ch, :, :, :]
            if ch > 0:
                TT("dve", [b_carry[jt], b_prm8], [b_cinj], cinj[:, :, :], carry[:, :, js],
                   prm8[:, 0, 2, js].unsqueeze(1).to_broadcast([128, 2, 4]), ALU.mult)
                TT("dve", [b_cinj, b_gin], [b_gin], gch[:, :, :, 0], gch[:, :, :, 0], cinj[:, :, :], ALU.add)
            P.op("dve", lambda e: e.tensor_tensor_scan(out=G[:, :, :, :].rearrange("p x j c -> p (x j c)"),
                                                        data0=r8m.rearrange("p x j c -> p (x j c)"),
                                                        data1=gch.rearrange("p x j c -> p (x j c)"),
                                                        initial=0.0, op0=ALU.mult, op1=ALU.add),
                 [b_gin, b_r8m], [b_G])
            Cr = Ut[:, 0, js, :]
            Ci = Ut[:, 1, js, :]
            Gr, Gi = G[:, 0, :, :], G[:, 1, :, :]
            Gx = G[:, :, :, :]
            Crx = Cr.unsqueeze(1).to_broadcast([128, 2, 4, TCB])
            Cix = Ci.unsqueeze(1).to_broadcast([128, 2, 4, TCB])
            TT("dve", [b_G, b_Ut], [b_dmd], dmd[:, 0:2, :, :], Gx, Crx, ALU.mult)
            TT("dve", [b_G, b_Ut], [b_dmd], dmd[:, 2:4, :, :], Gx, Cix, ALU.mult)
            D = [dmd[:, 0, :, :], dmd[:, 3, :, :], dmd[:, 2, :, :], dmd[:, 1, :, :]]
            TT("dve", [b_dmd], [b_carry[jt]], carry[:, 0, js], D[0][:, :, n - 1], D[1][:, :, n - 1], ALU.subtract)
            TT("dve", [b_dmd, b_carry[jt]], [b_carry[jt]], carry[:, 1, js], D[2][:, :, n - 1], D[3][:, :, n - 1], ALU.add)
            hs_ = slice(ch * TCB + 1, (ch + 1) * TCB + 1)
            TT("dve", [b_dmd, b_Hp[p_]], [b_Hp[p_]], Hp[p_][:, :, 0, hs_], D[0], D[1], ALU.subtract)
            TT("dve", [b_dmd, b_Hp[p_]], [b_Hp[p_]], Hp[p_][:, :, 1, hs_], D[2], D[3], ALU.add)

        ga_s = [ea, xnb[0][:, :].bitcast(F32)]; b_ga_s = [b_ea, Buf("ga2")]
        ge_s = [eb2, xnb[1][:, :].bitcast(F32)]; b_ge_s = [b_eb2, Buf("ge2")]
        gst = [junk[:, 0:512], junk[:, 512:1024]]; b_gst = [Buf("gst0"), Buf("gst1")]
        ybanks = [(psI[0][:, :], bpsI[0]), (psI[1][:, :], bpsI[1]), (psT[:, :].bitcast(F32), bpsT[0])]

        def make_y(jt, g):
            p_ = jt % 2
            i = ycount[0]
            ycount[0] += 1
            st_ = i % 2
            ybank, b_yb = ybanks[i % 3]
            sample = (g == NCH)
            if not sample:
                c0 = g * 64
                t0 = 8 * c0
                nel, nt = 512, 8
                yb = ybank[:, 0:512]
                yv = yb.rearrange("p (s c) -> p s c", s=8)
                uv = uT[:, jt, 0:LP].rearrange("p (s c) -> p s c", s=8)[:, :, c0:c0 + 64]
                gv = gT[:, jt, t0:t0 + 512].rearrange("p (c s) -> p s c", s=8)
                hv = lambda jm, x: Hp[p_][:, jm, x, c0:c0 + 64]
                bh = b_Hp[p_]
            else:
                nel, nt = NSAMP, LS
                yb = ybank[:, 0:NSAMP]
                yv = yb.rearrange("p (t s) -> p t s", t=LS)
                uv = uT[:, jt, LP:NT].rearrange("p (t s) -> p t s", t=LS)
                gv = gT[:, jt, LP:NT].rearrange("p (s t) -> p t s", t=LS)
                hv = lambda jm, x: HpS[p_][:, jm, x, :]
                bh = b_HpS[p_]
            a = ga_s[st_][:, 0:nel]
            e_ = ge_s[st_][:, 0:nel]
            b_ga, b_ge = b_ga_s[st_], b_ge_s[st_]
            gs = gst[st_][:, 0:nel]
            gsv = gs.rearrange("p (s c) -> p s c", s=nt)

            def g1():
                def my(e):
                    ins = None
                    for k in range(nt):
                        ins = e.matmul(yv[:, k:nt, :], lhsT=KT[:, jt, k, :], rhs=uv[:, 0:nt - k, :], start=(k == 0), stop=False,
                                       skip_group_check=True)
                    for jm in range(4):
                        j = 4 * jt + jm
                        for tau in range(nt):
                            for x in range(2):
                                last = (jm == 3 and tau == nt - 1 and x == 1)
                                ins = e.matmul(yv[32 * jm:32 * jm + 32, tau, :], lhsT=CWt[:, j, tau, x, :], rhs=hv(jm, x),
                                               start=False, stop=last, tile_position=(0, 32 * jm), skip_group_check=True)
                    return ins
                P.op("pe", my, [b_KT, b_uT[jt], b_CWt, bh], [b_yb])

            def g2():
                ACT([b_yb], [b_gT], gv, yv, AF.Gelu_apprx_tanh)

            def g3():
                pass

            def g4():
                pass

            def g5():
                pass

            def g6():
                pass
            return (g1, g2, g3, g4, g5, g6)

        items = []
        for jt in range(4):
            for ch in range(NCH + 1):
                items.append((jt, ch))
        ys = {}
        stage_e(0)
        n_items = len(items)
        for idx in range(n_items + 2):
            if idx < n_items:
                jt, ch = items[idx]
                if ch == 0:
                    stage_mod(jt)
                    stage_sample(jt)
                    if jt + 1 < 4:
                        stage_e(jt + 1)
                if ch < NCH:
                    stage_scan(jt, ch)
                ys[idx] = make_y(jt, ch)
                ys[idx][0]()
                ys[idx][1]()
            if 0 <= idx - 1 < n_items:
                ys[idx - 1][2]()
                ys[idx - 1][3]()
            if 0 <= idx - 2 < n_items:
                ys[idx - 2][4]()
                ys[idx - 2][5]()

        tmflat = tm[:, :, :, :].rearrange("p q j c -> p (q j c)")
        sos = [(tmflat[0:16, 0:512], b_tm),
               (G2[0][0:16, :, :, :].rearrange("p x j c -> p (x j c)"), b_G2[0]),
               (G2[1][0:16, :, :, :].rearrange("p x j c -> p (x j c)"), b_G2[1])]
        sk = [0]

        def nxt_so():
            r = sos[sk[0] % 3]
            sk[0] += 1
            return r
        for x, dst in enumerate((o_pre, o_pim)):
            so, b_so = nxt_so()
            P.op("pe", lambda e, x=x: e.transpose(out=psI[x][0:16, 0:128], in_=carry[:, x, :], identity=ident[:, :]),
                 b_carry + [b_ident], [bpsI[x]])
            CP("act", [bpsI[x]], [b_so], so[:, 0:128], psI[x][0:16, 0:128])
            P.dma(dst[:, :], so[:, 0:128], reads=[b_so], is_output=True)
        for x, dst in enumerate((o_sre, o_sim)):
            for qt in range(4):
                so, b_so = nxt_so()

                def trs(e, x=x, qt=qt):
                    ins = None
                    for jj in range(4):
                        j = 4 * qt + jj
                        ins = e.transpose(out=psI[qt % 2][0:16, jj * 128:(jj + 1) * 128], in_=hS[:, x, j, :], identity=ident[:, :])
                    return ins
                P.op("pe", trs, [b_hS, b_ident], [bpsI[qt % 2]])
                CP("act", [bpsI[qt % 2]], [b_so], so[:, :], psI[qt % 2][0:16, :])
                P.dma(dst[:, qt * 512:(qt + 1) * 512], so[:, :], reads=[b_so], is_output=True)

    def load_wg():
        for hf in range(2):
            s = wcount[0] % 2
            wcount[0] += 1
            stv = wst[s][:, :, :].rearrange("p k c -> p (k c)").rearrange("p (k c) -> p k c", k=4)
            P.dma(stv, w_glu[:, hf * 256:(hf + 1) * 256].rearrange("(k p) c -> p k c", p=128), writes=[b_wst[s]])
            CP("act", [b_wst[s]], [b_wg], wg[:, :, hf * 256:(hf + 1) * 256], stv)

    def load_wo():
        wst2 = [sb("wst2", [128, 8, 128]) for _ in range(2)]
        b_wst2 = [Buf("wst2") for _ in range(2)]
        for cb in range(8):
            s = cb % 2
            P.dma(wst2[s][:], w_out[:, cb * 128:(cb + 1) * 128].rearrange("(k p) c -> p k c", p=128), writes=[b_wst2[s]], eng="pool")
            CP("pool", [b_wst2[s]], [b_wo], wo[:, :, cb * 128:(cb + 1) * 128], wst2[s][:])

    def glu():
        ibank_sel[0] = ibanks4
        gcount = [0]
        a2, b2, b_ab2 = glu_tmp2[0]
        sets = [(ea, eb2, b_ea, b_eb2), (a2, b2, b_ab2, b_ab2)]
        for fo in range(4):
            wt, bw = load_w_tile(512 + 128 * fo, 512 + 128 * (fo + 1) if fo < 3 else None)
            for (c0, n) in col_chunks():
                ps1, bps1 = inproj_chunk(wt, bw, c0, n)
                k = gcount[0] % 2
                ps2 = psBU[k][:, 0:n]

                def mg(e, fo=fo, c0=c0, n=n, ps2=ps2):
                    ins = None
                    for kf in range(4):
                        ins = e.matmul(ps2, lhsT=wg[:, kf, 128 * fo:128 * fo + 128], rhs=gT[:, kf, c0:c0 + n],
                                       start=(kf == 0), stop=(kf == 3))
                    return ins
                P.op("pe", mg, [b_wg, b_gT], [bpsBU[k]])
                A_, B_, bA, bB = sets[gcount[0] % 2]
                gcount[0] += 1
                a = A_[:, 0:n]
                b = B_[:, 0:n]
                ACT([bps1], [bA], a, ps1, AF.Sigmoid)
                ACT([bpsBU[k], b_nbg], [bB], b, ps2, AF.Sigmoid, scale=1.0, bias=pbg[:, fo:fo + 1])
                TT("dve", [bA, bB], [bA], a, a, b, ALU.mult)
                TT("dve", [bps1, bA, bB], [bB], b, ps1, a, ALU.mult)
                TT("dve", [bB, b_gT, b_uT[fo]], [b_uT[fo]], uT[:, fo, c0:c0 + n], b, gT[:, fo, c0:c0 + n], ALU.mult)

        ibank_sel[0] = ibanks2

    def hgrn():
        ibank_sel[0] = ibanks4
        hs = []
        for i in range(2):
            hs.append(dict(
                ebt=sb("ebt", [128, NT]), kT=sb("kT", [128, NT], BF16), qT=sb("qT", [128, NT], BF16),
                vtk=sb("vtk", [128, 17, 128], BF16), szh=sb("szh", [128, NT], BF16),
                b_ebt=Buf("ebt"), b_kT=Buf("kT"), b_qT=Buf("qT"), b_vtk=Buf("vtk"), b_szh=Buf("szh")))
        pieces = [res[0][:, 0:512], res[0][:, 512:1024], res[1][:, 0:512], res[1][:, 512:1024],
                  xin[0][:, 0:512], xin[0][:, 512:1024], xin[1][:, 0:512], xin[1][:, 512:1024]]
        tmps = [[(pieces[4 * i + k], Buf("tmp")) for k in range(4)] for i in range(2)]
        NSB = 10
        S0f = [sb("S0f%d" % i, [128, 128]) for i in range(NSB)]; b_S0f = [Buf("S0f%d" % i) for i in range(NSB)]
        S0b = [sb("S0b%d" % i, [128, 128], BF16) for i in range(NSB)]; b_S0b = [Buf("S0b%d" % i) for i in range(NSB)]
        scount = [0]

        def mkctx(hp):
            c = {}
            def two(name, shape, dt=F32):
                c[name] = [sb(name, shape, dt) for _ in range(2)]
                c["b_" + name] = [Buf(name) for _ in range(2)]
            two("S2", [128, 128]); two("Sb", [128, 128], BF16); two("attm", [128, 128], BF16); two("kh", [128, 128], BF16)
            two("khT", [128, 128], BF16); two("o32", [128, 128]); two("osq", [128, 128], BF16); two("rs", [128, 128])
            two("hgt", [128, 128]); two("khm", [64, 128], BF16)
            if hp == 0:
                bk0, bk1, bk2 = Buf("hb0", True), Buf("hb1", True), Buf("hb2", True)
                c["psA"], c["psR"], c["b_psA"], c["b_psR"] = psE[:, 0:128], psE[:, 128:256], bk0, bk0
                c["psO"], c["b_psO"] = psE[:, 512:640], bk1
                c["psS"], c["b_psS"] = [psE[:, 1024:1152], psE[:, 1152:1280]], [bk2, bk2]
            else:
                bk3 = Buf("hb3", True)
                c["psA"], c["psR"], c["b_psA"], c["b_psR"] = psE[:, 1536:1664], psE[:, 1664:1792], bk3, bk3
                c["psO"], c["b_psO"] = psI[0][:, 0:128], bpsI[0]
                c["psS"], c["b_psS"] = [psI[1][:, 0:128], psI[1][:, 128:256]], [bpsI[1], bpsI[1]]
            return c
        ctxs = [mkctx(0), mkctx(1)]
        tcount = [0]
        ccount = [0]

        def seg_of(c0, n):
            return segm[:, 0:n] if c0 < LP else segm[:, 512:512 + n]

        def stage_a(h):
            H = hs[h % 2]
            ebt, b_ebt = H["ebt"], H["b_ebt"]
            lb_ = lbc[:, 0, h:h + 1]
            om_ = lbc[:, 1, h:h + 1]
            nom_ = nom[:, h:h + 1]
            wt, bw = load_w_tile(1536 + 128 * h, 2560 + 128 * h)
            for (c0, n) in col_chunks():
                ps, bps = inproj_chunk(wt, bw, c0, n)
                ACT([bps], [b_ebt], ebt[:, c0:c0 + n], ps, AF.Sigmoid)
                yield
            wt, bw = load_w_tile(2560 + 128 * h, 2048 + 128 * h)
            for (c0, n) in col_chunks():
                Tm = tmps[ccount[0] % 2]
                ccount[0] += 1
                (ta, b_ta) = Tm[0]
                ps, bps = inproj_chunk(wt, bw, c0, n)
                ACT([bps], [b_ta], ta[:, 0:n], ps, AF.Sigmoid)
                TT("dve", [bps, b_ta], [H["b_szh"]], H["szh"][:, c0:c0 + n], ps, ta[:, 0:n], ALU.mult)
                yield
            wt, bw = load_w_tile(2048 + 128 * h, 1024 + 128 * h)

            def v_tile(tt, wt=wt, bw=bw):
                rows = 128 if tt < 16 else NSAMP
                c0 = tt * 128
                s_ = icount[0] % 2
                icount[0] += 1

                def mv(e):
                    ins = None
                    for kd in range(8):
                        ins = e.matmul(psI[s_][0:rows, 0:128], lhsT=xT[:, kd, c0:c0 + rows], rhs=wt[:, kd, :],
                                       start=(kd == 0), stop=(kd == 7))
                    return ins
                P.op("pe", mv, [bw, b_xT[tt]], [bpsI[s_]])
                CP("act", [bpsI[s_]], [H["b_vtk"]], H["vtk"][0:rows, tt, :], psI[s_][0:rows, 0:128])
            vt = 0
            for ci, (c0, n) in enumerate(col_chunks()):
                Tm = tmps[ccount[0] % 2]
                ccount[0] += 1
                (ta, b_ta), (tb, b_tb), (tc_, b_tc), (te, b_te) = Tm
                sg = ebt[:, c0:c0 + n]
                ACT([b_ebt, b_lbc], [b_tb], tb[:, 0:n], sg, AF.Ln, scale=om_, bias=lb_)
                ACT([b_ebt, b_lbc, b_nom], [b_tc], tc_[:, 0:n], sg, AF.Identity, scale=nom_, bias=om_)
                P.op("dve", lambda e, c0=c0, n=n, te=te, tb=tb: e.tensor_tensor_scan(
                    out=te[:, 0:n], data0=seg_of(c0, n), data1=tb[:, 0:n], initial=0.0, op0=ALU.mult, op1=ALU.add),
                    [b_tb, b_segm], [b_te])
                for _ in range(4 if ci < 4 else 1):
                    v_tile(vt)
                    vt += 1
                ACT([b_te], [b_ebt], ebt[:, c0:c0 + n], te[:, 0:n], AF.Exp)
                ACT([b_te, b_tc], [b_ta], ta[:, 0:n], te[:, 0:n], AF.Exp, scale=-1.0)
                TT("dve", [b_tc, b_ta], [H["b_kT"]], H["kT"][:, c0:c0 + n], tc_[:, 0:n], ta[:, 0:n], ALU.mult)
                yield
            assert vt == 17
            wt, bw = load_w_tile(1024 + 128 * h, 1536 + 128 * (h + 1) if h < 3 else None)
            for (c0, n) in col_chunks():
                ps, bps = inproj_chunk(wt, bw, c0, n)
                TT("dve", [bps, b_ebt], [H["b_qT"]], H["qT"][:, c0:c0 + n], ps, ebt[:, c0:c0 + n], ALU.mult)
                yield

        def stage_b(h, cx):
            S2, Sb, attm, kh, khT, o32, osq, rs, hgt, khm = (cx[k] for k in ("S2", "Sb", "attm", "kh", "khT", "o32", "osq", "rs", "hgt", "khm"))
            b_S2, b_Sb, b_attm, b_kh, b_khT, b_o32, b_osq, b_rs, b_hgt, b_khm = (
                cx["b_" + k] for k in ("S2", "Sb", "attm", "kh", "khT", "o32", "osq", "rs", "hgt", "khm"))
            psA, psR, psO, psS = cx["psA"], cx["psR"], cx["psO"], cx["psS"]
            b_psA, b_psR, b_psO, b_psS = cx["b_psA"], cx["b_psR"], cx["b_psO"], cx["b_psS"]

            def evac_o(p, n):
                CP("dve", [b_psO], [b_o32[p]], o32[p][:, 0:n], psO[:, 0:n])
                ACT([b_psO], [b_osq[p]], osq[p][:, 0:n], psO[:, 0:n], AF.Square)

            def finish_o(h, H, p, c0, n):
                P.op("pe", lambda e: e.matmul(psR[:, 0:n], lhsT=onesb[:, :], rhs=osq[p][:, 0:n], start=True, stop=True),
                     [b_onesb, b_osq[p]], [b_psR])
                ACT([b_psR], [b_rs[p]], rs[p][:, 0:n], psR[:, 0:n], AF.Ln, scale=1.0 / 128.0, bias=EPS)
                ACT([b_rs[p]], [b_rs[p]], rs[p][:, 0:n], rs[p][:, 0:n], AF.Exp, scale=-0.5)
                STT([b_o32[p], b_rs[p], b_ogc], [b_hgt[p]], hgt[p][:, 0:n], o32[p][:, 0:n], ogc[:, h:h + 1], rs[p][:, 0:n], ALU.mult, ALU.mult)
                TT("dve", [b_hgt[p], H["b_szh"]], [b_mixH[h]], mixH[:, h, c0:c0 + n], hgt[p][:, 0:n], H["szh"][:, c0:c0 + n], ALU.mult)

            H = hs[h % 2]
            kT, qT, vtk, ebt = H["kT"], H["qT"], H["vtk"], H["ebt"]
            b_kT, b_qT, b_vtk, b_ebt = H["b_kT"], H["b_qT"], H["b_vtk"], H["b_ebt"]
            MEMSET("pool", [b_S2[1]], S2[1][:], 0.0)
            MEMSET("pool", [b_Sb[1]], Sb[1][:], 0.0)

            def T1a(tt):
                c0 = tt * 128
                p = tt % 2
                P.op("pe", lambda e: e.matmul(psA, lhsT=kT[:, c0:c0 + 128], rhs=qT[:, c0:c0 + 128], start=True, stop=True),
                     [b_kT, b_qT], [b_psA])
                TT("dve", [b_psA, b_mask2], [b_attm[p]], attm[p][:, :], psA, mask2[:, :], ALU.mult)
                ACT([b_kT, b_ebt, b_kh[p]], [b_kh[p]], kh[p][:, :], kT[:, c0:c0 + 128], AF.Copy, scale=ebt[:, c0 + 127:c0 + 128])

            def T1b(tt):
                p = tt % 2
                pst = psT[:, 0:128]
                P.op("pe", lambda e: e.transpose(out=pst, in_=kh[p][:, :], identity=identb[:, :]), [b_kh[p], b_identb], [bpsT[0]])
                CP("act", [bpsT[0]], [b_khT[p]], khT[p][:, :], pst)

            def T2a(tt):
                c0 = tt * 128
                p = tt % 2
                for half in range(2):
                    r0 = 64 * half
                    P.op("pe", lambda e, r0=r0, half=half: e.matmul(psS[half], lhsT=khT[p][r0:r0 + 64, :], rhs=vtk[r0:r0 + 64, tt, :],
                                                                    start=True, stop=True),
                         [b_khT[p], b_vtk], [b_psS[half]])
                for half in range(2):
                    r0 = 64 * half
                    cur, prev = half, 1 - half
                    STT([b_S2[prev], b_ebt], [b_S2[cur], b_psS[half]], S2[cur][:, :], S2[prev][:, :], ebt[:, c0 + r0 + 63:c0 + r0 + 64],
                        psS[half], ALU.mult, ALU.add)
                    CP("act", [b_S2[cur]], [b_Sb[cur]], Sb[cur][:, :], S2[cur][:, :])

                def mo(e, r0=0, sprev=1):
                    e.matmul(psO[:, r0:r0 + 64], lhsT=vtk[r0:r0 + 64, tt, :], rhs=attm[p][r0:r0 + 64, r0:r0 + 64], start=True, stop=False)
                    return e.matmul(psO[:, r0:r0 + 64], lhsT=Sb[sprev][:, :], rhs=qT[:, c0 + r0:c0 + r0 + 64], start=False, stop=True)
                return mo

            def T2b(tt):
                c0 = tt * 128
                p = tt % 2

                def mo0(e):
                    e.matmul(psO[:, 0:64], lhsT=vtk[0:64, tt, :], rhs=attm[p][0:64, 0:64], start=True, stop=False)
                    return e.matmul(psO[:, 0:64], lhsT=Sb[1][:, :], rhs=qT[:, c0:c0 + 64], start=False, stop=True)

                def mo1(e):
                    e.matmul(psO[:, 64:128], lhsT=vtk[64:128, tt, :], rhs=attm[p][64:128, 64:128], start=True, stop=False)
                    return e.matmul(psO[:, 64:128], lhsT=Sb[0][:, :], rhs=qT[:, c0 + 64:c0 + 128], start=False, stop=True)
                return mo0, mo1

            for tt in range(16):
                c0 = tt * 128
                p = tt % 2
                cur, prev = tt % 2, 1 - (tt % 2)
                if tt == 0:
                    T1a(0)
                    T1b(0)

                def mo(e, tt=tt, p=p, c0=c0, prev=prev):
                    e.matmul(psO[:, 0:128], lhsT=vtk[:, tt, :], rhs=attm[p][:, :], start=True, stop=False)
                    return e.matmul(psO[:, 0:128], lhsT=Sb[prev][:, :], rhs=qT[:, c0:c0 + 128], start=False, stop=True)
                P.op("pe", mo, [b_vtk, b_attm[p], b_Sb[prev], b_qT], [b_psO])
                P.op("pe", lambda e, tt=tt, p=p: e.matmul(psS[0], lhsT=khT[p][:, :], rhs=vtk[:, tt, :], start=True, stop=True),
                     [b_khT[p], b_vtk], [b_psS[0]])
                STT([b_S2[prev], b_ebt], [b_S2[cur], b_psS[0]], S2[cur][:, :], S2[prev][:, :], ebt[:, c0 + 127:c0 + 128], psS[0],
                    ALU.mult, ALU.add)
                CP("dve", [b_S2[cur]], [b_Sb[cur]], Sb[cur][:, :], S2[cur][:, :])
                yield
                if tt + 1 < 16:
                    T1a(tt + 1)
                evac_o(p, 128)
                yield
                if tt + 1 < 16:
                    T1b(tt + 1)
                if tt >= 1:
                    finish_o(h, H, 1 - p, c0 - 128, 128)
                yield
            finish_o(h, H, 1, 15 * 128, 128)
            P.dma(o_phg[h, :, :], S2[1][:, :], reads=[b_S2[1]], is_output=True)
            p = 0
            P.op("pe", lambda e: e.matmul(psA[0:64, 0:64], lhsT=kT[:, LP:NT], rhs=qT[:, LP:NT], start=True, stop=True),
                 [b_kT, b_qT], [b_psA])
            TT("dve", [b_psA, b_masks], [b_attm[p]], attm[p][0:64, 0:64], psA[0:64, 0:64], masks[:, :], ALU.mult)
            k3 = kT[:, LP:NT].rearrange("p (s t) -> p s t", t=LS)
            e3 = ebt[:, LP:NT].rearrange("p (s t) -> p s t", t=LS)[:, :, LS - 1:LS].to_broadcast([128, NS, LS])
            TT("dve", [b_kT, b_ebt, b_kh[p]], [b_kh[p]], kh[p][:, 0:64].rearrange("p (s t) -> p s t", t=LS), k3, e3, ALU.mult)
            pst = psT[0:64, 0:128]
            P.op("pe", lambda e, pst=pst: e.transpose(out=pst, in_=kh[p][:, 0:64], identity=identb[:, :]), [b_kh[p], b_identb], [bpsT[0]])
            CP("act", [bpsT[0]], [b_khT[p]], khT[p][0:64, :], pst)
            P.op("pe", lambda e: e.matmul(psO[:, 0:64], lhsT=vtk[0:64, 16, :], rhs=attm[p][0:64, 0:64], start=True, stop=False),
                 [b_vtk, b_attm[p]], [b_psO])
            slot = {}

            def ld(sq):
                r = scount[0] % NSB
                scount[0] += 1
                slot[sq] = r
                P.dma(S0f[r][:, :], hg0[sq, h, :, :], writes=[b_S0f[r]])

            def st(sq):
                r = slot[sq]
                P.dma(o_shg[sq, h, :, :], S0f[r][:, :], reads=[b_S0f[r]], is_output=True, eng="pool")
            ld(0)
            ld(1)
            for sq in range(NS):
                if sq + 2 < NS:
                    ld(sq + 2)
                r = slot[sq]
                CP("act", [b_S0f[r]], [b_S0b[r]], S0b[r][:, :], S0f[r][:, :])
                P.op("pe", lambda e, sq=sq, r=r: e.matmul(psO[:, LS * sq:LS * sq + LS], lhsT=S0b[r][:, :], rhs=qT[:, LP + LS * sq:LP + LS * sq + LS],
                                                        start=False, stop=(sq == NS - 1), skip_group_check=True),
                     [b_S0b[r], b_qT], [b_psO])
                m = sq % 2
                ACT([b_khT[p], b_rowm, b_khm[m]], [b_khm[m]], khm[m][:, :], khT[p][0:64, :], AF.Copy, scale=rowm[:, sq:sq + 1])
                P.op("pe", lambda e, m=m: e.matmul(psS[m], lhsT=khm[m][:, :], rhs=vtk[0:64, 16, :], start=True, stop=True),
                     [b_khm[m], b_vtk], [b_psS[m]])
                STT([b_S0f[r], b_ebt], [b_S0f[r], b_psS[m]], S0f[r][:, :], S0f[r][:, :], ebt[:, LP + LS * sq + LS - 1:LP + LS * sq + LS], psS[m],
                    ALU.mult, ALU.add)
                if sq >= 1:
                    st(sq - 1)
                if sq % 2 == 1:
                    yield
            st(NS - 1)
            evac_o(p, NSAMP)
            finish_o(h, H, p, LP, NSAMP)
            yield

        for pair in ((0, 1), (2, 3)):
            for h in pair:
                for _ in stage_a(h):
                    pass
            gens = [stage_b(h, ctxs[i]) for i, h in enumerate(pair)]
            while gens:
                for g in list(gens):
                    try:
                        next(g)
                    except StopIteration:
                        gens.remove(g)

    def outproj():
        banks = [[(psI[0][:, :], bpsI[0]), (psI[1][:, :], bpsI[1])],
                 [(psBU[0][:, 0:512], bpsBU[0]), (psBU[0][:, 512:1024], bpsBU[0])]]

        def mk(tt):
            rows = 128 if tt < 16 else NSAMP
            c0 = tt * 128
            src = xp[c0:c0 + 128, :] if tt < 16 else xs[:, :]
            dst = yp[c0:c0 + 128, :] if tt < 16 else ys[:, :]
            s_ = tt % len(xin)
            bk = banks[tt % 2]

            def L():
                P.dma(xin[s_][0:rows, :], src, writes=[b_xin[s_]])

            def M():
                for dn in range(2):
                    def mo(e, dn=dn):
                        ins = None
                        for kf in range(8):
                            lh = uT[:, kf, c0:c0 + rows] if kf < 4 else mixH[:, kf - 4, c0:c0 + rows]
                            ins = e.matmul(bk[dn][0][0:rows, :], lhsT=lh, rhs=wo[:, kf, dn * 512:(dn + 1) * 512], start=(kf == 0), stop=(kf == 7))
                        return ins
                    P.op("pe", mo, b_uT + b_mixH + [b_wo], [bk[dn][1]])

            def R():
                for dn in range(2):
                    TT("dve", [bk[dn][1], b_xin[s_], b_res[s_]], [b_res[s_]], res[s_][0:rows, dn * 512:(dn + 1) * 512], bk[dn][0][0:rows, :],
                       xin[s_][0:rows, dn * 512:(dn + 1) * 512], ALU.add)

            def N():
                ACT([b_res[s_]], [b_junk, b_stat[s_]], junk[0:rows, :], res[s_][0:rows, :], AF.Square, accum_out=stat[s_][0:rows, 0:1])
                ACT([b_stat[s_]], [b_stat[s_]], stat[s_][0:rows, 1:2], stat[s_][0:rows, 0:1], AF.Ln, scale=1.0 / D, bias=EPS)
                ACT([b_stat[s_]], [b_stat[s_]], stat[s_][0:rows, 2:3], stat[s_][0:rows, 1:2], AF.Exp, scale=-0.5)

            def F():
                STT([b_res[s_], b_stat[s_], b_fgb], [b_res[s_]], res[s_][0:rows, :], res[s_][0:rows, :], stat[s_][0:rows, 2:3], fgb[0:rows, :],
                    ALU.mult, ALU.mult)
                P.dma(dst, res[s_][0:rows, :], reads=[b_res[s_]], is_output=True, eng="act")
            return (L, M, R, N, F)
        T = [mk(tt) for tt in range(17)]
        T[0][0]()
        T[1][0]()
        T[0][1]()
        for t in range(17):
            if t + 2 < 17:
                T[t + 2][0]()
            if t + 1 < 17:
                T[t + 1][1]()
            T[t][2]()
            T[t][3]()
            if t >= 1:
                T[t - 1][4]()
        T[16][4]()

    m_tmp = A.mark()
    print("arena marks: s5p", m_s5p, "tmp", m_tmp)
    def u_inproj():
        for jt in range(4):
            wt, bw = load_w_tile(jt * 128, (jt + 1) * 128 if jt < 3 else None)
            for (c0, n) in col_chunks():
                ps, bps = inproj_chunk(wt, bw, c0, n)
                if c0 < LP:
                    CP("act", [bps], [b_uT[jt]], uT[:, jt, 0:LP].rearrange("p (s c) -> p s c", s=8)[:, :, c0 // 8:c0 // 8 + n // 8],
                       ps.rearrange("p (c s) -> p s c", s=8))
                else:
                    CP("act", [bps], [b_uT[jt]], uT[:, jt, LP:NT].rearrange("p (t s) -> p t s", t=LS),
                       ps.rearrange("p (s t) -> p t s", t=LS))
                yield

    def chain(*gens):
        for g in gens:
            for _ in g:
                yield

    ga_ = chain(phase_a(), u_inproj())
    gp_ = s5_prep()
    for _ in range(4):
        next(ga_)
    alive = [ga_, ga_, gp_]
    while alive:
        for g in list(alive):
            if g not in alive:
                continue
            try:
                next(g)
            except StopIteration:
                while g in alive:
                    alive.remove(g)
    load_wg()
    P.barrier()
    A.reset(m_tmp)
    ea = sb("ea", [128, 512])
    eb2 = sb("eb2", [128, 512])
    glu_tmp2 = []
    s5_loop()
    glu()
    P.barrier()
    A.reset(m_s5p)
    mixH = sb("mixH", [128, 4, NT], BF16)
    wo = sb("wo", [128, 8, D], BF16)
    load_wo()
    m_hg = A.mark()
    hgrn()
    P.barrier()
    A.reset(m_hg)
    NOB = 4
    for _ in range(NOB - 2):
        xin.append(sb("xinx", [128, D])); b_xin.append(Buf("xinx"))
        res.append(sb("resx", [128, D])); b_res.append(Buf("resx"))
        stat.append(sb("statx", [128, 4])); b_stat.append(Buf("statx"))
    outproj()
    print("SBUF peak bytes/partition:", A.peak)
    P.emit(nc)
    return nc


_NC_CACHE = {}


def _consts():
    ident = np.eye(128, dtype=np.float32)
    s = np.arange(128)
    mask2 = (s[:, None] <= s[None, :]).astype(np.float32)
    t = np.arange(64)
    masks = ((t[:, None] // LS == t[None, :] // LS) & (t[:, None] <= t[None, :])).astype(np.float32)
    seg = np.ones((1, 512 + NSAMP), np.float32)
    seg[0, 0:512:128] = 0.0
    seg[0, 512::LS] = 0.0
    rowm = (t[:, None] // LS == np.arange(NS)[None, :]).astype(np.float32)
    q = np.arange(128)
    glm = (q[:, None] // 64 == np.arange(2)[None, :]).astype(np.float32)
    rm4 = (q[:, None] // 32 == np.arange(4)[None, :]).astype(np.float32)
    return dict(c_ident=ident, c_mask2=mask2, c_masks=masks, c_seg=seg, c_rowm=rowm, c_glm=glm, c_rm4=rm4)


def kernel(x_prompt, x_sample, state_s5_re, state_s5_im, state_hgrn, norm_g, w_in, s5_lambda_re, s5_lambda_im,
           s5_log_step, s5_b_re, s5_b_im, s5_c_re, s5_c_im, s5_d, w_glu, b_glu, hgrn_lb_logits, hgrn_onorm_g,
           w_out, final_norm_g):
    f = lambda a: np.ascontiguousarray(np.asarray(a, dtype=np.float32))
    x_prompt, x_sample = f(x_prompt), f(x_sample)
    state_s5_re, state_s5_im, state_hgrn = f(state_s5_re), f(state_s5_im), f(state_hgrn)
    shared = dict(
        norm_g=f(norm_g)[0], w_in=f(w_in)[0], lam_re=f(s5_lambda_re)[0], lam_im=f(s5_lambda_im)[0],
        log_step=f(s5_log_step)[0], b_re=f(s5_b_re)[0], b_im=f(s5_b_im)[0], c_re=f(s5_c_re)[0], c_im=f(s5_c_im)[0],
        s5_d=f(s5_d)[0].reshape(512), w_glu=f(w_glu)[0], b_glu=f(b_glu)[0], lb_logits=f(hgrn_lb_logits),
        onorm_g=f(hgrn_onorm_g)[0], w_out=f(w_out)[0], fin_g=f(final_norm_g).reshape(1, D))
    shared.update(_consts())
    in_maps = []
    for c in range(NCORES):
        m = dict(shared)
        m["xp"] = x_prompt[c]
        m["xs"] = np.ascontiguousarray(x_sample[NS * c:NS * (c + 1)].reshape(NSAMP, D))
        m["s5re0"] = np.ascontiguousarray(state_s5_re[0, NS * c:NS * (c + 1)].reshape(NS, 2048))
        m["s5im0"] = np.ascontiguousarray(state_s5_im[0, NS * c:NS * (c + 1)].reshape(NS, 2048))
        m["hg0"] = np.ascontiguousarray(state_hgrn[0, NS * c:NS * (c + 1)])
        in_maps.append(m)
    if "nc" not in _NC_CACHE:
        _NC_CACHE["nc"] = build_program()
    nc = _NC_CACHE["nc"]
    res = run_bass_kernel_spmd(nc, in_maps, core_ids=list(range(NCORES)))
    R = res.results
    y_prompt = np.stack([R[c]["yp"] for c in range(NCORES)]).astype(np.float32)
    y_sample = np.concatenate([R[c]["ys"].reshape(NS, LS, D) for c in range(NCORES)]).astype(np.float32)
    p_re = np.stack([R[c]["o_pre"].reshape(32, 64) for c in range(NCORES)])[None].astype(np.float32)
    p_im = np.stack([R[c]["o_pim"].reshape(32, 64) for c in range(NCORES)])[None].astype(np.float32)
    p_hg = np.stack([R[c]["o_phg"] for c in range(NCORES)])[None].astype(np.float32)
    s_re = np.concatenate([R[c]["o_sre"].reshape(NS, 32, 64) for c in range(NCORES)])[None].astype(np.float32)
    s_im = np.concatenate([R[c]["o_sim"].reshape(NS, 32, 64) for c in range(NCORES)])[None].astype(np.float32)
    s_hg = np.concatenate([R[c]["o_shg"] for c in range(NCORES)])[None].astype(np.float32)
    return (y_prompt, y_sample, p_re, p_im, p_hg, s_re, s_im, s_hg)
```

```python
import math
import numpy as np
import concourse.bass as bass
import concourse.mybir as mybir
from concourse.bass_utils import run_bass_kernel_spmd

F32 = mybir.dt.float32
BF16 = mybir.dt.bfloat16
I32 = mybir.dt.int32
ALU = mybir.AluOpType
AF = mybir.ActivationFunctionType

NCORES = 8
D = 1024
LP = 2048
NS = 16
LS = 4
NSAMP = NS * LS
NT = LP + NSAMP
TC = 128
EPS = 1e-6
ENGS = ("pe", "act", "dve", "pool", "sp")
N_DMA_SEMS = 20
N_SW_SEMS = 8
SB_BASE = 16512
SB_LIMIT = 229376


class Buf:
    __slots__ = ("name", "writer", "readers", "excl")

    def __init__(self, name, excl=False):
        self.name = name
        self.writer = None
        self.readers = []
        self.excl = excl


class Op:
    __slots__ = ("eng", "fn", "deps", "signals", "count", "is_dma", "dma_sem", "dma_val")

    def __init__(self, eng, fn, is_dma=False):
        self.eng = eng
        self.fn = fn
        self.deps = []
        self.signals = False
        self.count = None
        self.is_dma = is_dma
        self.dma_sem = None
        self.dma_val = None


class Prog:
    def __init__(self):
        self.ops = {e: [] for e in ENGS}
        self.n_dma = 0
        self.n_swdma = 0
        self.dma_last = [None] * (N_DMA_SEMS + N_SW_SEMS)
        self.dma_cnt = [0] * (N_DMA_SEMS + N_SW_SEMS)
        self.out_dmas = []
        self.pending_barrier = {}
        self.last_op = {e: None for e in ENGS}
        self.dmas_since_barrier = []

    def _add(self, op, reads, writes):
        ex = [b for b in reads if b.excl]
        if ex:
            reads = [b for b in reads if not b.excl]
            writes = list(writes) + [b for b in ex if b not in writes]
        deps = []
        for b in reads:
            if b.writer is not None:
                deps.append(b.writer)
        for b in writes:
            if b.writer is not None:
                deps.append(b.writer)
            deps.extend(b.readers)
        if op.eng in self.pending_barrier:
            deps.extend(self.pending_barrier.pop(op.eng))
        seen = set()
        for d in deps:
            if d is op or id(d) in seen:
                continue
            seen.add(id(d))
            op.deps.append(d)
            d.signals = True
        for b in reads:
            b.readers.append(op)
        for b in writes:
            b.writer = op
            b.readers = []
        self.ops[op.eng].append(op)
        if op.is_dma:
            self.dmas_since_barrier.append(op)
        else:
            self.last_op[op.eng] = op
        return op

    def op(self, eng, fn, reads=(), writes=()):
        return self._add(Op(eng, fn), reads, writes)

    def dma(self, out, in_, reads=(), writes=(), eng="sp", is_output=False, **kw):
        def fn(e, out=out, in_=in_, kw=kw):
            return e.dma_start(out=out, in_=in_, **kw)
        op = Op(eng, fn, is_dma=True)
        if eng == "pool":
            k = N_DMA_SEMS + (self.n_swdma % N_SW_SEMS)
            self.n_swdma += 1
        else:
            k = self.n_dma % N_DMA_SEMS
            self.n_dma += 1
        prev = self.dma_last[k]
        self.dma_cnt[k] += 1
        op.dma_sem = k
        op.dma_val = 16 * self.dma_cnt[k]
        self._add(op, reads, writes)
        if prev is not None and prev not in op.deps:
            op.deps.append(prev)
            prev.signals = True
        self.dma_last[k] = op
        if is_output:
            self.out_dmas.append(op)
            op.signals = True
        return op

    def barrier(self):
        pre = [o for o in self.last_op.values() if o is not None] + list(self.dmas_since_barrier)
        self.dmas_since_barrier = []
        for e in ENGS:
            self.pending_barrier[e] = list(self.pending_barrier.get(e, [])) + pre

    def emit(self, nc):
        import contextlib
        for e in ENGS:
            c = 0
            for op in self.ops[e]:
                if not op.is_dma and op.signals:
                    c += 1
                    op.count = c
        with contextlib.ExitStack() as st:
            esem = {e: st.enter_context(nc.semaphore("s_" + e)) for e in ENGS}
            dsem = [st.enter_context(nc.semaphore("d_%d" % i)) for i in range(N_DMA_SEMS + N_SW_SEMS)]
            block = st.enter_context(nc.Block())

            def run(e, engobj):
                waited = {}

                def wait_for(d):
                    if d.is_dma:
                        key, sem, val = ("d", d.dma_sem), dsem[d.dma_sem], d.dma_val
                    else:
                        key, sem, val = ("e", d.eng), esem[d.eng], d.count
                    if waited.get(key, 0) >= val:
                        return
                    waited[key] = val
                    engobj.wait_ge(sem, val)

                for op in self.ops[e]:
                    for d in op.deps:
                        wait_for(d)
                    ins = op.fn(engobj)
                    if op.is_dma:
                        ins.then_inc(dsem[op.dma_sem], 16)
                    elif op.signals:
                        ins.then_inc(esem[e], 1)
                if e == "sp":
                    for d in self.out_dmas:
                        wait_for(d)

            @block.tensor
            def _(eng):
                run("pe", eng)

            @block.scalar
            def _(eng):
                run("act", eng)

            @block.vector
            def _(eng):
                run("dve", eng)

            @block.gpsimd
            def _(eng):
                run("pool", eng)

            @block.sync
            def _(eng):
                run("sp", eng)


class Arena:
    def __init__(self, nc):
        self.nc = nc
        self.off = SB_BASE
        self.n = 0
        self.peak = SB_BASE

    def alloc(self, name, shape, dt):
        esz = 2 if dt == BF16 else 4
        size = esz
        for s in shape[1:]:
            size *= s
        size = (size + 31) // 32 * 32
        self.n += 1
        h = self.nc.alloc_sbuf_tensor_at("%s_%d" % (name, self.n), list(shape), dt, offset=self.off)
        self.off += size
        self.peak = max(self.peak, self.off)
        assert self.off <= SB_LIMIT, "SBUF overflow at %s: %d" % (name, self.off)
        return h

    def mark(self):
        return self.off

    def reset(self, m):
        self.off = m


def col_chunks():
    return [(i * 512, 512) for i in range(4)] + [(LP, NSAMP)]


def build_program():
    nc = bass.Bass("TRN2", target_bir_lowering=False)

    def din(name, shape, dt=F32):
        return nc.dram_tensor(name, list(shape), dt, kind="ExternalInput").ap()

    def dout(name, shape):
        return nc.dram_tensor(name, list(shape), F32, kind="ExternalOutput").ap()

    xp = din("xp", [LP, D])
    xs = din("xs", [NSAMP, D])
    s5re0 = din("s5re0", [NS, 2048])
    s5im0 = din("s5im0", [NS, 2048])
    hg0 = din("hg0", [NS, 4, 128, 128])
    norm_g = din("norm_g", [D])
    w_in = din("w_in", [D, 3072])
    lam_re = din("lam_re", [32, 64])
    lam_im = din("lam_im", [32, 64])
    log_step = din("log_step", [32])
    b_re = din("b_re", [32, 64, 16])
    b_im = din("b_im", [32, 64, 16])
    c_re = din("c_re", [32, 16, 64])
    c_im = din("c_im", [32, 16, 64])
    s5_d = din("s5_d", [512])
    w_glu = din("w_glu", [512, 512])
    b_glu = din("b_glu", [512])
    lb_logits = din("lb_logits", [2, 512])
    onorm_g = din("onorm_g", [512])
    w_out = din("w_out", [D, D])
    fin_g = din("fin_g", [1, D])
    c_ident = din("c_ident", [128, 128])
    c_mask2 = din("c_mask2", [128, 128])
    c_masks = din("c_masks", [64, 64])
    c_seg = din("c_seg", [1, 512 + NSAMP])
    c_rowm = din("c_rowm", [64, NS])
    c_glm = din("c_glm", [128, 2])
    c_rm4 = din("c_rm4", [128, 4])

    yp = dout("yp", [LP, D])
    ys = dout("ys", [NSAMP, D])
    o_pre = dout("o_pre", [16, 128])
    o_pim = dout("o_pim", [16, 128])
    o_phg = dout("o_phg", [4, 128, 128])
    o_sre = dout("o_sre", [NS, 2048])
    o_sim = dout("o_sim", [NS, 2048])
    o_shg = dout("o_shg", [NS, 4, 128, 128])

    P = Prog()
    A = Arena(nc)

    def sb(name, shape, dt=F32):
        return A.alloc(name, shape, dt)

    def ACT(reads, writes, out, in_, func, scale=1.0, bias=0.0, accum_out=None):
        def fn(e):
            if accum_out is not None:
                return e.activation(out=out, in_=in_, func=func, scale=scale, bias=bias, accum_out=accum_out)
            return e.activation(out=out, in_=in_, func=func, scale=scale, bias=bias)
        return P.op("act", fn, reads, writes)

    def TT(eng, reads, writes, out, in0, in1, op):
        return P.op(eng, lambda e: e.tensor_tensor(out=out, in0=in0, in1=in1, op=op), reads, writes)

    def TS(eng, reads, writes, out, in0, s1, s2, op0, op1=None):
        if op1 is None:
            return P.op(eng, lambda e: e.tensor_scalar(out=out, in0=in0, scalar1=s1, scalar2=None, op0=op0), reads, writes)
        return P.op(eng, lambda e: e.tensor_scalar(out=out, in0=in0, scalar1=s1, scalar2=s2, op0=op0, op1=op1), reads, writes)

    def STT(reads, writes, out, in0, scalar, in1, op0, op1):
        return P.op("dve", lambda e: e.scalar_tensor_tensor(out=out, in0=in0, scalar=scalar, in1=in1, op0=op0, op1=op1), reads, writes)

    def CP(eng, reads, writes, out, in_):
        if eng == "act":
            return ACT(reads, writes, out, in_, AF.Copy)
        return P.op(eng, lambda e: e.tensor_copy(out=out, in_=in_), reads, writes)

    def MEMSET(eng, writes, ap, val):
        return P.op(eng, lambda e: e.memset(ap, val), (), writes)

    def RECIP(reads, writes, out, in_):
        return P.op("dve", lambda e: e.reciprocal(out=out, in_=in_), reads, writes)

    psT = nc.alloc_psum_tensor("psT", [128, 1024], BF16)
    psI = [nc.alloc_psum_tensor("psI%d" % i, [128, 512], F32) for i in range(2)]
    psE = nc.alloc_psum_tensor("psE", [128, 2048], F32)
    psBU = [psE[:, 0:1024], psE[:, 1024:2048]]
    psX = nc.alloc_psum_tensor("psX", [128, 512], F32)
    bpsT = [Buf("psT", True)]
    bpsI = [Buf("psI%d" % i, True) for i in range(2)]
    bpsBU = [Buf("psBU%d" % i, True) for i in range(2)]
    b_psX = Buf("psX", True)

    ident = sb("ident", [128, 128]); b_ident = Buf("ident")
    identb = sb("identb", [128, 128], BF16); b_identb = Buf("identb")
    onesb = sb("onesb", [128, 128], BF16); b_onesb = Buf("onesb")
    xT = sb("xT", [128, 8, NT], BF16); b_xT = [Buf("xT%d" % i) for i in range(17)]
    uT = sb("uT", [128, 4, NT], BF16); b_uT = [Buf("uT%d" % i) for i in range(4)]
    b_gT = Buf("gT")
    b_mixH = [Buf("mixH%d" % i) for i in range(4)]
    b_wo = Buf("wo")
    wg = sb("wg", [128, 4, 512], BF16); b_wg = Buf("wg")
    wst = [sb("wst%d" % i, [128, 8, 128]) for i in range(2)]; b_wst = [Buf("wst%d" % i) for i in range(2)]
    wbf = [sb("wbf%d" % i, [128, 8, 128], BF16) for i in range(2)]; b_wbf = [Buf("wbf%d" % i) for i in range(2)]
    gcol = sb("gcol", [128, 8]); b_gcol = Buf("gcol")
    fgb = sb("fgb", [128, D]); b_fgb = Buf("fgb")
    segm = sb("segm", [128, 512 + NSAMP]); b_segm = Buf("segm")
    mask2 = sb("mask2", [128, 128]); b_mask2 = Buf("mask2")
    masks = sb("masks", [64, 64]); b_masks = Buf("masks")
    rowm = sb("rowm", [64, NS]); b_rowm = Buf("rowm")
    glm = sb("glm", [128, 2]); b_glm = Buf("glm")
    rm4 = sb("rm4", [128, 4]); b_rm4 = Buf("rm4")
    dcol = sb("dcol", [128, 4]); b_dcol = Buf("dcol")
    nbg = sb("nbg", [128, 4]); b_nbg = Buf("nbg")
    pbg = sb("pbg", [128, 4])
    lbc = sb("lbc", [128, 2, 4]); b_lbc = Buf("lbc")
    nom = sb("nom", [128, 4]); b_nom = Buf("nom")
    ogc = sb("ogc", [128, 4]); b_ogc = Buf("ogc")
    xin = [sb("xin%d" % i, [128, D]) for i in range(2)]; b_xin = [Buf("xin%d" % i) for i in range(2)]
    xnb = [sb("xnb%d" % i, [128, D], BF16) for i in range(2)]; b_xnb = [Buf("xnb%d" % i) for i in range(2)]
    junk = sb("junk", [128, D], BF16); b_junk = Buf("junk")
    stat = [sb("stat%d" % i, [128, 4]) for i in range(2)]; b_stat = [Buf("stat%d" % i) for i in range(2)]

    P.dma(ident[:], c_ident[:, :], writes=[b_ident])
    CP("dve", [b_ident], [b_identb], identb[:], ident[:])
    MEMSET("pool", [b_onesb], onesb[:], 1.0)
    P.dma(gcol[:], norm_g.rearrange("(k p) -> p k", p=128), writes=[b_gcol], allow_slow_non_contiguous=True)
    P.dma(fgb[:], fin_g[0:1, :].partition_broadcast(128), writes=[b_fgb])
    P.dma(segm[:], c_seg[0:1, :].partition_broadcast(128), writes=[b_segm])
    P.dma(mask2[:], c_mask2[:, :], writes=[b_mask2])
    P.dma(masks[:], c_masks[:, :], writes=[b_masks])
    P.dma(rowm[:], c_rowm[:, :], writes=[b_rowm])
    P.dma(glm[:], c_glm[:, :], writes=[b_glm])
    P.dma(rm4[:], c_rm4[:, :], writes=[b_rm4])
    P.dma(dcol[:], s5_d.rearrange("(t p) -> p t", p=128), writes=[b_dcol], allow_slow_non_contiguous=True)
    P.dma(nbg[:], b_glu.rearrange("(t p) -> p t", p=128), writes=[b_nbg], allow_slow_non_contiguous=True)
    CP("dve", [b_nbg], [b_nbg], pbg[:], nbg[:])
    TS("dve", [b_nbg], [b_nbg], nbg[:], nbg[:], -1.0, None, ALU.mult)
    P.dma(ogc[:], onorm_g.rearrange("(t p) -> p t", p=128), writes=[b_ogc], allow_slow_non_contiguous=True)
    P.dma(lbc[:], lb_logits.rearrange("r (t p) -> p r t", p=128), writes=[b_lbc], allow_slow_non_contiguous=True)
    TT("dve", [b_lbc], [b_nom], nom[:], lbc[:, 1, :], lbc[:, 0, :], ALU.subtract)
    ACT([b_nom], [b_nom], nom[:], nom[:], AF.Exp)
    TS("dve", [b_nom], [b_nom], nom[:], nom[:], 1.0, None, ALU.add)
    RECIP([b_nom], [b_lbc], lbc[:, 0, :], nom[:])
    TS("dve", [b_lbc], [b_lbc], lbc[:, 1, :], lbc[:, 0, :], -1.0, 1.0, ALU.mult, ALU.add)
    TS("dve", [b_lbc], [b_nom], nom[:], lbc[:, 1, :], -1.0, None, ALU.mult)

    b_ea = Buf("ea")
    b_eb2 = Buf("eb2")
    res = [sb("res%d" % i, [128, D]) for i in range(2)]; b_res = [Buf("res%d" % i) for i in range(2)]
    m_s5p = A.mark()

    def phase_a():
        for tt in range(17):
            rows = 128 if tt < 16 else NSAMP
            src = xp[tt * 128:(tt + 1) * 128, :] if tt < 16 else xs[:, :]
            s = tt % 2
            P.dma(xin[s][0:rows, :], src, writes=[b_xin[s]])
            ACT([b_xin[s]], [b_junk, b_stat[s]], junk[0:rows, :], xin[s][0:rows, :], AF.Square, accum_out=stat[s][0:rows, 0:1])
            ACT([b_stat[s]], [b_stat[s]], stat[s][0:rows, 1:2], stat[s][0:rows, 0:1], AF.Ln, scale=1.0 / D, bias=EPS)
            ACT([b_stat[s]], [b_stat[s]], stat[s][0:rows, 2:3], stat[s][0:rows, 1:2], AF.Exp, scale=-0.5)
            ACT([b_xin[s], b_stat[s]], [b_xnb[s]], xnb[s][0:rows, :], xin[s][0:rows, :], AF.Copy, scale=stat[s][0:rows, 2:3])

            def tr(e, s=s, rows=rows):
                ins = None
                for kd in range(8):
                    ins = e.transpose(out=psT[:, kd * 128:kd * 128 + rows], in_=xnb[s][0:rows, kd * 128:(kd + 1) * 128],
                                      identity=identb[0:rows, 0:rows])
                return ins
            P.op("pe", tr, [b_xnb[s], b_identb], bpsT)
            c0 = tt * 128
            TT("dve", bpsT + [b_gcol], [b_xT[tt]], xT[:, :, c0:c0 + rows],
               psT[:, :].rearrange("p (k c) -> p k c", k=8)[:, :, 0:rows],
               gcol[:, :].unsqueeze(2).to_broadcast([128, 8, rows]), ALU.mult)
            yield

    wcount = [0]
    wcache = {}

    def _load_w(col0):
        s = wcount[0] % 2
        wcount[0] += 1
        P.dma(wst[s][:], w_in[:, col0:col0 + 128].rearrange("(k p) c -> p k c", p=128), writes=[b_wst[s]])
        CP("act", [b_wst[s]], [b_wbf[s]], wbf[s][:], wst[s][:])
        return wbf[s], b_wbf[s]

    def load_w_tile(col0, nxt=None):
        if col0 in wcache:
            r = wcache.pop(col0)
        else:
            r = _load_w(col0)
        if nxt is not None and nxt not in wcache:
            wcache[nxt] = _load_w(nxt)
        return r

    icount = [0]
    ibanks2 = [(psI[0], bpsI[0]), (psI[1], bpsI[1])]
    ibanks4 = ibanks2 + [(psX, b_psX), (psT[:, :].bitcast(F32), bpsT[0])]
    ibank_sel = [ibanks2]

    def inproj_chunk(wt, bw, c0, n):
        banks = ibank_sel[0]
        s = icount[0] % len(banks)
        icount[0] += 1
        pst_, bst_ = banks[s]
        tts = sorted(set([c0 // 128 + i for i in range((n + 127) // 128)]))

        def mm(e):
            ins = None
            for kd in range(8):
                ins = e.matmul(pst_[:, 0:n], lhsT=wt[:, kd, :], rhs=xT[:, kd, c0:c0 + n], start=(kd == 0), stop=(kd == 7))
            return ins
        P.op("pe", mm, [bw] + [b_xT[t] for t in tts], [bst_])
        return pst_[:, 0:n], bst_

    gT = sb("gT", [128, 4, NT], BF16)
    prm = sb("prm", [128, 7, 16]); b_prm = Buf("prm")
    prm8 = sb("prm8", [128, 2, 3, 16]); b_prm8 = Buf("prm8")
    Apw = sb("Apw", [128, 3, 9, 16]); b_Apw = Buf("Apw")
    PK = sb("PK", [128, 4, 2, 8, 128], BF16); b_PK = Buf("PK")
    CWt = sb("CWt", [128, 16, 8, 2, 32], BF16); b_CWt = Buf("CWt")
    KT = sb("KT", [128, 4, 8, 128], BF16); b_KT = Buf("KT")
    TCB = 64
    Ut = sb("Ut", [128, 2, 16, TCB]); b_Ut = Buf("Ut")
    pw = sb("pw", [128, 2, 2, 16]); b_pw = Buf("pw")
    h0s = sb("h0s", [128, 2, 16, NS]); b_h0s = Buf("h0s")
    carry = sb("carry", [128, 2, 16]); b_carry = [Buf("carry%d" % i) for i in range(4)]
    hS = sb("hS", [128, 2, 16, NS]); b_hS = Buf("hS")
    m_loop = A.mark()

    def s5_prep():
        rl = sb("rl", [16, 16, 128]); b_rl = Buf("rl")
        lsr = sb("lsr", [16, 2]); b_lsr = Buf("lsr")
        rli = sb("rli", [16, 128], I32); b_rli = Buf("rli")
        R = lambda k: rl[:, k, :]
        R3 = lambda k: rl[:, k, :].rearrange("j (gl p) -> j gl p", gl=2)
        P.dma(R(0), lam_re.rearrange("(j gl) p -> j (gl p)", gl=2), writes=[b_rl])
        P.dma(R(1), lam_im.rearrange("(j gl) p -> j (gl p)", gl=2), writes=[b_rl])
        P.dma(lsr[:], log_step.rearrange("(j gl) -> j gl", gl=2), writes=[b_lsr])
        ACT([b_lsr], [b_lsr], lsr[:], lsr[:], AF.Exp)
        dtb = lsr[:, :].unsqueeze(2).to_broadcast([16, 2, 64])
        rr, rw = [b_rl], [b_rl]
        TS("dve", rr, rw, R(0), R(0), -1e-4, None, ALU.min)
        TT("dve", rr + [b_lsr], rw, R3(2), R3(0), dtb, ALU.mult)
        TT("dve", rr + [b_lsr], rw, R3(3), R3(1), dtb, ALU.mult)
        ACT(rr, rw, R(4), R(2), AF.Exp)
        TS("dve", rr, rw, R(3), R(3), 1.0 / (2.0 * math.pi), None, ALU.mult)

        def wrap(dst, src, add):
            TS("dve", rr, rw, dst, src, add, None, ALU.add)
            CP("dve", rr, [b_rli], rli[:], dst)
            CP("dve", [b_rli], rw, R(11), rli[:])
            TT("dve", rr, rw, dst, dst, R(11), ALU.subtract)
            TS("dve", rr, rw, R(11), dst, 0.5, None, ALU.is_gt)
            TT("dve", rr, rw, dst, dst, R(11), ALU.subtract)
            TS("dve", rr, rw, R(11), dst, -0.5, None, ALU.is_lt)
            TT("dve", rr, rw, dst, dst, R(11), ALU.add)
        wrap(R(5), R(3), 0.0)
        wrap(R(6), R(3), 0.25)
        ACT(rr, rw, R(5), R(5), AF.Sin, scale=6.28318)
        ACT(rr, rw, R(6), R(6), AF.Sin, scale=6.28318)
        TT("dve", rr, rw, R(7), R(4), R(6), ALU.mult)
        TT("dve", rr, rw, R(8), R(4), R(5), ALU.mult)
        TT("dve", rr, rw, R(12), R(0), R(0), ALU.mult)
        TT("dve", rr, rw, R(13), R(1), R(1), ALU.mult)
        TT("dve", rr, rw, R(12), R(12), R(13), ALU.add)
        RECIP(rr, rw, R(12), R(12))
        TS("dve", rr, rw, R(13), R(7), -1.0, None, ALU.add)
        TT("dve", rr, rw, R(14), R(13), R(0), ALU.mult)
        TT("dve", rr, rw, R(15), R(8), R(1), ALU.mult)
        TT("dve", rr, rw, R(14), R(14), R(15), ALU.add)
        TT("dve", rr, rw, R(9), R(14), R(12), ALU.mult)
        TT("dve", rr, rw, R(14), R(8), R(0), ALU.mult)
        TT("dve", rr, rw, R(15), R(13), R(1), ALU.mult)
        TT("dve", rr, rw, R(14), R(14), R(15), ALU.subtract)
        TT("dve", rr, rw, R(10), R(14), R(12), ALU.mult)

        order = [7, 8, 9, 10, 6, 5, 4]

        def trp(e):
            ins = None
            for k, slot in enumerate(order):
                ins = e.transpose(out=psI[0][:, k * 16:(k + 1) * 16], in_=R(slot), identity=ident[0:16, 0:16])
            return ins
        P.op("pe", trp, rr + [b_ident], [bpsI[0]])
        CP("dve", [bpsI[0]], [b_prm], prm[:, :, :], psI[0][:, 0:112].rearrange("p (k j) -> p k j", k=7))

        Bs = sb("Bs", [128, 2, 16, 16]); b_Bs = Buf("Bs")
        bb = sb("bb", [128, 2, 16, 16]); b_bb = Buf("bb")
        T12 = sb("T12", [128, 2, 16, 32]); b_T12 = Buf("T12")
        b_Bs1 = Buf("Bs1")
        P.dma(Bs[:, 0, :, :], b_re.rearrange("(j gl) p c -> (gl p) j c", gl=2), writes=[b_Bs])
        P.dma(Bs[:, 1, :, :], b_im.rearrange("(j gl) p c -> (gl p) j c", gl=2), writes=[b_Bs1])
        zrb = prm[:, 2, :].unsqueeze(2).to_broadcast([128, 16, 16])
        zib = prm[:, 3, :].unsqueeze(2).to_broadcast([128, 16, 16])
        t1 = T12[:, 0, :, 0:16]
        t2 = T12[:, 1, :, 0:16]
        TT("dve", [b_Bs, b_Bs1, b_prm], [b_T12], t1, Bs[:, 0, :, :], zrb, ALU.mult)
        TT("dve", [b_Bs, b_prm, b_T12], [b_T12], t2, Bs[:, 1, :, :], zib, ALU.mult)
        TT("dve", [b_T12], [b_bb], bb[:, 0, :, :], t1, t2, ALU.subtract)
        TT("dve", [b_Bs, b_prm, b_bb], [b_T12], t1, Bs[:, 1, :, :], zrb, ALU.mult)
        TT("dve", [b_Bs, b_prm, b_T12], [b_T12], t2, Bs[:, 0, :, :], zib, ALU.mult)
        TT("dve", [b_T12, b_bb], [b_bb], bb[:, 1, :, :], t1, t2, ALU.add)

        yield
        io = [b_Apw, b_prm, b_T12]
        q1 = T12[:, 0, :, 16]
        q2 = T12[:, 1, :, 16]
        MEMSET("dve", [b_Apw], Apw[:, 0, 0, :], 1.0)
        MEMSET("dve", [b_Apw], Apw[:, 1:3, 0, :], 0.0)
        CP("dve", io, [b_Apw], Apw[:, 0:2, 1, :], prm[:, 0:2, :])
        TS("dve", io, [b_Apw], Apw[:, 2, 1, :], prm[:, 1, :], -1.0, None, ALU.mult)
        for k in range(2, 9):
            TT("dve", io, [b_T12], q1, Apw[:, 0, k - 1, :], prm[:, 0, :], ALU.mult)
            TT("dve", io, [b_T12], q2, Apw[:, 1, k - 1, :], prm[:, 1, :], ALU.mult)
            TT("dve", io, [b_Apw], Apw[:, 0, k, :], q1, q2, ALU.subtract)
            TT("dve", io, [b_T12], q1, Apw[:, 0, k - 1, :], prm[:, 1, :], ALU.mult)
            TT("dve", io, [b_T12], q2, Apw[:, 1, k - 1, :], prm[:, 0, :], ALU.mult)
            TT("dve", io, [b_Apw], Apw[:, 1, k, :], q1, q2, ALU.add)
            TS("dve", io, [b_Apw], Apw[:, 2, k, :], Apw[:, 1, k, :], -1.0, None, ALU.mult)
        yield
        io8 = [b_prm8, b_prm, b_T12]
        CP("dve", io8, [b_prm8], prm8[:, 1, :, :], prm[:, 4:7, :])
        cur = 1
        for _ in range(3):
            nx = 1 - cur
            c_, s_, r_ = prm8[:, cur, 0, :], prm8[:, cur, 1, :], prm8[:, cur, 2, :]
            TT("dve", io8, [b_T12], q1, c_, c_, ALU.mult)
            TT("dve", io8, [b_T12], q2, s_, s_, ALU.mult)
            TT("dve", io8, [b_prm8], prm8[:, nx, 0, :], q1, q2, ALU.subtract)
            STT(io8, [b_prm8], prm8[:, nx, 1, :], c_, 2.0, s_, ALU.mult, ALU.mult)
            TT("dve", io8, [b_prm8], prm8[:, nx, 2, :], r_, r_, ALU.mult)
            cur = nx
        assert cur == 0

        yield
        Cn = sb("Cn", [128, 2, 2, 128]); b_Cn = Buf("Cn")
        b_Cnl = []
        for x, csrc in enumerate((c_re, c_im)):
            for j in range(16):
                b_Cnl.append(Buf("Cn%d_%d" % (x, j)))
                P.dma(Cn[16 * (j % 8):16 * (j % 8) + 16, x, j // 8, :].rearrange("c (gl p) -> c gl p", gl=2),
                      csrc[2 * j:2 * j + 2, :, :].rearrange("gl c p -> c gl p"), writes=[b_Cnl[-1]])

        def trc(e):
            ins = None
            for x in range(2):
                for jj in range(2):
                    sl = (x * 2 + jj) * 128
                    ins = e.transpose(out=psI[1][:, sl:sl + 128], in_=Cn[:, x, jj, :], identity=ident[:, :])
            return ins
        P.op("pe", trc, b_Cnl + [b_ident], [bpsI[1]])
        Csl = sb("Csl", [128, 3, 16, 16]); b_Csl = Buf("Csl")
        for x in range(2):
            for jj in range(2):
                sl = (x * 2 + jj) * 128
                CP("dve", [bpsI[1], b_Csl], [b_Csl], Csl[:, x, 8 * jj:8 * jj + 8, :],
                   psI[1][:, sl:sl + 128].rearrange("p (j c) -> p j c", j=8))
        TS("dve", [b_Csl], [b_Csl], Csl[:, 2, :, :], Csl[:, 1, :, :], -1.0, None, ALU.mult)
        yield
        Czp = sb("Czp", [128, 16, 2, 128], BF16); b_Czp = Buf("Czp")
        MEMSET("pool", [b_Czp], Czp[:], 0.0)
        glm4t = glm[:, :].unsqueeze(1).unsqueeze(3).to_broadcast([128, 4, 2, 16])
        glm4 = glm[:, :].unsqueeze(1).unsqueeze(3).to_broadcast([128, 16, 2, 16])
        for jm in range(4):
            for x in range(2):
                TT("pool", [b_Csl, b_glm, b_Czp], [b_Czp],
                   Czp[:, jm::4, x, 32 * jm:32 * jm + 32].rearrange("p t (g c) -> p t g c", g=2),
                   Csl[:, (0 if x == 0 else 2), jm::4, :].unsqueeze(2).to_broadcast([128, 4, 2, 16]), glm4t, ALU.mult)
        yield
        Pq = sb("Pq", [128, 2, 16, 16]); b_Pq = Buf("Pq")
        T12p = sb("T12p", [128, 2, 16, 16]); b_T12p = Buf("T12p")
        u1 = T12p[:, 0, :, :]
        u2 = T12p[:, 1, :, :]
        for tau in range(8):
            yield
            k = tau + 1
            Ar = Apw[:, 0, k, :].unsqueeze(2).to_broadcast([128, 16, 16])
            Ai = Apw[:, 1, k, :].unsqueeze(2).to_broadcast([128, 16, 16])
            nAi = Apw[:, 2, k, :].unsqueeze(2).to_broadcast([128, 16, 16])
            ioc = [b_Csl, b_Apw, b_T12p, b_Pq]
            TT("pool", ioc, [b_T12p], u1, Csl[:, 0, :, :], Ar, ALU.mult)
            TT("pool", ioc, [b_T12p], u2, Csl[:, 1, :, :], Ai, ALU.mult)
            TT("pool", ioc, [b_Pq], Pq[:, 0, :, :], u1, u2, ALU.subtract)
            TT("pool", ioc, [b_T12p], u1, Csl[:, 0, :, :], nAi, ALU.mult)
            TT("pool", ioc, [b_T12p], u2, Csl[:, 1, :, :], Ar, ALU.mult)
            TT("pool", ioc, [b_Pq], Pq[:, 1, :, :], u1, u2, ALU.subtract)
            for x in range(2):
                TT("pool", [b_Pq, b_glm, b_CWt], [b_CWt], CWt[:, :, tau, x, :].rearrange("p j (g c) -> p j g c", g=2),
                   Pq[:, x, :, :].unsqueeze(2).to_broadcast([128, 16, 2, 16]), glm4, ALU.mult)

        yield
        Xx = sb("Xx", [128, 2, 16, 16]); b_Xx = Buf("Xx")
        XKb = [sb("XKb", [128, 2, 16, 2, 16], BF16) for _ in range(2)]; b_XKb = [Buf("XKb") for _ in range(2)]
        for k in range(8):
            yield
            i = k % 2
            if k == 0:
                Xsrc, b_Xsrc = bb, b_bb
            else:
                Ar = Apw[:, 0, k, :].unsqueeze(2).to_broadcast([128, 16, 16])
                Ai = Apw[:, 1, k, :].unsqueeze(2).to_broadcast([128, 16, 16])
                iox = [b_bb, b_Apw, b_T12, b_Xx]
                TT("dve", iox, [b_T12], t1, bb[:, 0, :, :], Ar, ALU.mult)
                TT("dve", iox, [b_T12], t2, bb[:, 1, :, :], Ai, ALU.mult)
                TT("dve", iox, [b_Xx], Xx[:, 0, :, :], t1, t2, ALU.subtract)
                TT("dve", iox, [b_T12], t1, bb[:, 0, :, :], Ai, ALU.mult)
                TT("dve", iox, [b_T12], t2, bb[:, 1, :, :], Ar, ALU.mult)
                TT("dve", iox, [b_Xx], Xx[:, 1, :, :], t1, t2, ALU.add)
                Xsrc, b_Xsrc = Xx, b_Xx
            for x in range(2):
                TT("dve", [b_Xsrc, b_glm, b_XKb[i]], [b_XKb[i]], XKb[i][:, x, :, :, :],
                   Xsrc[:, x, :, :].unsqueeze(2).to_broadcast([128, 16, 2, 16]), glm4, ALU.mult)

            def trk(e, i=i):
                ins = None
                for jt in range(4):
                    for x in range(2):
                        sl = (jt * 2 + x) * 128
                        ins = e.transpose(out=psT[:, sl:sl + 128],
                                          in_=XKb[i][:, x, 4 * jt:4 * jt + 4, :, :].rearrange("p j g c -> p (j g c)"),
                                          identity=identb[:, :])
                return ins
            P.op("pe", trk, [b_XKb[i], b_identb], bpsT)
            CP("act", bpsT, [b_PK], PK[:, :, :, k, :].rearrange("p t x c -> p (t x) c"),
               psT[:, :].rearrange("p (s c) -> p s c", s=8))
            kb = k % 2

            def mk(e, i=i, kb=kb):
                ins = None
                for jt in range(4):
                    for jm in range(4):
                        j = 4 * jt + jm
                        for x in range(2):
                            ins = e.matmul(psI[kb][32 * jm:32 * jm + 32, jt * 128:(jt + 1) * 128],
                                           lhsT=XKb[i][:, x, j, :, :].rearrange("p g c -> p (g c)"), rhs=Czp[:, j, x, :],
                                           start=(x == 0), stop=(x == 1), tile_position=(0, 32 * jm))
                return ins
            P.op("pe", mk, [b_XKb[i], b_Czp], [bpsI[kb]])
            if k == 0:
                for jt in range(4):
                    STT([b_ident, b_dcol], [b_KT, bpsI[kb]], KT[:, jt, 0, :], ident[:, :], dcol[:, jt:jt + 1],
                        psI[kb][:, jt * 128:(jt + 1) * 128], ALU.mult, ALU.add)
            else:
                CP("act", [bpsI[kb]], [b_KT], KT[:, :, k, :], psI[kb][:, :].rearrange("p (t c) -> p t c", t=4))

        yield
        h0v = rl[:, :, :].rearrange("s a b -> s (a b)")
        for x in range(2):
            P.dma(h0v, (s5re0 if x == 0 else s5im0)[:, :], writes=[b_rl])

            def trh(e, x=x):
                ins = None
                for j in range(16):
                    ins = e.transpose(out=psBU[x][:, j * 16:(j + 1) * 16], in_=h0v[:, 128 * j:128 * j + 128],
                                      identity=ident[0:16, 0:16])
                return ins
            P.op("pe", trh, [b_rl, b_ident], [bpsBU[x]])
            CP("act", [bpsBU[x]], [b_h0s], h0s[:, x, :, :], psBU[x][:, 0:256].rearrange("p (j s) -> p j s", j=16))

        yield
        CP("pool", [b_prm8], [b_Ut], Ut[:, 0, :, 0], prm8[:, 0, 0, :])
        CP("pool", [b_prm8, b_Ut], [b_Ut], Ut[:, 1, :, 0], prm8[:, 0, 1, :])
        CP("pool", [b_prm8], [b_pw], pw[:, 0, :, :], prm8[:, 0, 0:2, :])
        n = 1
        cur = 0
        ta = T12p[:, :, :, :].rearrange("p x j c -> p (x j c)").rearrange("p (j n) -> p j n", n=32)
        tb = Pq[:, :, :, :].rearrange("p x j c -> p (x j c)").rearrange("p (j n) -> p j n", n=32)
        while n < TCB:
            yield
            cn = pw[:, cur, 0, :].unsqueeze(2).to_broadcast([128, 16, n])
            sn = pw[:, cur, 1, :].unsqueeze(2).to_broadcast([128, 16, n])
            ur = Ut[:, 0, :, 0:n]
            ui = Ut[:, 1, :, 0:n]
            io = [b_Ut, b_pw, b_T12p, b_Pq]
            TT("pool", io, [b_T12p], ta[:, :, 0:n], ur, cn, ALU.mult)
            TT("pool", io, [b_T12p], tb[:, :, 0:n], ui, sn, ALU.mult)
            TT("pool", io, [b_Ut], Ut[:, 0, :, n:2 * n], ta[:, :, 0:n], tb[:, :, 0:n], ALU.subtract)
            TT("pool", io, [b_T12p], ta[:, :, 0:n], ur, sn, ALU.mult)
            TT("pool", io, [b_T12p], tb[:, :, 0:n], ui, cn, ALU.mult)
            TT("pool", io, [b_Ut], Ut[:, 1, :, n:2 * n], ta[:, :, 0:n], tb[:, :, 0:n], ALU.add)
            if 2 * n < TCB:
                c_ = pw[:, cur, 0, :]
                s_ = pw[:, cur, 1, :]
                nx = 1 - cur
                TT("pool", io, [b_T12p], ta[:, :, 0], c_, c_, ALU.mult)
                TT("pool", io, [b_T12p], tb[:, :, 0], s_, s_, ALU.mult)
                TT("pool", io, [b_pw], pw[:, nx, 0, :], ta[:, :, 0], tb[:, :, 0], ALU.subtract)
                TT("pool", io, [b_T12p], tb[:, :, 0], c_, s_, ALU.mult)
                TS("pool", io, [b_pw], pw[:, nx, 1, :], tb[:, :, 0], 2.0, 1.0, ALU.mult, ALU.mult)
                cur = nx
            n *= 2

    def s5_loop():
        NB = LP // 8
        NCH = NB // TCB
        um = [sb("um", [128, NT], BF16) for _ in range(2)]; b_um = [Buf("um") for _ in range(2)]
        Hp = [sb("Hp", [128, 4, 2, NB + 8], BF16) for _ in range(2)]; b_Hp = [Buf("Hp") for _ in range(2)]
        HpS = [sb("HpS", [128, 4, 2, NS], BF16) for _ in range(2)]; b_HpS = [Buf("HpS") for _ in range(2)]
        tm = sb("tm", [128, 4, 4, TCB]); b_tm = Buf("tm")
        gin = sb("gin", [128, NCH, 2, 4, TCB]); b_gin = Buf("gin")
        gflat = gin[:, :, :, :, :].rearrange("p h x j c -> p (h x j c)")
        glu_tmp2.append((gflat[:, 0:512], gflat[:, 1024:1536], b_gin))
        r8m = xnb[0][:, :].bitcast(F32).rearrange("p (x j c) -> p x j c", x=2, j=4); b_r8m = Buf("r8m")
        cinj = sb("cinj", [128, 2, 4]); b_cinj = Buf("cinj")
        G2 = [sb("G", [128, 2, 4, TCB]) for _ in range(2)]; b_G2 = [Buf("G") for _ in range(2)]
        dmd, b_dmd = tm, b_tm
        gcnt = [0]
        lc = sb("lc", [128, 4, 4]); b_lc = Buf("lc")
        ls_ = sb("ls_", [128, 4, 4, NS]); b_ls = Buf("ls")
        bpsE = bpsBU
        Ev = psE[:, :].rearrange("p (j x c) -> p j x c", j=4, x=2)
        Es = psX[:, 0:8 * NS].rearrange("p (j x s) -> p j x s", j=4, x=2)
        for p_ in range(2):
            MEMSET("pool", [b_Hp[p_]], Hp[p_][:, :, :, 0:1], 0.0)
        ycount = [0]

        def stage_e(jt):
            for jm in range(4):
                j = 4 * jt + jm
                i = j % 2
                ACT([b_uT[jt], b_rm4], [b_um[i]], um[i][:, :], uT[:, jt, :], AF.Copy, scale=rm4[:, jm:jm + 1])

                def me(e, jm=jm, i=i):
                    ins = None
                    for x in range(2):
                        for sg in range(8):
                            ins = e.matmul(Ev[:, jm, x, :], lhsT=PK[:, jt, x, 7 - sg, :], rhs=um[i][:, sg * 256:(sg + 1) * 256],
                                           start=(sg == 0), stop=(sg == 7))
                    return ins
                P.op("pe", me, [b_PK, b_um[i]], bpsE)

                def mes(e, jm=jm, i=i):
                    ins = None
                    for x in range(2):
                        for sg in range(LS):
                            ins = e.matmul(Es[:, jm, x, :], lhsT=PK[:, jt, x, LS - 1 - sg, :], rhs=um[i][:, LP + sg * NS:LP + (sg + 1) * NS],
                                           start=(sg == 0), stop=(sg == LS - 1))
                    return ins
                P.op("pe", mes, [b_PK, b_um[i]], [b_psX])

        def stage_mod(jt):
            js = slice(4 * jt, 4 * jt + 4)
            Cr = Ut[:, 0, js, :].unsqueeze(2).to_broadcast([128, 4, NCH, TCB])
            Ci = Ut[:, 1, js, :].unsqueeze(2).to_broadcast([128, 4, NCH, TCB])
            v4 = lambda ap: ap.rearrange("p j (h c) -> p j h c", h=NCH)
            Br = v4(Ev[:, :, 0, :])
            Bi = v4(Ev[:, :, 1, :])
            gr = gin[:, :, 0, :, :].rearrange("p h j c -> p j h c")
            gi = gin[:, :, 1, :, :].rearrange("p h j c -> p j h c")
            TT("dve", [b_prm8, b_segm, b_r8m], [b_r8m], r8m,
               prm8[:, 0, 2, js].unsqueeze(1).unsqueeze(3).to_broadcast([128, 2, 4, TCB]),
               segm[:, 0:TCB].unsqueeze(1).unsqueeze(2).to_broadcast([128, 2, 4, TCB]), ALU.mult)
            tmp = tm[:, :, :, :]
            rd = bpsE + [b_Ut]
            TT("dve", rd + [b_gin], [b_gin], gr, Br, Cr, ALU.mult)
            TT("dve", rd, [b_tm], tmp, Bi, Ci, ALU.mult)
            TT("dve", [b_tm, b_gin], [b_gin], gr, gr, tmp, ALU.add)
            TT("dve", rd + [b_gin], [b_gin], gi, Bi, Cr, ALU.mult)
            TT("dve", rd + [b_tm], [b_tm], tmp, Br, Ci, ALU.mult)
            TT("dve", [b_tm, b_gin], [b_gin], gi, gi, tmp, ALU.subtract)

        def stage_sample(jt):
            js = slice(4 * jt, 4 * jt + 4)
            p_ = jt % 2
            A4r = Apw[:, 0, LS, js].unsqueeze(2).to_broadcast([128, 4, NS])
            A4i = Apw[:, 1, LS, js].unsqueeze(2).to_broadcast([128, 4, NS])
            hr, hi = h0s[:, 0, js, :], h0s[:, 1, js, :]
            io = [b_h0s, b_Apw, b_ls]
            TT("dve", io, [b_ls], ls_[:, 0, :, :], hr, A4r, ALU.mult)
            TT("dve", io, [b_ls], ls_[:, 1, :, :], hi, A4i, ALU.mult)
            TT("dve", io, [b_ls], ls_[:, 2, :, :], hr, A4i, ALU.mult)
            TT("dve", io, [b_ls], ls_[:, 3, :, :], hi, A4r, ALU.mult)
            TT("dve", io, [b_ls], ls_[:, 0, :, :], ls_[:, 0, :, :], ls_[:, 1, :, :], ALU.subtract)
            TT("dve", io, [b_ls], ls_[:, 2, :, :], ls_[:, 2, :, :], ls_[:, 3, :, :], ALU.add)
            TT("dve", [b_ls, b_hS], [b_hS, b_psX], hS[:, 0, js, :], ls_[:, 0, :, :], Es[:, :, 0, :], ALU.add)
            TT("dve", [b_ls, b_hS], [b_hS, b_psX], hS[:, 1, js, :], ls_[:, 2, :, :], Es[:, :, 1, :], ALU.add)
            for x in range(2):
                CP("act", [b_h0s, b_HpS[p_]], [b_HpS[p_]], HpS[p_][:, :, x, :], h0s[:, x, js, :])

        def stage_scan(jt, ch):
            G, b_G = G2[gcnt[0] % 2], b_G2[gcnt[0] % 2]
            gcnt[0] += 1
            js = slice(4 * jt, 4 * jt + 4)
            p_ = jt % 2
            cs = slice(ch * TCB, (ch + 1) * TCB)
            n = TCB
            gch = gin[:, ch, :, :, :]
            if ch > 0:
                TT("dve", [b_carry[jt], b_prm8], [b_cinj], cinj[:, :, :], carry[:, :, js],
                   prm8[:, 0, 2, js].unsqueeze(1).to_broadcast([128, 2, 4]), ALU.mult)
                TT("dve", [b_cinj, b_gin], [b_gin], gch[:, :, :, 0], gch[:, :, :, 0], cinj[:, :, :], ALU.add)
            P.op("dve", lambda e: e.tensor_tensor_scan(out=G[:, :, :, :].rearrange("p x j c -> p (x j c)"),
                                                        data0=r8m.rearrange("p x j c -> p (x j c)"),
                                                        data1=gch.rearrange("p x j c -> p (x j c)"),
                                                        initial=0.0, op0=ALU.mult, op1=ALU.add),
                 [b_gin, b_r8m], [b_G])
            Cr = Ut[:, 0, js, :]
            Ci = Ut[:, 1, js, :]
            Gr, Gi = G[:, 0, :, :], G[:, 1, :, :]
            Gx = G[:, :, :, :]
            Crx = Cr.unsqueeze(1).to_broadcast([128, 2, 4, TCB])
            Cix = Ci.unsqueeze(1).to_broadcast([128, 2, 4, TCB])
            TT("dve", [b_G, b_Ut], [b_dmd], dmd[:, 0:2, :, :], Gx, Crx, ALU.mult)
            TT("dve", [b_G, b_Ut], [b_dmd], dmd[:, 2:4, :, :], Gx, Cix, ALU.mult)
            D = [dmd[:, 0, :, :], dmd[:, 3, :, :], dmd[:, 2, :, :], dmd[:, 1, :, :]]
            TT("dve", [b_dmd], [b_carry[jt]], carry[:, 0, js], D[0][:, :, n - 1], D[1][:, :, n - 1], ALU.subtract)
            TT("dve", [b_dmd, b_carry[jt]], [b_carry[jt]], carry[:, 1, js], D[2][:, :, n - 1], D[3][:, :, n - 1], ALU.add)
            hs_ = slice(ch * TCB + 1, (ch + 1) * TCB + 1)
            TT("dve", [b_dmd, b_Hp[p_]], [b_Hp[p_]], Hp[p_][:, :, 0, hs_], D[0], D[1], ALU.subtract)
            TT("dve", [b_dmd, b_Hp[p_]], [b_Hp[p_]], Hp[p_][:, :, 1, hs_], D[2], D[3], ALU.add)

        ga_s = [ea, xnb[0][:, :].bitcast(F32)]; b_ga_s = [b_ea, Buf("ga2")]
        ge_s = [eb2, xnb[1][:, :].bitcast(F32)]; b_ge_s = [b_eb2, Buf("ge2")]
        gst = [junk[:, 0:512], junk[:, 512:1024]]; b_gst = [Buf("gst0"), Buf("gst1")]
        ybanks = [(psI[0][:, :], bpsI[0]), (psI[1][:, :], bpsI[1]), (psT[:, :].bitcast(F32), bpsT[0])]

        def make_y(jt, g):
            p_ = jt % 2
            i = ycount[0]
            ycount[0] += 1
            st_ = i % 2
            ybank, b_yb = ybanks[i % 3]
            sample = (g == NCH)
            if not sample:
                c0 = g * 64
                t0 = 8 * c0
                nel, nt = 512, 8
                yb = ybank[:, 0:512]
                yv = yb.rearrange("p (s c) -> p s c", s=8)
                uv = uT[:, jt, 0:LP].rearrange("p (s c) -> p s c", s=8)[:, :, c0:c0 + 64]
                gv = gT[:, jt, t0:t0 + 512].rearrange("p (c s) -> p s c", s=8)
                hv = lambda jm, x: Hp[p_][:, jm, x, c0:c0 + 64]
                bh = b_Hp[p_]
            else:
                nel, nt = NSAMP, LS
                yb = ybank[:, 0:NSAMP]
                yv = yb.rearrange("p (t s) -> p t s", t=LS)
                uv = uT[:, jt, LP:NT].rearrange("p (t s) -> p t s", t=LS)
                gv = gT[:, jt, LP:NT].rearrange("p (s t) -> p t s", t=LS)
                hv = lambda jm, x: HpS[p_][:, jm, x, :]
                bh = b_HpS[p_]
            a = ga_s[st_][:, 0:nel]
            e_ = ge_s[st_][:, 0:nel]
            b_ga, b_ge = b_ga_s[st_], b_ge_s[st_]
            gs = gst[st_][:, 0:nel]
            gsv = gs.rearrange("p (s c) -> p s c", s=nt)

            def g1():
                def my(e):
                    ins = None
                    for k in range(nt):
                        ins = e.matmul(yv[:, k:nt, :], lhsT=KT[:, jt, k, :], rhs=uv[:, 0:nt - k, :], start=(k == 0), stop=False,
                                       skip_group_check=True)
                    for jm in range(4):
                        j = 4 * jt + jm
                        for tau in range(nt):
                            for x in range(2):
                                last = (jm == 3 and tau == nt - 1 and x == 1)
                                ins = e.matmul(yv[32 * jm:32 * jm + 32, tau, :], lhsT=CWt[:, j, tau, x, :], rhs=hv(jm, x),
                                               start=False, stop=last, tile_position=(0, 32 * jm), skip_group_check=True)
                    return ins
                P.op("pe", my, [b_KT, b_uT[jt], b_CWt, bh], [b_yb])

            def g2():
                ACT([b_yb], [b_gT], gv, yv, AF.Gelu_apprx_tanh)

            def g3():
                pass

            def g4():
                pass

            def g5():
                pass

            def g6():
                pass
            return (g1, g2, g3, g4, g5, g6)

        items = []
        for jt in range(4):
            for ch in range(NCH + 1):
                items.append((jt, ch))
        ys = {}
        stage_e(0)
        n_items = len(items)
        for idx in range(n_items + 2):
            if idx < n_items:
                jt, ch = items[idx]
                if ch == 0:
                    stage_mod(jt)
                    stage_sample(jt)
                    if jt + 1 < 4:
                        stage_e(jt + 1)
                if ch < NCH:
                    stage_scan(jt, ch)
                ys[idx] = make_y(jt, ch)
                ys[idx][0]()
                ys[idx][1]()
            if 0 <= idx - 1 < n_items:
                ys[idx - 1][2]()
                ys[idx - 1][3]()
            if 0 <= idx - 2 < n_items:
                ys[idx - 2][4]()
                ys[idx - 2][5]()

        tmflat = tm[:, :, :, :].rearrange("p q j c -> p (q j c)")
        sos = [(tmflat[0:16, 0:512], b_tm),
               (G2[0][0:16, :, :, :].rearrange("p x j c -> p (x j c)"), b_G2[0]),
               (G2[1][0:16, :, :, :].rearrange("p x j c -> p (x j c)"), b_G2[1])]
        sk = [0]

        def nxt_so():
            r = sos[sk[0] % 3]
            sk[0] += 1
            return r
        for x, dst in enumerate((o_pre, o_pim)):
            so, b_so = nxt_so()
            P.op("pe", lambda e, x=x: e.transpose(out=psI[x][0:16, 0:128], in_=carry[:, x, :], identity=ident[:, :]),
                 b_carry + [b_ident], [bpsI[x]])
            CP("act", [bpsI[x]], [b_so], so[:, 0:128], psI[x][0:16, 0:128])
            P.dma(dst[:, :], so[:, 0:128], reads=[b_so], is_output=True)
        for x, dst in enumerate((o_sre, o_sim)):
            for qt in range(4):
                so, b_so = nxt_so()

                def trs(e, x=x, qt=qt):
                    ins = None
                    for jj in range(4):
                        j = 4 * qt + jj
                        ins = e.transpose(out=psI[qt % 2][0:16, jj * 128:(jj + 1) * 128], in_=hS[:, x, j, :], identity=ident[:, :])
                    return ins
                P.op("pe", trs, [b_hS, b_ident], [bpsI[qt % 2]])
                CP("act", [bpsI[qt % 2]], [b_so], so[:, :], psI[qt % 2][0:16, :])
                P.dma(dst[:, qt * 512:(qt + 1) * 512], so[:, :], reads=[b_so], is_output=True)

    def load_wg():
        for hf in range(2):
            s = wcount[0] % 2
            wcount[0] += 1
            stv = wst[s][:, :, :].rearrange("p k c -> p (k c)").rearrange("p (k c) -> p k c", k=4)
            P.dma(stv, w_glu[:, hf * 256:(hf + 1) * 256].rearrange("(k p) c -> p k c", p=128), writes=[b_wst[s]])
            CP("act", [b_wst[s]], [b_wg], wg[:, :, hf * 256:(hf + 1) * 256], stv)

    def load_wo():
        wst2 = [sb("wst2", [128, 8, 128]) for _ in range(2)]
        b_wst2 = [Buf("wst2") for _ in range(2)]
        for cb in range(8):
            s = cb % 2
            P.dma(wst2[s][:], w_out[:, cb * 128:(cb + 1) * 128].rearrange("(k p) c -> p k c", p=128), writes=[b_wst2[s]], eng="pool")
            CP("pool", [b_wst2[s]], [b_wo], wo[:, :, cb * 128:(cb + 1) * 128], wst2[s][:])

    def glu():
        ibank_sel[0] = ibanks4
        gcount = [0]
        a2, b2, b_ab2 = glu_tmp2[0]
        sets = [(ea, eb2, b_ea, b_eb2), (a2, b2, b_ab2, b_ab2)]
        for fo in range(4):
            wt, bw = load_w_tile(512 + 128 * fo, 512 + 128 * (fo + 1) if fo < 3 else 1536)
            for (c0, n) in col_chunks():
                ps1, bps1 = inproj_chunk(wt, bw, c0, n)
                k = gcount[0] % 2
                ps2 = psBU[k][:, 0:n]

                def mg(e, fo=fo, c0=c0, n=n, ps2=ps2):
                    ins = None
                    for kf in range(4):
                        ins = e.matmul(ps2, lhsT=wg[:, kf, 128 * fo:128 * fo + 128], rhs=gT[:, kf, c0:c0 + n],
                                       start=(kf == 0), stop=(kf == 3))
                    return ins
                P.op("pe", mg, [b_wg, b_gT], [bpsBU[k]])
                A_, B_, bA, bB = sets[gcount[0] % 2]
                gcount[0] += 1
                a = A_[:, 0:n]
                b = B_[:, 0:n]
                ACT([bps1], [bA], a, ps1, AF.Sigmoid)
                ACT([bpsBU[k], b_nbg], [bB], b, ps2, AF.Sigmoid, scale=1.0, bias=pbg[:, fo:fo + 1])
                TT("dve", [bA, bB], [bA], a, a, b, ALU.mult)
                TT("dve", [bps1, bA, bB], [bB], b, ps1, a, ALU.mult)
                TT("dve", [bB, b_gT, b_uT[fo]], [b_uT[fo]], uT[:, fo, c0:c0 + n], b, gT[:, fo, c0:c0 + n], ALU.mult)

        ibank_sel[0] = ibanks2

    def hgrn():
        ibank_sel[0] = ibanks4
        hs = []
        for i in range(2):
            hs.append(dict(
                ebt=sb("ebt", [128, NT]), kT=sb("kT", [128, NT], BF16), qT=sb("qT", [128, NT], BF16),
                vtk=sb("vtk", [128, 17, 128], BF16), szh=sb("szh", [128, NT], BF16),
                b_ebt=Buf("ebt"), b_kT=Buf("kT"), b_qT=Buf("qT"), b_vtk=Buf("vtk"), b_szh=Buf("szh")))
        pieces = [res[0][:, 0:512], res[0][:, 512:1024], res[1][:, 0:512], res[1][:, 512:1024],
                  xin[0][:, 0:512], xin[0][:, 512:1024], xin[1][:, 0:512], xin[1][:, 512:1024]]
        tmps = [[(pieces[4 * i + k], Buf("tmp")) for k in range(4)] for i in range(2)]
        NSB = 10
        S0f = [sb("S0f%d" % i, [128, 128]) for i in range(NSB)]; b_S0f = [Buf("S0f%d" % i) for i in range(NSB)]
        S0b = [sb("S0b%d" % i, [128, 128], BF16) for i in range(NSB)]; b_S0b = [Buf("S0b%d" % i) for i in range(NSB)]
        scount = [0]

        def mkctx(hp):
            c = {}
            def two(name, shape, dt=F32):
                c[name] = [sb(name, shape, dt) for _ in range(2)]
                c["b_" + name] = [Buf(name) for _ in range(2)]
            two("S2", [128, 128]); two("Sb", [128, 128], BF16); two("attm", [128, 128], BF16); two("kh", [128, 128], BF16)
            two("khT", [128, 128], BF16); two("o32", [128, 128]); two("osq", [128, 128], BF16); two("rs", [128, 128])
            two("hgt", [128, 128]); two("khm", [64, 128], BF16)
            if hp == 0:
                bk0, bk1, bk2 = Buf("hb0", True), Buf("hb1", True), Buf("hb2", True)
                c["psA"], c["psR"], c["b_psA"], c["b_psR"] = psE[:, 0:128], psE[:, 128:256], bk0, bk0
                c["psO"], c["b_psO"] = psE[:, 512:640], bk1
                c["psS"], c["b_psS"] = [psE[:, 1024:1152], psE[:, 1152:1280]], [bk2, bk2]
            else:
                bk3 = Buf("hb3", True)
                c["psA"], c["psR"], c["b_psA"], c["b_psR"] = psE[:, 1536:1664], psE[:, 1664:1792], bk3, bk3
                c["psO"], c["b_psO"] = psI[0][:, 0:128], bpsI[0]
                c["psS"], c["b_psS"] = [psI[1][:, 0:128], psI[1][:, 128:256]], [bpsI[1], bpsI[1]]
            return c
        ctxs = [mkctx(0), mkctx(1)]
        tcount = [0]
        ccount = [0]

        def seg_of(c0, n):
            return segm[:, 0:n] if c0 < LP else segm[:, 512:512 + n]

        def stage_a(h):
            H = hs[h % 2]
            ebt, b_ebt = H["ebt"], H["b_ebt"]
            lb_ = lbc[:, 0, h:h + 1]
            om_ = lbc[:, 1, h:h + 1]
            nom_ = nom[:, h:h + 1]
            wt, bw = load_w_tile(1536 + 128 * h, 2560 + 128 * h)
            for (c0, n) in col_chunks():
                ps, bps = inproj_chunk(wt, bw, c0, n)
                ACT([bps], [b_ebt], ebt[:, c0:c0 + n], ps, AF.Sigmoid)
                yield
            wt, bw = load_w_tile(2560 + 128 * h, 2048 + 128 * h)
            for (c0, n) in col_chunks():
                Tm = tmps[ccount[0] % 2]
                ccount[0] += 1
                (ta, b_ta) = Tm[0]
                ps, bps = inproj_chunk(wt, bw, c0, n)
                ACT([bps], [b_ta], ta[:, 0:n], ps, AF.Sigmoid)
                TT("dve", [bps, b_ta], [H["b_szh"]], H["szh"][:, c0:c0 + n], ps, ta[:, 0:n], ALU.mult)
                yield
            wt, bw = load_w_tile(2048 + 128 * h, 1024 + 128 * h)

            def v_tile(tt, wt=wt, bw=bw):
                rows = 128 if tt < 16 else NSAMP
                c0 = tt * 128
                s_ = icount[0] % 2
                icount[0] += 1

                def mv(e):
                    ins = None
                    for kd in range(8):
                        ins = e.matmul(psI[s_][0:rows, 0:128], lhsT=xT[:, kd, c0:c0 + rows], rhs=wt[:, kd, :],
                                       start=(kd == 0), stop=(kd == 7))
                    return ins
                P.op("pe", mv, [bw, b_xT[tt]], [bpsI[s_]])
                CP("act", [bpsI[s_]], [H["b_vtk"]], H["vtk"][0:rows, tt, :], psI[s_][0:rows, 0:128])
            vt = 0
            for ci, (c0, n) in enumerate(col_chunks()):
                Tm = tmps[ccount[0] % 2]
                ccount[0] += 1
                (ta, b_ta), (tb, b_tb), (tc_, b_tc), (te, b_te) = Tm
                sg = ebt[:, c0:c0 + n]
                ACT([b_ebt, b_lbc], [b_tb], tb[:, 0:n], sg, AF.Ln, scale=om_, bias=lb_)
                ACT([b_ebt, b_lbc, b_nom], [b_tc], tc_[:, 0:n], sg, AF.Identity, scale=nom_, bias=om_)
                P.op("dve", lambda e, c0=c0, n=n, te=te, tb=tb: e.tensor_tensor_scan(
                    out=te[:, 0:n], data0=seg_of(c0, n), data1=tb[:, 0:n], initial=0.0, op0=ALU.mult, op1=ALU.add),
                    [b_tb, b_segm], [b_te])
                for _ in range(4 if ci < 4 else 1):
                    v_tile(vt)
                    vt += 1
                ACT([b_te], [b_ebt], ebt[:, c0:c0 + n], te[:, 0:n], AF.Exp)
                ACT([b_te, b_tc], [b_ta], ta[:, 0:n], te[:, 0:n], AF.Exp, scale=-1.0)
                TT("dve", [b_tc, b_ta], [H["b_kT"]], H["kT"][:, c0:c0 + n], tc_[:, 0:n], ta[:, 0:n], ALU.mult)
                yield
            assert vt == 17
            wt, bw = load_w_tile(1024 + 128 * h, 1536 + 128 * (h + 1) if h < 3 else None)
            for (c0, n) in col_chunks():
                ps, bps = inproj_chunk(wt, bw, c0, n)
                TT("dve", [bps, b_ebt], [H["b_qT"]], H["qT"][:, c0:c0 + n], ps, ebt[:, c0:c0 + n], ALU.mult)
                yield

        def stage_b(h, cx):
            S2, Sb, attm, kh, khT, o32, osq, rs, hgt, khm = (cx[k] for k in ("S2", "Sb", "attm", "kh", "khT", "o32", "osq", "rs", "hgt", "khm"))
            b_S2, b_Sb, b_attm, b_kh, b_khT, b_o32, b_osq, b_rs, b_hgt, b_khm = (
                cx["b_" + k] for k in ("S2", "Sb", "attm", "kh", "khT", "o32", "osq", "rs", "hgt", "khm"))
            psA, psR, psO, psS = cx["psA"], cx["psR"], cx["psO"], cx["psS"]
            b_psA, b_psR, b_psO, b_psS = cx["b_psA"], cx["b_psR"], cx["b_psO"], cx["b_psS"]

            def evac_o(p, n):
                CP("dve", [b_psO], [b_o32[p]], o32[p][:, 0:n], psO[:, 0:n])
                ACT([b_psO], [b_osq[p]], osq[p][:, 0:n], psO[:, 0:n], AF.Square)

            def finish_o(h, H, p, c0, n):
                P.op("pe", lambda e: e.matmul(psR[:, 0:n], lhsT=onesb[:, :], rhs=osq[p][:, 0:n], start=True, stop=True),
                     [b_onesb, b_osq[p]], [b_psR])
                ACT([b_psR], [b_rs[p]], rs[p][:, 0:n], psR[:, 0:n], AF.Ln, scale=1.0 / 128.0, bias=EPS)
                ACT([b_rs[p]], [b_rs[p]], rs[p][:, 0:n], rs[p][:, 0:n], AF.Exp, scale=-0.5)
                STT([b_o32[p], b_rs[p], b_ogc], [b_hgt[p]], hgt[p][:, 0:n], o32[p][:, 0:n], ogc[:, h:h + 1], rs[p][:, 0:n], ALU.mult, ALU.mult)
                TT("dve", [b_hgt[p], H["b_szh"]], [b_mixH[h]], mixH[:, h, c0:c0 + n], hgt[p][:, 0:n], H["szh"][:, c0:c0 + n], ALU.mult)

            H = hs[h % 2]
            kT, qT, vtk, ebt = H["kT"], H["qT"], H["vtk"], H["ebt"]
            b_kT, b_qT, b_vtk, b_ebt = H["b_kT"], H["b_qT"], H["b_vtk"], H["b_ebt"]
            MEMSET("pool", [b_S2[1]], S2[1][:], 0.0)
            MEMSET("pool", [b_Sb[1]], Sb[1][:], 0.0)

            def T1a(tt):
                c0 = tt * 128
                p = tt % 2
                P.op("pe", lambda e: e.matmul(psA, lhsT=kT[:, c0:c0 + 128], rhs=qT[:, c0:c0 + 128], start=True, stop=True),
                     [b_kT, b_qT], [b_psA])
                TT("dve", [b_psA, b_mask2], [b_attm[p]], attm[p][:, :], psA, mask2[:, :], ALU.mult)
                ACT([b_kT, b_ebt, b_kh[p]], [b_kh[p]], kh[p][:, :], kT[:, c0:c0 + 128], AF.Copy, scale=ebt[:, c0 + 127:c0 + 128])

            def T1b(tt):
                p = tt % 2
                pst = psT[:, 0:128]
                P.op("pe", lambda e: e.transpose(out=pst, in_=kh[p][:, :], identity=identb[:, :]), [b_kh[p], b_identb], [bpsT[0]])
                CP("act", [bpsT[0]], [b_khT[p]], khT[p][:, :], pst)

            def T2a(tt):
                c0 = tt * 128
                p = tt % 2
                for half in range(2):
                    r0 = 64 * half
                    P.op("pe", lambda e, r0=r0, half=half: e.matmul(psS[half], lhsT=khT[p][r0:r0 + 64, :], rhs=vtk[r0:r0 + 64, tt, :],
                                                                    start=True, stop=True),
                         [b_khT[p], b_vtk], [b_psS[half]])
                for half in range(2):
                    r0 = 64 * half
                    cur, prev = half, 1 - half
                    STT([b_S2[prev], b_ebt], [b_S2[cur], b_psS[half]], S2[cur][:, :], S2[prev][:, :], ebt[:, c0 + r0 + 63:c0 + r0 + 64],
                        psS[half], ALU.mult, ALU.add)
                    CP("act", [b_S2[cur]], [b_Sb[cur]], Sb[cur][:, :], S2[cur][:, :])

                def mo(e, r0=0, sprev=1):
                    e.matmul(psO[:, r0:r0 + 64], lhsT=vtk[r0:r0 + 64, tt, :], rhs=attm[p][r0:r0 + 64, r0:r0 + 64], start=True, stop=False)
                    return e.matmul(psO[:, r0:r0 + 64], lhsT=Sb[sprev][:, :], rhs=qT[:, c0 + r0:c0 + r0 + 64], start=False, stop=True)
                return mo

            def T2b(tt):
                c0 = tt * 128
                p = tt % 2

                def mo0(e):
                    e.matmul(psO[:, 0:64], lhsT=vtk[0:64, tt, :], rhs=attm[p][0:64, 0:64], start=True, stop=False)
                    return e.matmul(psO[:, 0:64], lhsT=Sb[1][:, :], rhs=qT[:, c0:c0 + 64], start=False, stop=True)

                def mo1(e):
                    e.matmul(psO[:, 64:128], lhsT=vtk[64:128, tt, :], rhs=attm[p][64:128, 64:128], start=True, stop=False)
                    return e.matmul(psO[:, 64:128], lhsT=Sb[0][:, :], rhs=qT[:, c0 + 64:c0 + 128], start=False, stop=True)
                return mo0, mo1

            for tt in range(16):
                c0 = tt * 128
                p = tt % 2
                cur, prev = tt % 2, 1 - (tt % 2)
                if tt == 0:
                    T1a(0)
                    T1b(0)

                def mo(e, tt=tt, p=p, c0=c0, prev=prev):
                    e.matmul(psO[:, 0:128], lhsT=vtk[:, tt, :], rhs=attm[p][:, :], start=True, stop=False)
                    return e.matmul(psO[:, 0:128], lhsT=Sb[prev][:, :], rhs=qT[:, c0:c0 + 128], start=False, stop=True)
                P.op("pe", mo, [b_vtk, b_attm[p], b_Sb[prev], b_qT], [b_psO])
                P.op("pe", lambda e, tt=tt, p=p: e.matmul(psS[0], lhsT=khT[p][:, :], rhs=vtk[:, tt, :], start=True, stop=True),
                     [b_khT[p], b_vtk], [b_psS[0]])
                STT([b_S2[prev], b_ebt], [b_S2[cur], b_psS[0]], S2[cur][:, :], S2[prev][:, :], ebt[:, c0 + 127:c0 + 128], psS[0],
                    ALU.mult, ALU.add)
                CP("dve", [b_S2[cur]], [b_Sb[cur]], Sb[cur][:, :], S2[cur][:, :])
                yield
                if tt + 1 < 16:
                    T1a(tt + 1)
                evac_o(p, 128)
                yield
                if tt + 1 < 16:
                    T1b(tt + 1)
                if tt >= 1:
                    finish_o(h, H, 1 - p, c0 - 128, 128)
                yield
            finish_o(h, H, 1, 15 * 128, 128)
            P.dma(o_phg[h, :, :], S2[1][:, :], reads=[b_S2[1]], is_output=True)
            p = 0
            P.op("pe", lambda e: e.matmul(psA[0:64, 0:64], lhsT=kT[:, LP:NT], rhs=qT[:, LP:NT], start=True, stop=True),
                 [b_kT, b_qT], [b_psA])
            TT("dve", [b_psA, b_masks], [b_attm[p]], attm[p][0:64, 0:64], psA[0:64, 0:64], masks[:, :], ALU.mult)
            k3 = kT[:, LP:NT].rearrange("p (s t) -> p s t", t=LS)
            e3 = ebt[:, LP:NT].rearrange("p (s t) -> p s t", t=LS)[:, :, LS - 1:LS].to_broadcast([128, NS, LS])
            TT("dve", [b_kT, b_ebt, b_kh[p]], [b_kh[p]], kh[p][:, 0:64].rearrange("p (s t) -> p s t", t=LS), k3, e3, ALU.mult)
            pst = psT[0:64, 0:128]
            P.op("pe", lambda e, pst=pst: e.transpose(out=pst, in_=kh[p][:, 0:64], identity=identb[:, :]), [b_kh[p], b_identb], [bpsT[0]])
            CP("act", [bpsT[0]], [b_khT[p]], khT[p][0:64, :], pst)
            P.op("pe", lambda e: e.matmul(psO[:, 0:64], lhsT=vtk[0:64, 16, :], rhs=attm[p][0:64, 0:64], start=True, stop=False),
                 [b_vtk, b_attm[p]], [b_psO])
            slot = {}

            def ld(sq):
                r = scount[0] % NSB
                scount[0] += 1
                slot[sq] = r
                P.dma(S0f[r][:, :], hg0[sq, h, :, :], writes=[b_S0f[r]])

            def st(sq):
                r = slot[sq]
                P.dma(o_shg[sq, h, :, :], S0f[r][:, :], reads=[b_S0f[r]], is_output=True, eng="pool")
            ld(0)
            ld(1)
            for sq in range(NS):
                if sq + 2 < NS:
                    ld(sq + 2)
                r = slot[sq]
                CP("act", [b_S0f[r]], [b_S0b[r]], S0b[r][:, :], S0f[r][:, :])
                P.op("pe", lambda e, sq=sq, r=r: e.matmul(psO[:, LS * sq:LS * sq + LS], lhsT=S0b[r][:, :], rhs=qT[:, LP + LS * sq:LP + LS * sq + LS],
                                                        start=False, stop=(sq == NS - 1), skip_group_check=True),
                     [b_S0b[r], b_qT], [b_psO])
                m = sq % 2
                ACT([b_khT[p], b_rowm, b_khm[m]], [b_khm[m]], khm[m][:, :], khT[p][0:64, :], AF.Copy, scale=rowm[:, sq:sq + 1])
                P.op("pe", lambda e, m=m: e.matmul(psS[m], lhsT=khm[m][:, :], rhs=vtk[0:64, 16, :], start=True, stop=True),
                     [b_khm[m], b_vtk], [b_psS[m]])
                STT([b_S0f[r], b_ebt], [b_S0f[r], b_psS[m]], S0f[r][:, :], S0f[r][:, :], ebt[:, LP + LS * sq + LS - 1:LP + LS * sq + LS], psS[m],
                    ALU.mult, ALU.add)
                if sq >= 1:
                    st(sq - 1)
                if sq % 2 == 1:
                    yield
            st(NS - 1)
            evac_o(p, NSAMP)
            finish_o(h, H, p, LP, NSAMP)
            yield

        for pair in ((0, 1), (2, 3)):
            for h in pair:
                for _ in stage_a(h):
                    pass
            gens = [stage_b(h, ctxs[i]) for i, h in enumerate(pair)]
            while gens:
                for g in list(gens):
                    try:
                        next(g)
                    except StopIteration:
                        gens.remove(g)

    def outproj():
        banks = [[(psI[0][:, :], bpsI[0]), (psI[1][:, :], bpsI[1])],
                 [(psBU[0][:, 0:512], bpsBU[0]), (psBU[0][:, 512:1024], bpsBU[0])]]

        def mk(tt):
            rows = 128 if tt < 16 else NSAMP
            c0 = tt * 128
            src = xp[c0:c0 + 128, :] if tt < 16 else xs[:, :]
            dst = yp[c0:c0 + 128, :] if tt < 16 else ys[:, :]
            s_ = tt % len(xin)
            bk = banks[tt % 2]

            def L():
                P.dma(xin[s_][0:rows, :], src, writes=[b_xin[s_]])

            def M():
                for dn in range(2):
                    def mo(e, dn=dn):
                        ins = None
                        for kf in range(8):
                            lh = uT[:, kf, c0:c0 + rows] if kf < 4 else mixH[:, kf - 4, c0:c0 + rows]
                            ins = e.matmul(bk[dn][0][0:rows, :], lhsT=lh, rhs=wo[:, kf, dn * 512:(dn + 1) * 512], start=(kf == 0), stop=(kf == 7))
                        return ins
                    P.op("pe", mo, b_uT + b_mixH + [b_wo], [bk[dn][1]])

            def R():
                for dn in range(2):
                    TT("dve", [bk[dn][1], b_xin[s_], b_res[s_]], [b_res[s_]], res[s_][0:rows, dn * 512:(dn + 1) * 512], bk[dn][0][0:rows, :],
                       xin[s_][0:rows, dn * 512:(dn + 1) * 512], ALU.add)

            def N():
                ACT([b_res[s_]], [b_junk, b_stat[s_]], junk[0:rows, :], res[s_][0:rows, :], AF.Square, accum_out=stat[s_][0:rows, 0:1])
                ACT([b_stat[s_]], [b_stat[s_]], stat[s_][0:rows, 1:2], stat[s_][0:rows, 0:1], AF.Ln, scale=1.0 / D, bias=EPS)
                ACT([b_stat[s_]], [b_stat[s_]], stat[s_][0:rows, 2:3], stat[s_][0:rows, 1:2], AF.Exp, scale=-0.5)

            def F():
                STT([b_res[s_], b_stat[s_], b_fgb], [b_res[s_]], res[s_][0:rows, :], res[s_][0:rows, :], stat[s_][0:rows, 2:3], fgb[0:rows, :],
                    ALU.mult, ALU.mult)
                P.dma(dst, res[s_][0:rows, :], reads=[b_res[s_]], is_output=True, eng="act")
            return (L, M, R, N, F)
        T = [mk(tt) for tt in range(17)]
        T[0][0]()
        T[1][0]()
        T[0][1]()
        for t in range(17):
            if t + 2 < 17:
                T[t + 2][0]()
            if t + 1 < 17:
                T[t + 1][1]()
            T[t][2]()
            T[t][3]()
            if t >= 1:
                T[t - 1][4]()
        T[16][4]()

    m_tmp = A.mark()
    print("arena marks: s5p", m_s5p, "tmp", m_tmp)
    def u_inproj():
        for jt in range(4):
            wt, bw = load_w_tile(jt * 128, (jt + 1) * 128 if jt < 3 else None)
            for (c0, n) in col_chunks():
                ps, bps = inproj_chunk(wt, bw, c0, n)
                if c0 < LP:
                    CP("act", [bps], [b_uT[jt]], uT[:, jt, 0:LP].rearrange("p (s c) -> p s c", s=8)[:, :, c0 // 8:c0 // 8 + n // 8],
                       ps.rearrange("p (c s) -> p s c", s=8))
                else:
                    CP("act", [bps], [b_uT[jt]], uT[:, jt, LP:NT].rearrange("p (t s) -> p t s", t=LS),
                       ps.rearrange("p (s t) -> p t s", t=LS))
                yield

    def chain(*gens):
        for g in gens:
            for _ in g:
                yield

    ga_ = chain(phase_a(), u_inproj())
    gp_ = s5_prep()
    alive = [ga_, ga_, gp_]
    while alive:
        for g in list(alive):
            if g not in alive:
                continue
            try:
                next(g)
            except StopIteration:
                while g in alive:
                    alive.remove(g)
    load_wg()
    P.barrier()
    A.reset(m_tmp)
    ea = sb("ea", [128, 512])
    eb2 = sb("eb2", [128, 512])
    glu_tmp2 = []
    s5_loop()
    glu()
    P.barrier()
    A.reset(m_s5p)
    mixH = sb("mixH", [128, 4, NT], BF16)
    wo = sb("wo", [128, 8, D], BF16)
    load_wo()
    m_hg = A.mark()
    hgrn()
    P.barrier()
    A.reset(m_hg)
    NOB = 4
    for _ in range(NOB - 2):
        xin.append(sb("xinx", [128, D])); b_xin.append(Buf("xinx"))
        res.append(sb("resx", [128, D])); b_res.append(Buf("resx"))
        stat.append(sb("statx", [128, 4])); b_stat.append(Buf("statx"))
    outproj()
    print("SBUF peak bytes/partition:", A.peak)
    P.emit(nc)
    return nc


_NC_CACHE = {}


def _consts():
    ident = np.eye(128, dtype=np.float32)
    s = np.arange(128)
    mask2 = (s[:, None] <= s[None, :]).astype(np.float32)
    t = np.arange(64)
    masks = ((t[:, None] // LS == t[None, :] // LS) & (t[:, None] <= t[None, :])).astype(np.float32)
    seg = np.ones((1, 512 + NSAMP), np.float32)
    seg[0, 0:512:128] = 0.0
    seg[0, 512::LS] = 0.0
    rowm = (t[:, None] // LS == np.arange(NS)[None, :]).astype(np.float32)
    q = np.arange(128)
    glm = (q[:, None] // 64 == np.arange(2)[None, :]).astype(np.float32)
    rm4 = (q[:, None] // 32 == np.arange(4)[None, :]).astype(np.float32)
    return dict(c_ident=ident, c_mask2=mask2, c_masks=masks, c_seg=seg, c_rowm=rowm, c_glm=glm, c_rm4=rm4)


def kernel(x_prompt, x_sample, state_s5_re, state_s5_im, state_hgrn, norm_g, w_in, s5_lambda_re, s5_lambda_im,
           s5_log_step, s5_b_re, s5_b_im, s5_c_re, s5_c_im, s5_d, w_glu, b_glu, hgrn_lb_logits, hgrn_onorm_g,
           w_out, final_norm_g):
    f = lambda a: np.ascontiguousarray(np.asarray(a, dtype=np.float32))
    x_prompt, x_sample = f(x_prompt), f(x_sample)
    state_s5_re, state_s5_im, state_hgrn = f(state_s5_re), f(state_s5_im), f(state_hgrn)
    shared = dict(
        norm_g=f(norm_g)[0], w_in=f(w_in)[0], lam_re=f(s5_lambda_re)[0], lam_im=f(s5_lambda_im)[0],
        log_step=f(s5_log_step)[0], b_re=f(s5_b_re)[0], b_im=f(s5_b_im)[0], c_re=f(s5_c_re)[0], c_im=f(s5_c_im)[0],
        s5_d=f(s5_d)[0].reshape(512), w_glu=f(w_glu)[0], b_glu=f(b_glu)[0], lb_logits=f(hgrn_lb_logits),
        onorm_g=f(hgrn_onorm_g)[0], w_out=f(w_out)[0], fin_g=f(final_norm_g).reshape(1, D))
    shared.update(_consts())
    in_maps = []
    for c in range(NCORES):
        m = dict(shared)
        m["xp"] = x_prompt[c]
        m["xs"] = np.ascontiguousarray(x_sample[NS * c:NS * (c + 1)].reshape(NSAMP, D))
        m["s5re0"] = np.ascontiguousarray(state_s5_re[0, NS * c:NS * (c + 1)].reshape(NS, 2048))
        m["s5im0"] = np.ascontiguousarray(state_s5_im[0, NS * c:NS * (c + 1)].reshape(NS, 2048))
        m["hg0"] = np.ascontiguousarray(state_hgrn[0, NS * c:NS * (c + 1)])
        in_maps.append(m)
    if "nc" not in _NC_CACHE:
        _NC_CACHE["nc"] = build_program()
    nc = _NC_CACHE["nc"]
    res = run_bass_kernel_spmd(nc, in_maps, core_ids=list(range(NCORES)))
    R = res.results
    y_prompt = np.stack([R[c]["yp"] for c in range(NCORES)]).astype(np.float32)
    y_sample = np.concatenate([R[c]["ys"].reshape(NS, LS, D) for c in range(NCORES)]).astype(np.float32)
    p_re = np.stack([R[c]["o_pre"].reshape(32, 64) for c in range(NCORES)])[None].astype(np.float32)
    p_im = np.stack([R[c]["o_pim"].reshape(32, 64) for c in range(NCORES)])[None].astype(np.float32)
    p_hg = np.stack([R[c]["o_phg"] for c in range(NCORES)])[None].astype(np.float32)
    s_re = np.concatenate([R[c]["o_sre"].reshape(NS, 32, 64) for c in range(NCORES)])[None].astype(np.float32)
    s_im = np.concatenate([R[c]["o_sim"].reshape(NS, 32, 64) for c in range(NCORES)])[None].astype(np.float32)
    s_hg = np.concatenate([R[c]["o_shg"] for c in range(NCORES)])[None].astype(np.float32)
    return (y_prompt, y_sample, p_re, p_im, p_hg, s_re, s_im, s_hg)
```

```python
import math
import numpy as np
import concourse.bass as bass
import concourse.mybir as mybir
from concourse.bass_utils import run_bass_kernel_spmd

F32 = mybir.dt.float32
BF16 = mybir.dt.bfloat16
I32 = mybir.dt.int32
ALU = mybir.AluOpType
AF = mybir.ActivationFunctionType

NCORES = 8
D = 1024
LP = 2048
NS = 16
LS = 4
NSAMP = NS * LS
NT = LP + NSAMP
TC = 128
EPS = 1e-6
ENGS = ("pe", "act", "dve", "pool", "sp")
N_DMA_SEMS = 20
N_SW_SEMS = 8
SB_BASE = 16512
SB_LIMIT = 229376


class Buf:
    __slots__ = ("name", "writer", "readers", "excl")

    def __init__(self, name, excl=False):
        self.name = name
        self.writer = None
        self.readers = []
        self.excl = excl


class Op:
    __slots__ = ("eng", "fn", "deps", "signals", "count", "is_dma", "dma_sem", "dma_val")

    def __init__(self, eng, fn, is_dma=False):
        self.eng = eng
        self.fn = fn
        self.deps = []
        self.signals = False
        self.count = None
        self.is_dma = is_dma
        self.dma_sem = None
        self.dma_val = None


class Prog:
    def __init__(self):
        self.ops = {e: [] for e in ENGS}
        self.n_dma = 0
        self.n_swdma = 0
        self.dma_last = [None] * (N_DMA_SEMS + N_SW_SEMS)
        self.dma_cnt = [0] * (N_DMA_SEMS + N_SW_SEMS)
        self.out_dmas = []
        self.pending_barrier = {}
        self.last_op = {e: None for e in ENGS}
        self.dmas_since_barrier = []

    def _add(self, op, reads, writes):
        ex = [b for b in reads if b.excl]
        if ex:
            reads = [b for b in reads if not b.excl]
            writes = list(writes) + [b for b in ex if b not in writes]
        deps = []
        for b in reads:
            if b.writer is not None:
                deps.append(b.writer)
        for b in writes:
            if b.writer is not None:
                deps.append(b.writer)
            deps.extend(b.readers)
        if op.eng in self.pending_barrier:
            deps.extend(self.pending_barrier.pop(op.eng))
        seen = set()
        for d in deps:
            if d is op or id(d) in seen:
                continue
            seen.add(id(d))
            op.deps.append(d)
            d.signals = True
        for b in reads:
            b.readers.append(op)
        for b in writes:
            b.writer = op
            b.readers = []
        self.ops[op.eng].append(op)
        if op.is_dma:
            self.dmas_since_barrier.append(op)
        else:
            self.last_op[op.eng] = op
        return op

    def op(self, eng, fn, reads=(), writes=()):
        return self._add(Op(eng, fn), reads, writes)

    def dma(self, out, in_, reads=(), writes=(), eng="sp", is_output=False, **kw):
        def fn(e, out=out, in_=in_, kw=kw):
            return e.dma_start(out=out, in_=in_, **kw)
        op = Op(eng, fn, is_dma=True)
        if eng == "pool":
            k = N_DMA_SEMS + (self.n_swdma % N_SW_SEMS)
            self.n_swdma += 1
        else:
            k = self.n_dma % N_DMA_SEMS
            self.n_dma += 1
        prev = self.dma_last[k]
        self.dma_cnt[k] += 1
        op.dma_sem = k
        op.dma_val = 16 * self.dma_cnt[k]
        self._add(op, reads, writes)
        if prev is not None and prev not in op.deps:
            op.deps.append(prev)
            prev.signals = True
        self.dma_last[k] = op
        if is_output:
            self.out_dmas.append(op)
            op.signals = True
        return op

    def barrier(self):
        pre = [o for o in self.last_op.values() if o is not None] + list(self.dmas_since_barrier)
        self.dmas_since_barrier = []
        for e in ENGS:
            self.pending_barrier[e] = list(self.pending_barrier.get(e, [])) + pre

    def emit(self, nc):
        import contextlib
        for e in ENGS:
            c = 0
            for op in self.ops[e]:
                if not op.is_dma and op.signals:
                    c += 1
                    op.count = c
        with contextlib.ExitStack() as st:
            esem = {e: st.enter_context(nc.semaphore("s_" + e)) for e in ENGS}
            dsem = [st.enter_context(nc.semaphore("d_%d" % i)) for i in range(N_DMA_SEMS + N_SW_SEMS)]
            block = st.enter_context(nc.Block())

            def run(e, engobj):
                waited = {}

                def wait_for(d):
                    if d.is_dma:
                        key, sem, val = ("d", d.dma_sem), dsem[d.dma_sem], d.dma_val
                    else:
                        key, sem, val = ("e", d.eng), esem[d.eng], d.count
                    if waited.get(key, 0) >= val:
                        return
                    waited[key] = val
                    engobj.wait_ge(sem, val)

                for op in self.ops[e]:
                    for d in op.deps:
                        wait_for(d)
                    ins = op.fn(engobj)
                    if op.is_dma:
                        ins.then_inc(dsem[op.dma_sem], 16)
                    elif op.signals:
                        ins.then_inc(esem[e], 1)
                if e == "sp":
                    for d in self.out_dmas:
                        wait_for(d)

            @block.tensor
            def _(eng):
                run("pe", eng)

            @block.scalar
            def _(eng):
                run("act", eng)

            @block.vector
            def _(eng):
                run("dve", eng)

            @block.gpsimd
            def _(eng):
                run("pool", eng)

            @block.sync
            def _(eng):
                run("sp", eng)


class Arena:
    def __init__(self, nc):
        self.nc = nc
        self.off = SB_BASE
        self.n = 0
        self.peak = SB_BASE

    def alloc(self, name, shape, dt):
        esz = 2 if dt == BF16 else 4
        size = esz
        for s in shape[1:]:
            size *= s
        size = (size + 31) // 32 * 32
        self.n += 1
        h = self.nc.alloc_sbuf_tensor_at("%s_%d" % (name, self.n), list(shape), dt, offset=self.off)
        self.off += size
        self.peak = max(self.peak, self.off)
        assert self.off <= SB_LIMIT, "SBUF overflow at %s: %d" % (name, self.off)
        return h

    def mark(self):
        return self.off

    def reset(self, m):
        self.off = m


def col_chunks():
    return [(i * 512, 512) for i in range(4)] + [(LP, NSAMP)]


def build_program():
    nc = bass.Bass("TRN2", target_bir_lowering=False)

    def din(name, shape, dt=F32):
        return nc.dram_tensor(name, list(shape), dt, kind="ExternalInput").ap()

    def dout(name, shape):
        return nc.dram_tensor(name, list(shape), F32, kind="ExternalOutput").ap()

    xp = din("xp", [LP, D])
    xs = din("xs", [NSAMP, D])
    s5re0 = din("s5re0", [NS, 2048])
    s5im0 = din("s5im0", [NS, 2048])
    hg0 = din("hg0", [NS, 4, 128, 128])
    norm_g = din("norm_g", [D])
    w_in = din("w_in", [D, 3072])
    lam_re = din("lam_re", [32, 64])
    lam_im = din("lam_im", [32, 64])
    log_step = din("log_step", [32])
    b_re = din("b_re", [32, 64, 16])
    b_im = din("b_im", [32, 64, 16])
    c_re = din("c_re", [32, 16, 64])
    c_im = din("c_im", [32, 16, 64])
    s5_d = din("s5_d", [512])
    w_glu = din("w_glu", [512, 512])
    b_glu = din("b_glu", [512])
    lb_logits = din("lb_logits", [2, 512])
    onorm_g = din("onorm_g", [512])
    w_out = din("w_out", [D, D])
    fin_g = din("fin_g", [1, D])
    c_ident = din("c_ident", [128, 128])
    c_mask2 = din("c_mask2", [128, 128])
    c_masks = din("c_masks", [64, 64])
    c_seg = din("c_seg", [1, 512 + NSAMP])
    c_rowm = din("c_rowm", [64, NS])
    c_glm = din("c_glm", [128, 2])
    c_rm4 = din("c_rm4", [128, 4])

    yp = dout("yp", [LP, D])
    ys = dout("ys", [NSAMP, D])
    o_pre = dout("o_pre", [16, 128])
    o_pim = dout("o_pim", [16, 128])
    o_phg = dout("o_phg", [4, 128, 128])
    o_sre = dout("o_sre", [NS, 2048])
    o_sim = dout("o_sim", [NS, 2048])
    o_shg = dout("o_shg", [NS, 4, 128, 128])

    P = Prog()
    A = Arena(nc)

    def sb(name, shape, dt=F32):
        return A.alloc(name, shape, dt)

    def ACT(reads, writes, out, in_, func, scale=1.0, bias=0.0, accum_out=None):
        def fn(e):
            if accum_out is not None:
                return e.activation(out=out, in_=in_, func=func, scale=scale, bias=bias, accum_out=accum_out)
            return e.activation(out=out, in_=in_, func=func, scale=scale, bias=bias)
        return P.op("act", fn, reads, writes)

    def TT(eng, reads, writes, out, in0, in1, op):
        return P.op(eng, lambda e: e.tensor_tensor(out=out, in0=in0, in1=in1, op=op), reads, writes)

    def TS(eng, reads, writes, out, in0, s1, s2, op0, op1=None):
        if op1 is None:
            return P.op(eng, lambda e: e.tensor_scalar(out=out, in0=in0, scalar1=s1, scalar2=None, op0=op0), reads, writes)
        return P.op(eng, lambda e: e.tensor_scalar(out=out, in0=in0, scalar1=s1, scalar2=s2, op0=op0, op1=op1), reads, writes)

    def STT(reads, writes, out, in0, scalar, in1, op0, op1):
        return P.op("dve", lambda e: e.scalar_tensor_tensor(out=out, in0=in0, scalar=scalar, in1=in1, op0=op0, op1=op1), reads, writes)

    def CP(eng, reads, writes, out, in_):
        if eng == "act":
            return ACT(reads, writes, out, in_, AF.Copy)
        return P.op(eng, lambda e: e.tensor_copy(out=out, in_=in_), reads, writes)

    def MEMSET(eng, writes, ap, val):
        return P.op(eng, lambda e: e.memset(ap, val), (), writes)

    def RECIP(reads, writes, out, in_):
        return P.op("dve", lambda e: e.reciprocal(out=out, in_=in_), reads, writes)

    psT = nc.alloc_psum_tensor("psT", [128, 1024], BF16)
    psI = [nc.alloc_psum_tensor("psI%d" % i, [128, 512], F32) for i in range(2)]
    psE = nc.alloc_psum_tensor("psE", [128, 2048], F32)
    psBU = [psE[:, 0:1024], psE[:, 1024:2048]]
    psX = nc.alloc_psum_tensor("psX", [128, 512], F32)
    bpsT = [Buf("psT", True)]
    bpsI = [Buf("psI%d" % i, True) for i in range(2)]
    bpsBU = [Buf("psBU%d" % i, True) for i in range(2)]
    b_psX = Buf("psX", True)

    ident = sb("ident", [128, 128]); b_ident = Buf("ident")
    identb = sb("identb", [128, 128], BF16); b_identb = Buf("identb")
    onesb = sb("onesb", [128, 128], BF16); b_onesb = Buf("onesb")
    xT = sb("xT", [128, 8, NT], BF16); b_xT = [Buf("xT%d" % i) for i in range(17)]
    uT = sb("uT", [128, 4, NT], BF16); b_uT = [Buf("uT%d" % i) for i in range(4)]
    b_gT = Buf("gT")
    b_mixH = [Buf("mixH%d" % i) for i in range(4)]
    b_wo = Buf("wo")
    wg = sb("wg", [128, 4, 512], BF16); b_wg = Buf("wg")
    wst = [sb("wst%d" % i, [128, 8, 128]) for i in range(2)]; b_wst = [Buf("wst%d" % i) for i in range(2)]
    wbf = [sb("wbf%d" % i, [128, 8, 128], BF16) for i in range(2)]; b_wbf = [Buf("wbf%d" % i) for i in range(2)]
    gcol = sb("gcol", [128, 8]); b_gcol = Buf("gcol")
    fgb = sb("fgb", [128, D]); b_fgb = Buf("fgb")
    segm = sb("segm", [128, 512 + NSAMP]); b_segm = Buf("segm")
    mask2 = sb("mask2", [128, 128]); b_mask2 = Buf("mask2")
    masks = sb("masks", [64, 64]); b_masks = Buf("masks")
    rowm = sb("rowm", [64, NS]); b_rowm = Buf("rowm")
    glm = sb("glm", [128, 2]); b_glm = Buf("glm")
    rm4 = sb("rm4", [128, 4]); b_rm4 = Buf("rm4")
    dcol = sb("dcol", [128, 4]); b_dcol = Buf("dcol")
    nbg = sb("nbg", [128, 4]); b_nbg = Buf("nbg")
    pbg = sb("pbg", [128, 4])
    lbc = sb("lbc", [128, 2, 4]); b_lbc = Buf("lbc")
    nom = sb("nom", [128, 4]); b_nom = Buf("nom")
    ogc = sb("ogc", [128, 4]); b_ogc = Buf("ogc")
    xin = [sb("xin%d" % i, [128, D]) for i in range(2)]; b_xin = [Buf("xin%d" % i) for i in range(2)]
    xnb = [sb("xnb%d" % i, [128, D], BF16) for i in range(2)]; b_xnb = [Buf("xnb%d" % i) for i in range(2)]
    junk = sb("junk", [128, D], BF16); b_junk = Buf("junk")
    stat = [sb("stat%d" % i, [128, 4]) for i in range(2)]; b_stat = [Buf("stat%d" % i) for i in range(2)]

    P.dma(ident[:], c_ident[:, :], writes=[b_ident])
    CP("dve", [b_ident], [b_identb], identb[:], ident[:])
    MEMSET("pool", [b_onesb], onesb[:], 1.0)
    P.dma(gcol[:], norm_g.rearrange("(k p) -> p k", p=128), writes=[b_gcol], allow_slow_non_contiguous=True)
    P.dma(fgb[:], fin_g[0:1, :].partition_broadcast(128), writes=[b_fgb])
    P.dma(segm[:], c_seg[0:1, :].partition_broadcast(128), writes=[b_segm])
    P.dma(mask2[:], c_mask2[:, :], writes=[b_mask2])
    P.dma(masks[:], c_masks[:, :], writes=[b_masks])
    P.dma(rowm[:], c_rowm[:, :], writes=[b_rowm])
    P.dma(glm[:], c_glm[:, :], writes=[b_glm])
    P.dma(rm4[:], c_rm4[:, :], writes=[b_rm4])
    P.dma(dcol[:], s5_d.rearrange("(t p) -> p t", p=128), writes=[b_dcol], allow_slow_non_contiguous=True)
    P.dma(nbg[:], b_glu.rearrange("(t p) -> p t", p=128), writes=[b_nbg], allow_slow_non_contiguous=True)
    CP("dve", [b_nbg], [b_nbg], pbg[:], nbg[:])
    TS("dve", [b_nbg], [b_nbg], nbg[:], nbg[:], -1.0, None, ALU.mult)
    P.dma(ogc[:], onorm_g.rearrange("(t p) -> p t", p=128), writes=[b_ogc], allow_slow_non_contiguous=True)
    P.dma(lbc[:], lb_logits.rearrange("r (t p) -> p r t", p=128), writes=[b_lbc], allow_slow_non_contiguous=True)
    TT("dve", [b_lbc], [b_nom], nom[:], lbc[:, 1, :], lbc[:, 0, :], ALU.subtract)
    ACT([b_nom], [b_nom], nom[:], nom[:], AF.Exp)
    TS("dve", [b_nom], [b_nom], nom[:], nom[:], 1.0, None, ALU.add)
    RECIP([b_nom], [b_lbc], lbc[:, 0, :], nom[:])
    TS("dve", [b_lbc], [b_lbc], lbc[:, 1, :], lbc[:, 0, :], -1.0, 1.0, ALU.mult, ALU.add)
    TS("dve", [b_lbc], [b_nom], nom[:], lbc[:, 1, :], -1.0, None, ALU.mult)

    b_ea = Buf("ea")
    b_eb2 = Buf("eb2")
    res = [sb("res%d" % i, [128, D]) for i in range(2)]; b_res = [Buf("res%d" % i) for i in range(2)]
    m_s5p = A.mark()

    def phase_a():
        for tt in range(17):
            rows = 128 if tt < 16 else NSAMP
            src = xp[tt * 128:(tt + 1) * 128, :] if tt < 16 else xs[:, :]
            s = tt % 2
            P.dma(xin[s][0:rows, :], src, writes=[b_xin[s]])
            ACT([b_xin[s]], [b_junk, b_stat[s]], junk[0:rows, :], xin[s][0:rows, :], AF.Square, accum_out=stat[s][0:rows, 0:1])
            ACT([b_stat[s]], [b_stat[s]], stat[s][0:rows, 1:2], stat[s][0:rows, 0:1], AF.Ln, scale=1.0 / D, bias=EPS)
            ACT([b_stat[s]], [b_stat[s]], stat[s][0:rows, 2:3], stat[s][0:rows, 1:2], AF.Exp, scale=-0.5)
            ACT([b_xin[s], b_stat[s]], [b_xnb[s]], xnb[s][0:rows, :], xin[s][0:rows, :], AF.Copy, scale=stat[s][0:rows, 2:3])

            def tr(e, s=s, rows=rows):
                ins = None
                for kd in range(8):
                    ins = e.transpose(out=psT[:, kd * 128:kd * 128 + rows], in_=xnb[s][0:rows, kd * 128:(kd + 1) * 128],
                                      identity=identb[0:rows, 0:rows])
                return ins
            P.op("pe", tr, [b_xnb[s], b_identb], bpsT)
            c0 = tt * 128
            TT("dve", bpsT + [b_gcol], [b_xT[tt]], xT[:, :, c0:c0 + rows],
               psT[:, :].rearrange("p (k c) -> p k c", k=8)[:, :, 0:rows],
               gcol[:, :].unsqueeze(2).to_broadcast([128, 8, rows]), ALU.mult)
            yield

    wcount = [0]
    wcache = {}

    def _load_w(col0):
        s = wcount[0] % 2
        wcount[0] += 1
        P.dma(wst[s][:], w_in[:, col0:col0 + 128].rearrange("(k p) c -> p k c", p=128), writes=[b_wst[s]])
        CP("act", [b_wst[s]], [b_wbf[s]], wbf[s][:], wst[s][:])
        return wbf[s], b_wbf[s]

    def load_w_tile(col0, nxt=None):
        if col0 in wcache:
            r = wcache.pop(col0)
        else:
            r = _load_w(col0)
        if nxt is not None and nxt not in wcache:
            wcache[nxt] = _load_w(nxt)
        return r

    icount = [0]
    ibanks2 = [(psI[0], bpsI[0]), (psI[1], bpsI[1])]
    ibanks4 = ibanks2 + [(psX, b_psX), (psT[:, :].bitcast(F32), bpsT[0])]
    ibank_sel = [ibanks2]

    def inproj_chunk(wt, bw, c0, n):
        banks = ibank_sel[0]
        s = icount[0] % len(banks)
        icount[0] += 1
        pst_, bst_ = banks[s]
        tts = sorted(set([c0 // 128 + i for i in range((n + 127) // 128)]))

        def mm(e):
            ins = None
            for kd in range(8):
                ins = e.matmul(pst_[:, 0:n], lhsT=wt[:, kd, :], rhs=xT[:, kd, c0:c0 + n], start=(kd == 0), stop=(kd == 7))
            return ins
        P.op("pe", mm, [bw] + [b_xT[t] for t in tts], [bst_])
        return pst_[:, 0:n], bst_

    gT = sb("gT", [128, 4, NT], BF16)
    prm = sb("prm", [128, 7, 16]); b_prm = Buf("prm")
    prm8 = sb("prm8", [128, 2, 3, 16]); b_prm8 = Buf("prm8")
    Apw = sb("Apw", [128, 3, 9, 16]); b_Apw = Buf("Apw")
    PK = sb("PK", [128, 4, 2, 8, 128], BF16); b_PK = Buf("PK")
    CWt = sb("CWt", [128, 16, 8, 2, 32], BF16); b_CWt = Buf("CWt")
    KT = sb("KT", [128, 4, 8, 128], BF16); b_KT = Buf("KT")
    TCB = 64
    Ut = sb("Ut", [128, 2, 16, TCB]); b_Ut = Buf("Ut")
    pw = sb("pw", [128, 2, 2, 16]); b_pw = Buf("pw")
    h0s = sb("h0s", [128, 2, 16, NS]); b_h0s = Buf("h0s")
    carry = sb("carry", [128, 2, 16]); b_carry = [Buf("carry%d" % i) for i in range(4)]
    hS = sb("hS", [128, 2, 16, NS]); b_hS = Buf("hS")
    m_loop = A.mark()

    def s5_prep():
        rl = sb("rl", [16, 16, 128]); b_rl = Buf("rl")
        lsr = sb("lsr", [16, 2]); b_lsr = Buf("lsr")
        rli = sb("rli", [16, 128], I32); b_rli = Buf("rli")
        R = lambda k: rl[:, k, :]
        R3 = lambda k: rl[:, k, :].rearrange("j (gl p) -> j gl p", gl=2)
        P.dma(R(0), lam_re.rearrange("(j gl) p -> j (gl p)", gl=2), writes=[b_rl])
        P.dma(R(1), lam_im.rearrange("(j gl) p -> j (gl p)", gl=2), writes=[b_rl])
        P.dma(lsr[:], log_step.rearrange("(j gl) -> j gl", gl=2), writes=[b_lsr])
        ACT([b_lsr], [b_lsr], lsr[:], lsr[:], AF.Exp)
        dtb = lsr[:, :].unsqueeze(2).to_broadcast([16, 2, 64])
        rr, rw = [b_rl], [b_rl]
        TS("dve", rr, rw, R(0), R(0), -1e-4, None, ALU.min)
        TT("dve", rr + [b_lsr], rw, R3(2), R3(0), dtb, ALU.mult)
        TT("dve", rr + [b_lsr], rw, R3(3), R3(1), dtb, ALU.mult)
        ACT(rr, rw, R(4), R(2), AF.Exp)
        TS("dve", rr, rw, R(3), R(3), 1.0 / (2.0 * math.pi), None, ALU.mult)

        def wrap(dst, src, add):
            TS("dve", rr, rw, dst, src, add, None, ALU.add)
            CP("dve", rr, [b_rli], rli[:], dst)
            CP("dve", [b_rli], rw, R(11), rli[:])
            TT("dve", rr, rw, dst, dst, R(11), ALU.subtract)
            TS("dve", rr, rw, R(11), dst, 0.5, None, ALU.is_gt)
            TT("dve", rr, rw, dst, dst, R(11), ALU.subtract)
            TS("dve", rr, rw, R(11), dst, -0.5, None, ALU.is_lt)
            TT("dve", rr, rw, dst, dst, R(11), ALU.add)
        wrap(R(5), R(3), 0.0)
        wrap(R(6), R(3), 0.25)
        ACT(rr, rw, R(5), R(5), AF.Sin, scale=6.28318)
        ACT(rr, rw, R(6), R(6), AF.Sin, scale=6.28318)
        TT("dve", rr, rw, R(7), R(4), R(6), ALU.mult)
        TT("dve", rr, rw, R(8), R(4), R(5), ALU.mult)
        TT("dve", rr, rw, R(12), R(0), R(0), ALU.mult)
        TT("dve", rr, rw, R(13), R(1), R(1), ALU.mult)
        TT("dve", rr, rw, R(12), R(12), R(13), ALU.add)
        RECIP(rr, rw, R(12), R(12))
        TS("dve", rr, rw, R(13), R(7), -1.0, None, ALU.add)
        TT("dve", rr, rw, R(14), R(13), R(0), ALU.mult)
        TT("dve", rr, rw, R(15), R(8), R(1), ALU.mult)
        TT("dve", rr, rw, R(14), R(14), R(15), ALU.add)
        TT("dve", rr, rw, R(9), R(14), R(12), ALU.mult)
        TT("dve", rr, rw, R(14), R(8), R(0), ALU.mult)
        TT("dve", rr, rw, R(15), R(13), R(1), ALU.mult)
        TT("dve", rr, rw, R(14), R(14), R(15), ALU.subtract)
        TT("dve", rr, rw, R(10), R(14), R(12), ALU.mult)

        order = [7, 8, 9, 10, 6, 5, 4]

        def trp(e):
            ins = None
            for k, slot in enumerate(order):
                ins = e.transpose(out=psI[0][:, k * 16:(k + 1) * 16], in_=R(slot), identity=ident[0:16, 0:16])
            return ins
        P.op("pe", trp, rr + [b_ident], [bpsI[0]])
        CP("dve", [bpsI[0]], [b_prm], prm[:, :, :], psI[0][:, 0:112].rearrange("p (k j) -> p k j", k=7))

        Bs = sb("Bs", [128, 2, 16, 16]); b_Bs = Buf("Bs")
        bb = sb("bb", [128, 2, 16, 16]); b_bb = Buf("bb")
        T12 = sb("T12", [128, 2, 16, 32]); b_T12 = Buf("T12")
        b_Bs1 = Buf("Bs1")
        P.dma(Bs[:, 0, :, :], b_re.rearrange("(j gl) p c -> (gl p) j c", gl=2), writes=[b_Bs])
        P.dma(Bs[:, 1, :, :], b_im.rearrange("(j gl) p c -> (gl p) j c", gl=2), writes=[b_Bs1])
        zrb = prm[:, 2, :].unsqueeze(2).to_broadcast([128, 16, 16])
        zib = prm[:, 3, :].unsqueeze(2).to_broadcast([128, 16, 16])
        t1 = T12[:, 0, :, 0:16]
        t2 = T12[:, 1, :, 0:16]
        TT("dve", [b_Bs, b_Bs1, b_prm], [b_T12], t1, Bs[:, 0, :, :], zrb, ALU.mult)
        TT("dve", [b_Bs, b_prm, b_T12], [b_T12], t2, Bs[:, 1, :, :], zib, ALU.mult)
        TT("dve", [b_T12], [b_bb], bb[:, 0, :, :], t1, t2, ALU.subtract)
        TT("dve", [b_Bs, b_prm, b_bb], [b_T12], t1, Bs[:, 1, :, :], zrb, ALU.mult)
        TT("dve", [b_Bs, b_prm, b_T12], [b_T12], t2, Bs[:, 0, :, :], zib, ALU.mult)
        TT("dve", [b_T12, b_bb], [b_bb], bb[:, 1, :, :], t1, t2, ALU.add)

        yield
        io = [b_Apw, b_prm, b_T12]
        q1 = T12[:, 0, :, 16]
        q2 = T12[:, 1, :, 16]
        MEMSET("dve", [b_Apw], Apw[:, 0, 0, :], 1.0)
        MEMSET("dve", [b_Apw], Apw[:, 1:3, 0, :], 0.0)
        CP("dve", io, [b_Apw], Apw[:, 0:2, 1, :], prm[:, 0:2, :])
        TS("dve", io, [b_Apw], Apw[:, 2, 1, :], prm[:, 1, :], -1.0, None, ALU.mult)
        for k in range(2, 9):
            TT("dve", io, [b_T12], q1, Apw[:, 0, k - 1, :], prm[:, 0, :], ALU.mult)
            TT("dve", io, [b_T12], q2, Apw[:, 1, k - 1, :], prm[:, 1, :], ALU.mult)
            TT("dve", io, [b_Apw], Apw[:, 0, k, :], q1, q2, ALU.subtract)
            TT("dve", io, [b_T12], q1, Apw[:, 0, k - 1, :], prm[:, 1, :], ALU.mult)
            TT("dve", io, [b_T12], q2, Apw[:, 1, k - 1, :], prm[:, 0, :], ALU.mult)
            TT("dve", io, [b_Apw], Apw[:, 1, k, :], q1, q2, ALU.add)
            TS("dve", io, [b_Apw], Apw[:, 2, k, :], Apw[:, 1, k, :], -1.0, None, ALU.mult)
        yield
        io8 = [b_prm8, b_prm, b_T12]
        CP("dve", io8, [b_prm8], prm8[:, 1, :, :], prm[:, 4:7, :])
        cur = 1
        for _ in range(3):
            nx = 1 - cur
            c_, s_, r_ = prm8[:, cur, 0, :], prm8[:, cur, 1, :], prm8[:, cur, 2, :]
            TT("dve", io8, [b_T12], q1, c_, c_, ALU.mult)
            TT("dve", io8, [b_T12], q2, s_, s_, ALU.mult)
            TT("dve", io8, [b_prm8], prm8[:, nx, 0, :], q1, q2, ALU.subtract)
            STT(io8, [b_prm8], prm8[:, nx, 1, :], c_, 2.0, s_, ALU.mult, ALU.mult)
            TT("dve", io8, [b_prm8], prm8[:, nx, 2, :], r_, r_, ALU.mult)
            cur = nx
        assert cur == 0

        yield
        Cn = sb("Cn", [128, 2, 2, 128]); b_Cn = Buf("Cn")
        b_Cnl = []
        for x, csrc in enumerate((c_re, c_im)):
            for j in range(16):
                b_Cnl.append(Buf("Cn%d_%d" % (x, j)))
                P.dma(Cn[16 * (j % 8):16 * (j % 8) + 16, x, j // 8, :].rearrange("c (gl p) -> c gl p", gl=2),
                      csrc[2 * j:2 * j + 2, :, :].rearrange("gl c p -> c gl p"), writes=[b_Cnl[-1]])

        def trc(e):
            ins = None
            for x in range(2):
                for jj in range(2):
                    sl = (x * 2 + jj) * 128
                    ins = e.transpose(out=psI[1][:, sl:sl + 128], in_=Cn[:, x, jj, :], identity=ident[:, :])
            return ins
        P.op("pe", trc, b_Cnl + [b_ident], [bpsI[1]])
        Csl = sb("Csl", [128, 3, 16, 16]); b_Csl = Buf("Csl")
        for x in range(2):
            for jj in range(2):
                sl = (x * 2 + jj) * 128
                CP("dve", [bpsI[1], b_Csl], [b_Csl], Csl[:, x, 8 * jj:8 * jj + 8, :],
                   psI[1][:, sl:sl + 128].rearrange("p (j c) -> p j c", j=8))
        TS("dve", [b_Csl], [b_Csl], Csl[:, 2, :, :], Csl[:, 1, :, :], -1.0, None, ALU.mult)
        yield
        Czp = sb("Czp", [128, 16, 2, 128], BF16); b_Czp = Buf("Czp")
        MEMSET("pool", [b_Czp], Czp[:], 0.0)
        glm4t = glm[:, :].unsqueeze(1).unsqueeze(3).to_broadcast([128, 4, 2, 16])
        glm4 = glm[:, :].unsqueeze(1).unsqueeze(3).to_broadcast([128, 16, 2, 16])
        for jm in range(4):
            for x in range(2):
                TT("pool", [b_Csl, b_glm, b_Czp], [b_Czp],
                   Czp[:, jm::4, x, 32 * jm:32 * jm + 32].rearrange("p t (g c) -> p t g c", g=2),
                   Csl[:, (0 if x == 0 else 2), jm::4, :].unsqueeze(2).to_broadcast([128, 4, 2, 16]), glm4t, ALU.mult)
        yield
        Pq = sb("Pq", [128, 2, 16, 16]); b_Pq = Buf("Pq")
        T12p = sb("T12p", [128, 2, 16, 16]); b_T12p = Buf("T12p")
        u1 = T12p[:, 0, :, :]
        u2 = T12p[:, 1, :, :]
        for tau in range(8):
            yield
            k = tau + 1
            Ar = Apw[:, 0, k, :].unsqueeze(2).to_broadcast([128, 16, 16])
            Ai = Apw[:, 1, k, :].unsqueeze(2).to_broadcast([128, 16, 16])
            nAi = Apw[:, 2, k, :].unsqueeze(2).to_broadcast([128, 16, 16])
            ioc = [b_Csl, b_Apw, b_T12p, b_Pq]
            TT("pool", ioc, [b_T12p], u1, Csl[:, 0, :, :], Ar, ALU.mult)
            TT("pool", ioc, [b_T12p], u2, Csl[:, 1, :, :], Ai, ALU.mult)
            TT("pool", ioc, [b_Pq], Pq[:, 0, :, :], u1, u2, ALU.subtract)
            TT("pool", ioc, [b_T12p], u1, Csl[:, 0, :, :], nAi, ALU.mult)
            TT("pool", ioc, [b_T12p], u2, Csl[:, 1, :, :], Ar, ALU.mult)
            TT("pool", ioc, [b_Pq], Pq[:, 1, :, :], u1, u2, ALU.subtract)
            for x in range(2):
                TT("pool", [b_Pq, b_glm, b_CWt], [b_CWt], CWt[:, :, tau, x, :].rearrange("p j (g c) -> p j g c", g=2),
                   Pq[:, x, :, :].unsqueeze(2).to_broadcast([128, 16, 2, 16]), glm4, ALU.mult)

        yield
        Xx = sb("Xx", [128, 2, 16, 16]); b_Xx = Buf("Xx")
        XKb = [sb("XKb", [128, 2, 16, 2, 16], BF16) for _ in range(2)]; b_XKb = [Buf("XKb") for _ in range(2)]
        for k in range(8):
            yield
            i = k % 2
            if k == 0:
                Xsrc, b_Xsrc = bb, b_bb
            else:
                Ar = Apw[:, 0, k, :].unsqueeze(2).to_broadcast([128, 16, 16])
                Ai = Apw[:, 1, k, :].unsqueeze(2).to_broadcast([128, 16, 16])
                iox = [b_bb, b_Apw, b_T12, b_Xx]
                TT("dve", iox, [b_T12], t1, bb[:, 0, :, :], Ar, ALU.mult)
                TT("dve", iox, [b_T12], t2, bb[:, 1, :, :], Ai, ALU.mult)
                TT("dve", iox, [b_Xx], Xx[:, 0, :, :], t1, t2, ALU.subtract)
                TT("dve", iox, [b_T12], t1, bb[:, 0, :, :], Ai, ALU.mult)
                TT("dve", iox, [b_T12], t2, bb[:, 1, :, :], Ar, ALU.mult)
                TT("dve", iox, [b_Xx], Xx[:, 1, :, :], t1, t2, ALU.add)
                Xsrc, b_Xsrc = Xx, b_Xx
            for x in range(2):
                TT("dve", [b_Xsrc, b_glm, b_XKb[i]], [b_XKb[i]], XKb[i][:, x, :, :, :],
                   Xsrc[:, x, :, :].unsqueeze(2).to_broadcast([128, 16, 2, 16]), glm4, ALU.mult)

            def trk(e, i=i):
                ins = None
                for jt in range(4):
                    for x in range(2):
                        sl = (jt * 2 + x) * 128
                        ins = e.transpose(out=psT[:, sl:sl + 128],
                                          in_=XKb[i][:, x, 4 * jt:4 * jt + 4, :, :].rearrange("p j g c -> p (j g c)"),
                                          identity=identb[:, :])
                return ins
            P.op("pe", trk, [b_XKb[i], b_identb], bpsT)
            CP("act", bpsT, [b_PK], PK[:, :, :, k, :].rearrange("p t x c -> p (t x) c"),
               psT[:, :].rearrange("p (s c) -> p s c", s=8))
            kb = k % 2

            def mk(e, i=i, kb=kb):
                ins = None
                for jt in range(4):
                    for jm in range(4):
                        j = 4 * jt + jm
                        for x in range(2):
                            ins = e.matmul(psI[kb][32 * jm:32 * jm + 32, jt * 128:(jt + 1) * 128],
                                           lhsT=XKb[i][:, x, j, :, :].rearrange("p g c -> p (g c)"), rhs=Czp[:, j, x, :],
                                           start=(x == 0), stop=(x == 1), tile_position=(0, 32 * jm))
                return ins
            P.op("pe", mk, [b_XKb[i], b_Czp], [bpsI[kb]])
            if k == 0:
                for jt in range(4):
                    STT([b_ident, b_dcol], [b_KT, bpsI[kb]], KT[:, jt, 0, :], ident[:, :], dcol[:, jt:jt + 1],
                        psI[kb][:, jt * 128:(jt + 1) * 128], ALU.mult, ALU.add)
            else:
                CP("act", [bpsI[kb]], [b_KT], KT[:, :, k, :], psI[kb][:, :].rearrange("p (t c) -> p t c", t=4))

        yield
        h0v = rl[:, :, :].rearrange("s a b -> s (a b)")
        for x in range(2):
            P.dma(h0v, (s5re0 if x == 0 else s5im0)[:, :], writes=[b_rl])

            def trh(e, x=x):
                ins = None
                for j in range(16):
                    ins = e.transpose(out=psBU[x][:, j * 16:(j + 1) * 16], in_=h0v[:, 128 * j:128 * j + 128],
                                      identity=ident[0:16, 0:16])
                return ins
            P.op("pe", trh, [b_rl, b_ident], [bpsBU[x]])
            CP("act", [bpsBU[x]], [b_h0s], h0s[:, x, :, :], psBU[x][:, 0:256].rearrange("p (j s) -> p j s", j=16))

        yield
        CP("pool", [b_prm8], [b_Ut], Ut[:, 0, :, 0], prm8[:, 0, 0, :])
        CP("pool", [b_prm8, b_Ut], [b_Ut], Ut[:, 1, :, 0], prm8[:, 0, 1, :])
        CP("pool", [b_prm8], [b_pw], pw[:, 0, :, :], prm8[:, 0, 0:2, :])
        n = 1
        cur = 0
        ta = T12p[:, :, :, :].rearrange("p x j c -> p (x j c)").rearrange("p (j n) -> p j n", n=32)
        tb = Pq[:, :, :, :].rearrange("p x j c -> p (x j c)").rearrange("p (j n) -> p j n", n=32)
        while n < TCB:
            yield
            cn = pw[:, cur, 0, :].unsqueeze(2).to_broadcast([128, 16, n])
            sn = pw[:, cur, 1, :].unsqueeze(2).to_broadcast([128, 16, n])
            ur = Ut[:, 0, :, 0:n]
            ui = Ut[:, 1, :, 0:n]
            io = [b_Ut, b_pw, b_T12p, b_Pq]
            TT("pool", io, [b_T12p], ta[:, :, 0:n], ur, cn, ALU.mult)
            TT("pool", io, [b_T12p], tb[:, :, 0:n], ui, sn, ALU.mult)
            TT("pool", io, [b_Ut], Ut[:, 0, :, n:2 * n], ta[:, :, 0:n], tb[:, :, 0:n], ALU.subtract)
            TT("pool", io, [b_T12p], ta[:, :, 0:n], ur, sn, ALU.mult)
            TT("pool", io, [b_T12p], tb[:, :, 0:n], ui, cn, ALU.mult)
            TT("pool", io, [b_Ut], Ut[:, 1, :, n:2 * n], ta[:, :, 0:n], tb[:, :, 0:n], ALU.add)
            if 2 * n < TCB:
                c_ = pw[:, cur, 0, :]
                s_ = pw[:, cur, 1, :]
                nx = 1 - cur
                TT("pool", io, [b_T12p], ta[:, :, 0], c_, c_, ALU.mult)
                TT("pool", io, [b_T12p], tb[:, :, 0], s_, s_, ALU.mult)
                TT("pool", io, [b_pw], pw[:, nx, 0, :], ta[:, :, 0], tb[:, :, 0], ALU.subtract)
                TT("pool", io, [b_T12p], tb[:, :, 0], c_, s_, ALU.mult)
                TS("pool", io, [b_pw], pw[:, nx, 1, :], tb[:, :, 0], 2.0, 1.0, ALU.mult, ALU.mult)
                cur = nx
            n *= 2

    def s5_loop():
        NB = LP // 8
        NCH = NB // TCB
        um = [sb("um", [128, NT], BF16) for _ in range(2)]; b_um = [Buf("um") for _ in range(2)]
        Hp = [sb("Hp", [128, 4, 2, NB + 8], BF16) for _ in range(2)]; b_Hp = [Buf("Hp") for _ in range(2)]
        HpS = [sb("HpS", [128, 4, 2, NS], BF16) for _ in range(2)]; b_HpS = [Buf("HpS") for _ in range(2)]
        tm = sb("tm", [128, 4, 4, TCB]); b_tm = Buf("tm")
        gin = sb("gin", [128, NCH, 2, 4, TCB]); b_gin = Buf("gin")
        gflat = gin[:, :, :, :, :].rearrange("p h x j c -> p (h x j c)")
        glu_tmp2.append((gflat[:, 0:512], gflat[:, 1024:1536], b_gin))
        r8m = xnb[0][:, :].bitcast(F32).rearrange("p (x j c) -> p x j c", x=2, j=4); b_r8m = Buf("r8m")
        cinj = sb("cinj", [128, 2, 4]); b_cinj = Buf("cinj")
        G2 = [sb("G", [128, 2, 4, TCB]) for _ in range(2)]; b_G2 = [Buf("G") for _ in range(2)]
        dmd, b_dmd = tm, b_tm
        gcnt = [0]
        lc = sb("lc", [128, 4, 4]); b_lc = Buf("lc")
        ls_ = sb("ls_", [128, 4, 4, NS]); b_ls = Buf("ls")
        bpsE = bpsBU
        Ev = psE[:, :].rearrange("p (j x c) -> p j x c", j=4, x=2)
        Es = psX[:, 0:8 * NS].rearrange("p (j x s) -> p j x s", j=4, x=2)
        for p_ in range(2):
            MEMSET("pool", [b_Hp[p_]], Hp[p_][:, :, :, 0:1], 0.0)
        ycount = [0]

        def stage_e(jt):
            for jm in range(4):
                j = 4 * jt + jm
                i = j % 2
                ACT([b_uT[jt], b_rm4], [b_um[i]], um[i][:, :], uT[:, jt, :], AF.Copy, scale=rm4[:, jm:jm + 1])

                def me(e, jm=jm, i=i):
                    ins = None
                    for x in range(2):
                        for sg in range(8):
                            ins = e.matmul(Ev[:, jm, x, :], lhsT=PK[:, jt, x, 7 - sg, :], rhs=um[i][:, sg * 256:(sg + 1) * 256],
                                           start=(sg == 0), stop=(sg == 7))
                    return ins
                P.op("pe", me, [b_PK, b_um[i]], bpsE)

                def mes(e, jm=jm, i=i):
                    ins = None
                    for x in range(2):
                        for sg in range(LS):
                            ins = e.matmul(Es[:, jm, x, :], lhsT=PK[:, jt, x, LS - 1 - sg, :], rhs=um[i][:, LP + sg * NS:LP + (sg + 1) * NS],
                                           start=(sg == 0), stop=(sg == LS - 1))
                    return ins
                P.op("pe", mes, [b_PK, b_um[i]], [b_psX])

        def stage_mod(jt):
            js = slice(4 * jt, 4 * jt + 4)
            Cr = Ut[:, 0, js, :].unsqueeze(2).to_broadcast([128, 4, NCH, TCB])
            Ci = Ut[:, 1, js, :].unsqueeze(2).to_broadcast([128, 4, NCH, TCB])
            v4 = lambda ap: ap.rearrange("p j (h c) -> p j h c", h=NCH)
            Br = v4(Ev[:, :, 0, :])
            Bi = v4(Ev[:, :, 1, :])
            gr = gin[:, :, 0, :, :].rearrange("p h j c -> p j h c")
            gi = gin[:, :, 1, :, :].rearrange("p h j c -> p j h c")
            TT("dve", [b_prm8, b_segm, b_r8m], [b_r8m], r8m,
               prm8[:, 0, 2, js].unsqueeze(1).unsqueeze(3).to_broadcast([128, 2, 4, TCB]),
               segm[:, 0:TCB].unsqueeze(1).unsqueeze(2).to_broadcast([128, 2, 4, TCB]), ALU.mult)
            tmp = tm[:, :, :, :]
            rd = bpsE + [b_Ut]
            TT("dve", rd + [b_gin], [b_gin], gr, Br, Cr, ALU.mult)
            TT("dve", rd, [b_tm], tmp, Bi, Ci, ALU.mult)
            TT("dve", [b_tm, b_gin], [b_gin], gr, gr, tmp, ALU.add)
            TT("dve", rd + [b_gin], [b_gin], gi, Bi, Cr, ALU.mult)
            TT("dve", rd + [b_tm], [b_tm], tmp, Br, Ci, ALU.mult)
            TT("dve", [b_tm, b_gin], [b_gin], gi, gi, tmp, ALU.subtract)

        def stage_sample(jt):
            js = slice(4 * jt, 4 * jt + 4)
            p_ = jt % 2
            A4r = Apw[:, 0, LS, js].unsqueeze(2).to_broadcast([128, 4, NS])
            A4i = Apw[:, 1, LS, js].unsqueeze(2).to_broadcast([128, 4, NS])
            hr, hi = h0s[:, 0, js, :], h0s[:, 1, js, :]
            io = [b_h0s, b_Apw, b_ls]
            TT("dve", io, [b_ls], ls_[:, 0, :, :], hr, A4r, ALU.mult)
            TT("dve", io, [b_ls], ls_[:, 1, :, :], hi, A4i, ALU.mult)
            TT("dve", io, [b_ls], ls_[:, 2, :, :], hr, A4i, ALU.mult)
            TT("dve", io, [b_ls], ls_[:, 3, :, :], hi, A4r, ALU.mult)
            TT("dve", io, [b_ls], ls_[:, 0, :, :], ls_[:, 0, :, :], ls_[:, 1, :, :], ALU.subtract)
            TT("dve", io, [b_ls], ls_[:, 2, :, :], ls_[:, 2, :, :], ls_[:, 3, :, :], ALU.add)
            TT("dve", [b_ls, b_hS], [b_hS, b_psX], hS[:, 0, js, :], ls_[:, 0, :, :], Es[:, :, 0, :], ALU.add)
            TT("dve", [b_ls, b_hS], [b_hS, b_psX], hS[:, 1, js, :], ls_[:, 2, :, :], Es[:, :, 1, :], ALU.add)
            for x in range(2):
                CP("act", [b_h0s, b_HpS[p_]], [b_HpS[p_]], HpS[p_][:, :, x, :], h0s[:, x, js, :])

        def stage_scan(jt, ch):
            G, b_G = G2[gcnt[0] % 2], b_G2[gcnt[0] % 2]
            gcnt[0] += 1
            js = slice(4 * jt, 4 * jt + 4)
            p_ = jt % 2
            cs = slice(ch * TCB, (ch + 1) * TCB)
            n = TCB
            gch = gin[:, ch, :, :, :]
            if ch > 0:
                TT("dve", [b_carry[jt], b_prm8], [b_cinj], cinj[:, :, :], carry[:, :, js],
                   prm8[:, 0, 2, js].unsqueeze(1).to_broadcast([128, 2, 4]), ALU.mult)
                TT("dve", [b_cinj, b_gin], [b_gin], gch[:, :, :, 0], gch[:, :, :, 0], cinj[:, :, :], ALU.add)
            P.op("dve", lambda e: e.tensor_tensor_scan(out=G[:, :, :, :].rearrange("p x j c -> p (x j c)"),
                                                        data0=r8m.rearrange("p x j c -> p (x j c)"),
                                                        data1=gch.rearrange("p x j c -> p (x j c)"),
                                                        initial=0.0, op0=ALU.mult, op1=ALU.add),
                 [b_gin, b_r8m], [b_G])
            Cr = Ut[:, 0, js, :]
            Ci = Ut[:, 1, js, :]
            Gr, Gi = G[:, 0, :, :], G[:, 1, :, :]
            Gx = G[:, :, :, :]
            Crx = Cr.unsqueeze(1).to_broadcast([128, 2, 4, TCB])
            Cix = Ci.unsqueeze(1).to_broadcast([128, 2, 4, TCB])
            TT("dve", [b_G, b_Ut], [b_dmd], dmd[:, 0:2, :, :], Gx, Crx, ALU.mult)
            TT("dve", [b_G, b_Ut], [b_dmd], dmd[:, 2:4, :, :], Gx, Cix, ALU.mult)
            D = [dmd[:, 0, :, :], dmd[:, 3, :, :], dmd[:, 2, :, :], dmd[:, 1, :, :]]
            TT("dve", [b_dmd], [b_carry[jt]], carry[:, 0, js], D[0][:, :, n - 1], D[1][:, :, n - 1], ALU.subtract)
            TT("dve", [b_dmd, b_carry[jt]], [b_carry[jt]], carry[:, 1, js], D[2][:, :, n - 1], D[3][:, :, n - 1], ALU.add)
            hs_ = slice(ch * TCB + 1, (ch + 1) * TCB + 1)
            TT("dve", [b_dmd, b_Hp[p_]], [b_Hp[p_]], Hp[p_][:, :, 0, hs_], D[0], D[1], ALU.subtract)
            TT("dve", [b_dmd, b_Hp[p_]], [b_Hp[p_]], Hp[p_][:, :, 1, hs_], D[2], D[3], ALU.add)

        ga_s = [ea, xnb[0][:, :].bitcast(F32)]; b_ga_s = [b_ea, Buf("ga2")]
        ge_s = [eb2, xnb[1][:, :].bitcast(F32)]; b_ge_s = [b_eb2, Buf("ge2")]
        gst = [junk[:, 0:512], junk[:, 512:1024]]; b_gst = [Buf("gst0"), Buf("gst1")]
        ybanks = [(psI[0][:, :], bpsI[0]), (psI[1][:, :], bpsI[1]), (psT[:, :].bitcast(F32), bpsT[0])]

        def make_y(jt, g):
            p_ = jt % 2
            i = ycount[0]
            ycount[0] += 1
            st_ = i % 2
            ybank, b_yb = ybanks[i % 3]
            sample = (g == NCH)
            if not sample:
                c0 = g * 64
                t0 = 8 * c0
                nel, nt = 512, 8
                yb = ybank[:, 0:512]
                yv = yb.rearrange("p (s c) -> p s c", s=8)
                uv = uT[:, jt, 0:LP].rearrange("p (s c) -> p s c", s=8)[:, :, c0:c0 + 64]
                gv = gT[:, jt, t0:t0 + 512].rearrange("p (c s) -> p s c", s=8)
                hv = lambda jm, x: Hp[p_][:, jm, x, c0:c0 + 64]
                bh = b_Hp[p_]
            else:
                nel, nt = NSAMP, LS
                yb = ybank[:, 0:NSAMP]
                yv = yb.rearrange("p (t s) -> p t s", t=LS)
                uv = uT[:, jt, LP:NT].rearrange("p (t s) -> p t s", t=LS)
                gv = gT[:, jt, LP:NT].rearrange("p (s t) -> p t s", t=LS)
                hv = lambda jm, x: HpS[p_][:, jm, x, :]
                bh = b_HpS[p_]
            a = ga_s[st_][:, 0:nel]
            e_ = ge_s[st_][:, 0:nel]
            b_ga, b_ge = b_ga_s[st_], b_ge_s[st_]
            gs = gst[st_][:, 0:nel]
            gsv = gs.rearrange("p (s c) -> p s c", s=nt)

            def g1():
                def my(e):
                    ins = None
                    for k in range(nt):
                        ins = e.matmul(yv[:, k:nt, :], lhsT=KT[:, jt, k, :], rhs=uv[:, 0:nt - k, :], start=(k == 0), stop=False,
                                       skip_group_check=True)
                    for jm in range(4):
                        j = 4 * jt + jm
                        for tau in range(nt):
                            for x in range(2):
                                last = (jm == 3 and tau == nt - 1 and x == 1)
                                ins = e.matmul(yv[32 * jm:32 * jm + 32, tau, :], lhsT=CWt[:, j, tau, x, :], rhs=hv(jm, x),
                                               start=False, stop=last, tile_position=(0, 32 * jm), skip_group_check=True)
                    return ins
                P.op("pe", my, [b_KT, b_uT[jt], b_CWt, bh], [b_yb])

            def g2():
                ACT([b_yb], [b_gT], gv, yv, AF.Gelu_apprx_tanh)

            def g3():
                pass

            def g4():
                pass

            def g5():
                pass

            def g6():
                pass
            return (g1, g2, g3, g4, g5, g6)

        items = []
        for jt in range(4):
            for ch in range(NCH + 1):
                items.append((jt, ch))
        ys = {}
        stage_e(0)
        n_items = len(items)
        for idx in range(n_items + 2):
            if idx < n_items:
                jt, ch = items[idx]
                if ch == 0:
                    stage_mod(jt)
                    stage_sample(jt)
                    if jt + 1 < 4:
                        stage_e(jt + 1)
                if ch < NCH:
                    stage_scan(jt, ch)
                ys[idx] = make_y(jt, ch)
                ys[idx][0]()
                ys[idx][1]()
            if 0 <= idx - 1 < n_items:
                ys[idx - 1][2]()
                ys[idx - 1][3]()
            if 0 <= idx - 2 < n_items:
                ys[idx - 2][4]()
                ys[idx - 2][5]()

        tmflat = tm[:, :, :, :].rearrange("p q j c -> p (q j c)")
        sos = [(tmflat[0:16, 0:512], b_tm),
               (G2[0][0:16, :, :, :].rearrange("p x j c -> p (x j c)"), b_G2[0]),
               (G2[1][0:16, :, :, :].rearrange("p x j c -> p (x j c)"), b_G2[1])]
        sk = [0]

        def nxt_so():
            r = sos[sk[0] % 3]
            sk[0] += 1
            return r
        for x, dst in enumerate((o_pre, o_pim)):
            so, b_so = nxt_so()
            P.op("pe", lambda e, x=x: e.transpose(out=psI[x][0:16, 0:128], in_=carry[:, x, :], identity=ident[:, :]),
                 b_carry + [b_ident], [bpsI[x]])
            CP("act", [bpsI[x]], [b_so], so[:, 0:128], psI[x][0:16, 0:128])
            P.dma(dst[:, :], so[:, 0:128], reads=[b_so], is_output=True)
        for x, dst in enumerate((o_sre, o_sim)):
            for qt in range(4):
                so, b_so = nxt_so()

                def trs(e, x=x, qt=qt):
                    ins = None
                    for jj in range(4):
                        j = 4 * qt + jj
                        ins = e.transpose(out=psI[qt % 2][0:16, jj * 128:(jj + 1) * 128], in_=hS[:, x, j, :], identity=ident[:, :])
                    return ins
                P.op("pe", trs, [b_hS, b_ident], [bpsI[qt % 2]])
                CP("act", [bpsI[qt % 2]], [b_so], so[:, :], psI[qt % 2][0:16, :])
                P.dma(dst[:, qt * 512:(qt + 1) * 512], so[:, :], reads=[b_so], is_output=True)

    def load_wg():
        for hf in range(2):
            s = wcount[0] % 2
            wcount[0] += 1
            stv = wst[s][:, :, :].rearrange("p k c -> p (k c)").rearrange("p (k c) -> p k c", k=4)
            P.dma(stv, w_glu[:, hf * 256:(hf + 1) * 256].rearrange("(k p) c -> p k c", p=128), writes=[b_wst[s]])
            CP("act", [b_wst[s]], [b_wg], wg[:, :, hf * 256:(hf + 1) * 256], stv)

    def load_wo():
        wst2 = [sb("wst2", [128, 8, 128]) for _ in range(2)]
        b_wst2 = [Buf("wst2") for _ in range(2)]
        for cb in range(8):
            s = cb % 2
            P.dma(wst2[s][:], w_out[:, cb * 128:(cb + 1) * 128].rearrange("(k p) c -> p k c", p=128), writes=[b_wst2[s]], eng="pool")
            CP("pool", [b_wst2[s]], [b_wo], wo[:, :, cb * 128:(cb + 1) * 128], wst2[s][:])

    def glu():
        ibank_sel[0] = ibanks4
        gcount = [0]
        a2, b2, b_ab2 = glu_tmp2[0]
        sets = [(ea, eb2, b_ea, b_eb2), (a2, b2, b_ab2, b_ab2)]
        for fo in range(4):
            wt, bw = load_w_tile(512 + 128 * fo, 512 + 128 * (fo + 1) if fo < 3 else 1536)
            for (c0, n) in col_chunks():
                ps1, bps1 = inproj_chunk(wt, bw, c0, n)
                k = gcount[0] % 2
                ps2 = psBU[k][:, 0:n]

                def mg(e, fo=fo, c0=c0, n=n, ps2=ps2):
                    ins = None
                    for kf in range(4):
                        ins = e.matmul(ps2, lhsT=wg[:, kf, 128 * fo:128 * fo + 128], rhs=gT[:, kf, c0:c0 + n],
                                       start=(kf == 0), stop=(kf == 3))
                    return ins
                P.op("pe", mg, [b_wg, b_gT], [bpsBU[k]])
                A_, B_, bA, bB = sets[gcount[0] % 2]
                gcount[0] += 1
                a = A_[:, 0:n]
                b = B_[:, 0:n]
                ACT([bps1], [bA], a, ps1, AF.Sigmoid)
                ACT([bpsBU[k], b_nbg], [bB], b, ps2, AF.Sigmoid, scale=1.0, bias=pbg[:, fo:fo + 1])
                TT("dve", [bA, bB], [bA], a, a, b, ALU.mult)
                TT("dve", [bps1, bA, bB], [bB], b, ps1, a, ALU.mult)
                TT("dve", [bB, b_gT, b_uT[fo]], [b_uT[fo]], uT[:, fo, c0:c0 + n], b, gT[:, fo, c0:c0 + n], ALU.mult)

        ibank_sel[0] = ibanks2

    def hgrn():
        ibank_sel[0] = ibanks4
        hs = []
        for i in range(2):
            hs.append(dict(
                ebt=sb("ebt", [128, NT]), kT=sb("kT", [128, NT], BF16), qT=sb("qT", [128, NT], BF16),
                vtk=sb("vtk", [128, 17, 128], BF16), szh=sb("szh", [128, NT], BF16),
                b_ebt=Buf("ebt"), b_kT=Buf("kT"), b_qT=Buf("qT"), b_vtk=Buf("vtk"), b_szh=Buf("szh")))
        pieces = [res[0][:, 0:512], res[0][:, 512:1024], res[1][:, 0:512], res[1][:, 512:1024],
                  xin[0][:, 0:512], xin[0][:, 512:1024], xin[1][:, 0:512], xin[1][:, 512:1024]]
        tmps = [[(pieces[4 * i + k], Buf("tmp")) for k in range(4)] for i in range(2)]
        NSB = 10
        S0f = [sb("S0f%d" % i, [128, 128]) for i in range(NSB)]; b_S0f = [Buf("S0f%d" % i) for i in range(NSB)]
        S0b = [sb("S0b%d" % i, [128, 128], BF16) for i in range(NSB)]; b_S0b = [Buf("S0b%d" % i) for i in range(NSB)]
        scount = [0]

        def mkctx(hp):
            c = {}
            def two(name, shape, dt=F32):
                c[name] = [sb(name, shape, dt) for _ in range(2)]
                c["b_" + name] = [Buf(name) for _ in range(2)]
            two("S2", [128, 128]); two("Sb", [128, 128], BF16); two("attm", [128, 128], BF16); two("kh", [128, 128], BF16)
            two("khT", [128, 128], BF16); two("o32", [128, 128]); two("osq", [128, 128], BF16); two("rs", [128, 128])
            two("hgt", [128, 128]); two("khm", [64, 128], BF16)
            if hp == 0:
                bk0, bk1, bk2 = Buf("hb0", True), Buf("hb1", True), Buf("hb2", True)
                c["psA"], c["psR"], c["b_psA"], c["b_psR"] = psE[:, 0:128], psE[:, 128:256], bk0, bk0
                c["psO"], c["b_psO"] = psE[:, 512:640], bk1
                c["psS"], c["b_psS"] = [psE[:, 1024:1152], psE[:, 1152:1280]], [bk2, bk2]
            else:
                bk3 = Buf("hb3", True)
                c["psA"], c["psR"], c["b_psA"], c["b_psR"] = psE[:, 1536:1664], psE[:, 1664:1792], bk3, bk3
                c["psO"], c["b_psO"] = psI[0][:, 0:128], bpsI[0]
                c["psS"], c["b_psS"] = [psI[1][:, 0:128], psI[1][:, 128:256]], [bpsI[1], bpsI[1]]
            return c
        ctxs = [mkctx(0), mkctx(1)]
        tcount = [0]
        ccount = [0]

        def seg_of(c0, n):
            return segm[:, 0:n] if c0 < LP else segm[:, 512:512 + n]

        def stage_a(h):
            H = hs[h % 2]
            ebt, b_ebt = H["ebt"], H["b_ebt"]
            lb_ = lbc[:, 0, h:h + 1]
            om_ = lbc[:, 1, h:h + 1]
            nom_ = nom[:, h:h + 1]
            wt, bw = load_w_tile(1536 + 128 * h, 2560 + 128 * h)
            for (c0, n) in col_chunks():
                ps, bps = inproj_chunk(wt, bw, c0, n)
                ACT([bps], [b_ebt], ebt[:, c0:c0 + n], ps, AF.Sigmoid)
                yield
            wt, bw = load_w_tile(2560 + 128 * h, 2048 + 128 * h)
            for (c0, n) in col_chunks():
                Tm = tmps[ccount[0] % 2]
                ccount[0] += 1
                (ta, b_ta) = Tm[0]
                ps, bps = inproj_chunk(wt, bw, c0, n)
                ACT([bps], [b_ta], ta[:, 0:n], ps, AF.Sigmoid)
                TT("dve", [bps, b_ta], [H["b_szh"]], H["szh"][:, c0:c0 + n], ps, ta[:, 0:n], ALU.mult)
                yield
            wt, bw = load_w_tile(2048 + 128 * h, 1024 + 128 * h)

            def v_tile(tt, wt=wt, bw=bw):
                rows = 128 if tt < 16 else NSAMP
                c0 = tt * 128
                s_ = icount[0] % 2
                icount[0] += 1

                def mv(e):
                    ins = None
                    for kd in range(8):
                        ins = e.matmul(psI[s_][0:rows, 0:128], lhsT=xT[:, kd, c0:c0 + rows], rhs=wt[:, kd, :],
                                       start=(kd == 0), stop=(kd == 7))
                    return ins
                P.op("pe", mv, [bw, b_xT[tt]], [bpsI[s_]])
                CP("act", [bpsI[s_]], [H["b_vtk"]], H["vtk"][0:rows, tt, :], psI[s_][0:rows, 0:128])
            vt = 0
            for ci, (c0, n) in enumerate(col_chunks()):
                Tm = tmps[ccount[0] % 2]
                ccount[0] += 1
                (ta, b_ta), (tb, b_tb), (tc_, b_tc), (te, b_te) = Tm
                sg = ebt[:, c0:c0 + n]
                ACT([b_ebt, b_lbc], [b_tb], tb[:, 0:n], sg, AF.Ln, scale=om_, bias=lb_)
                ACT([b_ebt, b_lbc, b_nom], [b_tc], tc_[:, 0:n], sg, AF.Identity, scale=nom_, bias=om_)
                P.op("dve", lambda e, c0=c0, n=n, te=te, tb=tb: e.tensor_tensor_scan(
                    out=te[:, 0:n], data0=seg_of(c0, n), data1=tb[:, 0:n], initial=0.0, op0=ALU.mult, op1=ALU.add),
                    [b_tb, b_segm], [b_te])
                for _ in range(4 if ci < 4 else 1):
                    v_tile(vt)
                    vt += 1
                ACT([b_te], [b_ebt], ebt[:, c0:c0 + n], te[:, 0:n], AF.Exp)
                ACT([b_te, b_tc], [b_ta], ta[:, 0:n], te[:, 0:n], AF.Exp, scale=-1.0)
                TT("dve", [b_tc, b_ta], [H["b_kT"]], H["kT"][:, c0:c0 + n], tc_[:, 0:n], ta[:, 0:n], ALU.mult)
                yield
            assert vt == 17
            wt, bw = load_w_tile(1024 + 128 * h, 1536 + 128 * (h + 1) if h < 3 else None)
            for (c0, n) in col_chunks():
                ps, bps = inproj_chunk(wt, bw, c0, n)
                TT("dve", [bps, b_ebt], [H["b_qT"]], H["qT"][:, c0:c0 + n], ps, ebt[:, c0:c0 + n], ALU.mult)
                yield

        def stage_b(h, cx):
            S2, Sb, attm, kh, khT, o32, osq, rs, hgt, khm = (cx[k] for k in ("S2", "Sb", "attm", "kh", "khT", "o32", "osq", "rs", "hgt", "khm"))
            b_S2, b_Sb, b_attm, b_kh, b_khT, b_o32, b_osq, b_rs, b_hgt, b_khm = (
                cx["b_" + k] for k in ("S2", "Sb", "attm", "kh", "khT", "o32", "osq", "rs", "hgt", "khm"))
            psA, psR, psO, psS = cx["psA"], cx["psR"], cx["psO"], cx["psS"]
            b_psA, b_psR, b_psO, b_psS = cx["b_psA"], cx["b_psR"], cx["b_psO"], cx["b_psS"]

            def evac_o(p, n):
                CP("dve", [b_psO], [b_o32[p]], o32[p][:, 0:n], psO[:, 0:n])
                ACT([b_psO], [b_osq[p]], osq[p][:, 0:n], psO[:, 0:n], AF.Square)

            def finish_o(h, H, p, c0, n):
                P.op("pe", lambda e: e.matmul(psR[:, 0:n], lhsT=onesb[:, :], rhs=osq[p][:, 0:n], start=True, stop=True),
                     [b_onesb, b_osq[p]], [b_psR])
                ACT([b_psR], [b_rs[p]], rs[p][:, 0:n], psR[:, 0:n], AF.Ln, scale=1.0 / 128.0, bias=EPS)
                ACT([b_rs[p]], [b_rs[p]], rs[p][:, 0:n], rs[p][:, 0:n], AF.Exp, scale=-0.5)
                STT([b_o32[p], b_rs[p], b_ogc], [b_hgt[p]], hgt[p][:, 0:n], o32[p][:, 0:n], ogc[:, h:h + 1], rs[p][:, 0:n], ALU.mult, ALU.mult)
                TT("dve", [b_hgt[p], H["b_szh"]], [b_mixH[h]], mixH[:, h, c0:c0 + n], hgt[p][:, 0:n], H["szh"][:, c0:c0 + n], ALU.mult)

            H = hs[h % 2]
            kT, qT, vtk, ebt = H["kT"], H["qT"], H["vtk"], H["ebt"]
            b_kT, b_qT, b_vtk, b_ebt = H["b_kT"], H["b_qT"], H["b_vtk"], H["b_ebt"]
            MEMSET("pool", [b_S2[1]], S2[1][:], 0.0)
            MEMSET("pool", [b_Sb[1]], Sb[1][:], 0.0)
            slot = {}

            def ld(sq):
                r = scount[0] % NSB
                scount[0] += 1
                slot[sq] = r
                P.dma(S0f[r][:, :], hg0[sq, h, :, :], writes=[b_S0f[r]])
            ld(0)
            ld(1)

            def T1a(tt):
                c0 = tt * 128
                p = tt % 2
                P.op("pe", lambda e: e.matmul(psA, lhsT=kT[:, c0:c0 + 128], rhs=qT[:, c0:c0 + 128], start=True, stop=True),
                     [b_kT, b_qT], [b_psA])
                TT("dve", [b_psA, b_mask2], [b_attm[p]], attm[p][:, :], psA, mask2[:, :], ALU.mult)
                ACT([b_kT, b_ebt, b_kh[p]], [b_kh[p]], kh[p][:, :], kT[:, c0:c0 + 128], AF.Copy, scale=ebt[:, c0 + 127:c0 + 128])

            def T1b(tt):
                p = tt % 2
                pst = psT[:, 0:128]
                P.op("pe", lambda e: e.transpose(out=pst, in_=kh[p][:, :], identity=identb[:, :]), [b_kh[p], b_identb], [bpsT[0]])
                CP("act", [bpsT[0]], [b_khT[p]], khT[p][:, :], pst)

            def T2a(tt):
                c0 = tt * 128
                p = tt % 2
                for half in range(2):
                    r0 = 64 * half
                    P.op("pe", lambda e, r0=r0, half=half: e.matmul(psS[half], lhsT=khT[p][r0:r0 + 64, :], rhs=vtk[r0:r0 + 64, tt, :],
                                                                    start=True, stop=True),
                         [b_khT[p], b_vtk], [b_psS[half]])
                for half in range(2):
                    r0 = 64 * half
                    cur, prev = half, 1 - half
                    STT([b_S2[prev], b_ebt], [b_S2[cur], b_psS[half]], S2[cur][:, :], S2[prev][:, :], ebt[:, c0 + r0 + 63:c0 + r0 + 64],
                        psS[half], ALU.mult, ALU.add)
                    CP("act", [b_S2[cur]], [b_Sb[cur]], Sb[cur][:, :], S2[cur][:, :])

                def mo(e, r0=0, sprev=1):
                    e.matmul(psO[:, r0:r0 + 64], lhsT=vtk[r0:r0 + 64, tt, :], rhs=attm[p][r0:r0 + 64, r0:r0 + 64], start=True, stop=False)
                    return e.matmul(psO[:, r0:r0 + 64], lhsT=Sb[sprev][:, :], rhs=qT[:, c0 + r0:c0 + r0 + 64], start=False, stop=True)
                return mo

            def T2b(tt):
                c0 = tt * 128
                p = tt % 2

                def mo0(e):
                    e.matmul(psO[:, 0:64], lhsT=vtk[0:64, tt, :], rhs=attm[p][0:64, 0:64], start=True, stop=False)
                    return e.matmul(psO[:, 0:64], lhsT=Sb[1][:, :], rhs=qT[:, c0:c0 + 64], start=False, stop=True)

                def mo1(e):
                    e.matmul(psO[:, 64:128], lhsT=vtk[64:128, tt, :], rhs=attm[p][64:128, 64:128], start=True, stop=False)
                    return e.matmul(psO[:, 64:128], lhsT=Sb[0][:, :], rhs=qT[:, c0 + 64:c0 + 128], start=False, stop=True)
                return mo0, mo1

            for tt in range(16):
                c0 = tt * 128
                p = tt % 2
                cur, prev = tt % 2, 1 - (tt % 2)
                if tt == 0:
                    T1a(0)
                    T1b(0)

                def mo(e, tt=tt, p=p, c0=c0, prev=prev):
                    e.matmul(psO[:, 0:128], lhsT=vtk[:, tt, :], rhs=attm[p][:, :], start=True, stop=False)
                    return e.matmul(psO[:, 0:128], lhsT=Sb[prev][:, :], rhs=qT[:, c0:c0 + 128], start=False, stop=True)
                P.op("pe", mo, [b_vtk, b_attm[p], b_Sb[prev], b_qT], [b_psO])
                P.op("pe", lambda e, tt=tt, p=p: e.matmul(psS[0], lhsT=khT[p][:, :], rhs=vtk[:, tt, :], start=True, stop=True),
                     [b_khT[p], b_vtk], [b_psS[0]])
                STT([b_S2[prev], b_ebt], [b_S2[cur], b_psS[0]], S2[cur][:, :], S2[prev][:, :], ebt[:, c0 + 127:c0 + 128], psS[0],
                    ALU.mult, ALU.add)
                CP("dve", [b_S2[cur]], [b_Sb[cur]], Sb[cur][:, :], S2[cur][:, :])
                yield
                if tt + 1 < 16:
                    T1a(tt + 1)
                evac_o(p, 128)
                yield
                if tt + 1 < 16:
                    T1b(tt + 1)
                if tt >= 1:
                    finish_o(h, H, 1 - p, c0 - 128, 128)
                yield
            finish_o(h, H, 1, 15 * 128, 128)
            P.dma(o_phg[h, :, :], S2[1][:, :], reads=[b_S2[1]], is_output=True)
            p = 0
            P.op("pe", lambda e: e.matmul(psA[0:64, 0:64], lhsT=kT[:, LP:NT], rhs=qT[:, LP:NT], start=True, stop=True),
                 [b_kT, b_qT], [b_psA])
            TT("dve", [b_psA, b_masks], [b_attm[p]], attm[p][0:64, 0:64], psA[0:64, 0:64], masks[:, :], ALU.mult)
            k3 = kT[:, LP:NT].rearrange("p (s t) -> p s t", t=LS)
            e3 = ebt[:, LP:NT].rearrange("p (s t) -> p s t", t=LS)[:, :, LS - 1:LS].to_broadcast([128, NS, LS])
            TT("dve", [b_kT, b_ebt, b_kh[p]], [b_kh[p]], kh[p][:, 0:64].rearrange("p (s t) -> p s t", t=LS), k3, e3, ALU.mult)
            pst = psT[0:64, 0:128]
            P.op("pe", lambda e, pst=pst: e.transpose(out=pst, in_=kh[p][:, 0:64], identity=identb[:, :]), [b_kh[p], b_identb], [bpsT[0]])
            CP("act", [bpsT[0]], [b_khT[p]], khT[p][0:64, :], pst)
            P.op("pe", lambda e: e.matmul(psO[:, 0:64], lhsT=vtk[0:64, 16, :], rhs=attm[p][0:64, 0:64], start=True, stop=False),
                 [b_vtk, b_attm[p]], [b_psO])
            def st(sq):
                r = slot[sq]
                P.dma(o_shg[sq, h, :, :], S0f[r][:, :], reads=[b_S0f[r]], is_output=True, eng="pool")
            for sq in range(NS):
                if sq + 2 < NS:
                    ld(sq + 2)
                r = slot[sq]
                CP("act", [b_S0f[r]], [b_S0b[r]], S0b[r][:, :], S0f[r][:, :])
                P.op("pe", lambda e, sq=sq, r=r: e.matmul(psO[:, LS * sq:LS * sq + LS], lhsT=S0b[r][:, :], rhs=qT[:, LP + LS * sq:LP + LS * sq + LS],
                                                        start=False, stop=(sq == NS - 1), skip_group_check=True),
                     [b_S0b[r], b_qT], [b_psO])
                m = sq % 2
                ACT([b_khT[p], b_rowm, b_khm[m]], [b_khm[m]], khm[m][:, :], khT[p][0:64, :], AF.Copy, scale=rowm[:, sq:sq + 1])
                P.op("pe", lambda e, m=m: e.matmul(psS[m], lhsT=khm[m][:, :], rhs=vtk[0:64, 16, :], start=True, stop=True),
                     [b_khm[m], b_vtk], [b_psS[m]])
                STT([b_S0f[r], b_ebt], [b_S0f[r], b_psS[m]], S0f[r][:, :], S0f[r][:, :], ebt[:, LP + LS * sq + LS - 1:LP + LS * sq + LS], psS[m],
                    ALU.mult, ALU.add)
                if sq >= 1:
                    st(sq - 1)
                if sq % 2 == 1:
                    yield
            st(NS - 1)
            evac_o(p, NSAMP)
            finish_o(h, H, p, LP, NSAMP)
            yield

        for pair in ((0, 1), (2, 3)):
            for h in pair:
                for _ in stage_a(h):
                    pass
            gens = [stage_b(h, ctxs[i]) for i, h in enumerate(pair)]
            while gens:
                for g in list(gens):
                    try:
                        next(g)
                    except StopIteration:
                        gens.remove(g)

    def outproj():
        banks = [[(psI[0][:, :], bpsI[0]), (psI[1][:, :], bpsI[1])],
                 [(psBU[0][:, 0:512], bpsBU[0]), (psBU[0][:, 512:1024], bpsBU[0])]]

        def mk(tt):
            rows = 128 if tt < 16 else NSAMP
            c0 = tt * 128
            src = xp[c0:c0 + 128, :] if tt < 16 else xs[:, :]
            dst = yp[c0:c0 + 128, :] if tt < 16 else ys[:, :]
            s_ = tt % len(xin)
            bk = banks[tt % 2]

            def L():
                P.dma(xin[s_][0:rows, :], src, writes=[b_xin[s_]])

            def M():
                for dn in range(2):
                    def mo(e, dn=dn):
                        ins = None
                        for kf in range(8):
                            lh = uT[:, kf, c0:c0 + rows] if kf < 4 else mixH[:, kf - 4, c0:c0 + rows]
                            ins = e.matmul(bk[dn][0][0:rows, :], lhsT=lh, rhs=wo[:, kf, dn * 512:(dn + 1) * 512], start=(kf == 0), stop=(kf == 7))
                        return ins
                    P.op("pe", mo, b_uT + b_mixH + [b_wo], [bk[dn][1]])

            def R():
                for dn in range(2):
                    TT("dve", [bk[dn][1], b_xin[s_], b_res[s_]], [b_res[s_]], res[s_][0:rows, dn * 512:(dn + 1) * 512], bk[dn][0][0:rows, :],
                       xin[s_][0:rows, dn * 512:(dn + 1) * 512], ALU.add)

            def N():
                ACT([b_res[s_]], [b_junk, b_stat[s_]], junk[0:rows, :], res[s_][0:rows, :], AF.Square, accum_out=stat[s_][0:rows, 0:1])
                ACT([b_stat[s_]], [b_stat[s_]], stat[s_][0:rows, 1:2], stat[s_][0:rows, 0:1], AF.Ln, scale=1.0 / D, bias=EPS)
                ACT([b_stat[s_]], [b_stat[s_]], stat[s_][0:rows, 2:3], stat[s_][0:rows, 1:2], AF.Exp, scale=-0.5)

            def F():
                STT([b_res[s_], b_stat[s_], b_fgb], [b_res[s_]], res[s_][0:rows, :], res[s_][0:rows, :], stat[s_][0:rows, 2:3], fgb[0:rows, :],
                    ALU.mult, ALU.mult)
                P.dma(dst, res[s_][0:rows, :], reads=[b_res[s_]], is_output=True, eng="act")
            return (L, M, R, N, F)
        T = [mk(tt) for tt in range(17)]
        T[0][0]()
        T[1][0]()
        T[0][1]()
        for t in range(17):
            if t + 2 < 17:
                T[t + 2][0]()
            if t + 1 < 17:
                T[t + 1][1]()
            T[t][2]()
            T[t][3]()
            if t >= 1:
                T[t - 1][4]()
        T[16][4]()

    m_tmp = A.mark()
    print("arena marks: s5p", m_s5p, "tmp", m_tmp)
    def u_inproj():
        for jt in range(4):
            wt, bw = load_w_tile(jt * 128, (jt + 1) * 128 if jt < 3 else None)
            for (c0, n) in col_chunks():
                ps, bps = inproj_chunk(wt, bw, c0, n)
                if c0 < LP:
                    CP("act", [bps], [b_uT[jt]], uT[:, jt, 0:LP].rearrange("p (s c) -> p s c", s=8)[:, :, c0 // 8:c0 // 8 + n // 8],
                       ps.rearrange("p (c s) -> p s c", s=8))
                else:
                    CP("act", [bps], [b_uT[jt]], uT[:, jt, LP:NT].rearrange("p (t s) -> p t s", t=LS),
                       ps.rearrange("p (s t) -> p t s", t=LS))
                yield

    def chain(*gens):
        for g in gens:
            for _ in g:
                yield

    ga_ = chain(phase_a(), u_inproj())
    gp_ = s5_prep()
    alive = [ga_, ga_, gp_]
    while alive:
        for g in list(alive):
            if g not in alive:
                continue
            try:
                next(g)
            except StopIteration:
                while g in alive:
                    alive.remove(g)
    load_wg()
    P.barrier()
    A.reset(m_tmp)
    ea = sb("ea", [128, 512])
    eb2 = sb("eb2", [128, 512])
    glu_tmp2 = []
    s5_loop()
    glu()
    P.barrier()
    A.reset(m_s5p)
    mixH = sb("mixH", [128, 4, NT], BF16)
    wo = sb("wo", [128, 8, D], BF16)
    load_wo()
    m_hg = A.mark()
    hgrn()
    P.barrier()
    A.reset(m_hg)
    NOB = 4
    for _ in range(NOB - 2):
        xin.append(sb("xinx", [128, D])); b_xin.append(Buf("xinx"))
        res.append(sb("resx", [128, D])); b_res.append(Buf("resx"))
        stat.append(sb("statx", [128, 4])); b_stat.append(Buf("statx"))
    outproj()
    print("SBUF peak bytes/partition:", A.peak)
    P.emit(nc)
    return nc


_NC_CACHE = {}


def _consts():
    ident = np.eye(128, dtype=np.float32)
    s = np.arange(128)
    mask2 = (s[:, None] <= s[None, :]).astype(np.float32)
    t = np.arange(64)
    masks = ((t[:, None] // LS == t[None, :] // LS) & (t[:, None] <= t[None, :])).astype(np.float32)
    seg = np.ones((1, 512 + NSAMP), np.float32)
    seg[0, 0:512:128] = 0.0
    seg[0, 512::LS] = 0.0
    rowm = (t[:, None] // LS == np.arange(NS)[None, :]).astype(np.float32)
    q = np.arange(128)
    glm = (q[:, None] // 64 == np.arange(2)[None, :]).astype(np.float32)
    rm4 = (q[:, None] // 32 == np.arange(4)[None, :]).astype(np.float32)
    return dict(c_ident=ident, c_mask2=mask2, c_masks=masks, c_seg=seg, c_rowm=rowm, c_glm=glm, c_rm4=rm4)


def kernel(x_prompt, x_sample, state_s5_re, state_s5_im, state_hgrn, norm_g, w_in, s5_lambda_re, s5_lambda_im,
           s5_log_step, s5_b_re, s5_b_im, s5_c_re, s5_c_im, s5_d, w_glu, b_glu, hgrn_lb_logits, hgrn_onorm_g,
           w_out, final_norm_g):
    f = lambda a: np.ascontiguousarray(np.asarray(a, dtype=np.float32))
    x_prompt, x_sample = f(x_prompt), f(x_sample)
    state_s5_re, state_s5_im, state_hgrn = f(state_s5_re), f(state_s5_im), f(state_hgrn)
    shared = dict(
        norm_g=f(norm_g)[0], w_in=f(w_in)[0], lam_re=f(s5_lambda_re)[0], lam_im=f(s5_lambda_im)[0],
        log_step=f(s5_log_step)[0], b_re=f(s5_b_re)[0], b_im=f(s5_b_im)[0], c_re=f(s5_c_re)[0], c_im=f(s5_c_im)[0],
        s5_d=f(s5_d)[0].reshape(512), w_glu=f(w_glu)[0], b_glu=f(b_glu)[0], lb_logits=f(hgrn_lb_logits),
        onorm_g=f(hgrn_onorm_g)[0], w_out=f(w_out)[0], fin_g=f(final_norm_g).reshape(1, D))
    shared.update(_consts())
    in_maps = []
    for c in range(NCORES):
        m = dict(shared)
        m["xp"] = x_prompt[c]
        m["xs"] = np.ascontiguousarray(x_sample[NS * c:NS * (c + 1)].reshape(NSAMP, D))
        m["s5re0"] = np.ascontiguousarray(state_s5_re[0, NS * c:NS * (c + 1)].reshape(NS, 2048))
        m["s5im0"] = np.ascontiguousarray(state_s5_im[0, NS * c:NS * (c + 1)].reshape(NS, 2048))
        m["hg0"] = np.ascontiguousarray(state_hgrn[0, NS * c:NS * (c + 1)])
        in_maps.append(m)
    if "nc" not in _NC_CACHE:
        _NC_CACHE["nc"] = build_program()
    nc = _NC_CACHE["nc"]
    res = run_bass_kernel_spmd(nc, in_maps, core_ids=list(range(NCORES)))
    R = res.results
    y_prompt = np.stack([R[c]["yp"] for c in range(NCORES)]).astype(np.float32)
    y_sample = np.concatenate([R[c]["ys"].reshape(NS, LS, D) for c in range(NCORES)]).astype(np.float32)
    p_re = np.stack([R[c]["o_pre"].reshape(32, 64) for c in range(NCORES)])[None].astype(np.float32)
    p_im = np.stack([R[c]["o_pim"].reshape(32, 64) for c in range(NCORES)])[None].astype(np.float32)
    p_hg = np.stack([R[c]["o_phg"] for c in range(NCORES)])[None].astype(np.float32)
    s_re = np.concatenate([R[c]["o_sre"].reshape(NS, 32, 64) for c in range(NCORES)])[None].astype(np.float32)
    s_im = np.concatenate([R[c]["o_sim"].reshape(NS, 32, 64) for c in range(NCORES)])[None].astype(np.float32)
    s_hg = np.concatenate([R[c]["o_shg"] for c in range(NCORES)])[None].astype(np.float32)
    return (y_prompt, y_sample, p_re, p_im, p_hg, s_re, s_im, s_hg)
```
